# Optimizing a Trainium2 kernel written in Bass

```python
import jax, jax.numpy as jnp
from jax import lax
import numpy as np

D_MODEL = 2048
BATCH = 4
SEQ = 2048
DEPTH = 1
DEC_BATCH = 128
DEC_SEQ = 8
PAST_LEN = 16384
PAGE_SIZE = 128

N_META = 16
LRU_WIDTH = D_MODEL // 2
LRU_BLOCKS = 16
LRU_BLOCK = LRU_WIDTH // LRU_BLOCKS
CONV_W = 4
LRU_C = 8.0
RET_HEADS = 8
RET_WIDTH = D_MODEL // 2
RET_DK = RET_WIDTH // RET_HEADS
RET_DV = RET_WIDTH // RET_HEADS
CHUNK = 128
D_FF = 4 * D_MODEL
ROPE_BASE = 10000.0
EPS = 1e-6
PROJ_SIZES = (LRU_WIDTH, LRU_WIDTH, RET_WIDTH, RET_WIDTH, RET_WIDTH, RET_WIDTH, D_MODEL, D_MODEL)
D_PROJ = 2 * LRU_WIDTH + 4 * RET_WIDTH + 2 * D_MODEL

kernel_name = 'hawk_retnet_meta_hybrid_step'

F32 = jnp.float32


def rmsnorm(x, g):
    xf = x.astype(F32)
    y = xf * lax.rsqrt(jnp.mean(xf * xf, axis=-1, keepdims=True) + EPS)
    return (y * g.astype(F32)).astype(x.dtype)


def split_proj(proj):
    idx, acc = [], 0
    for s in PROJ_SIZES[:-1]:
        acc += s
        idx.append(acc)
    return jnp.split(proj, idx, axis=-1)


def rope(x, pos):
    d = x.shape[-1]
    inv = ROPE_BASE ** (-jnp.arange(0, d, 2, dtype=F32) / d)
    ang = pos.astype(F32)[:, None] * inv[None, :]
    cos = jnp.cos(ang)[None, :, None, :]
    sin = jnp.sin(ang)[None, :, None, :]
    xf = x.astype(F32)
    x1, x2 = xf[..., : d // 2], xf[..., d // 2:]
    return jnp.concatenate([x1 * cos - x2 * sin, x2 * cos + x1 * sin], axis=-1)


def causal_conv(u, buf, w, b):
    T = u.shape[1]
    full = jnp.concatenate([buf.astype(u.dtype), u], axis=1)
    out = full[:, 0:T] * w[0]
    for j in range(1, CONV_W):
        out = out + full[:, j:j + T] * w[j]
    return out + b, full[:, T:]


def rglru(u, h0, pos, wa, ba, wx, bx, lam):
    B, T, _ = u.shape
    uf = u.astype(F32)
    ub = uf.reshape(B, T, LRU_BLOCKS, LRU_BLOCK)
    rec_gate = jax.nn.sigmoid(jnp.einsum('btnc,ncd->btnd', ub, wa.astype(F32)).reshape(B, T, LRU_WIDTH) + ba)
    in_gate = jax.nn.sigmoid(jnp.einsum('btnc,ncd->btnd', ub, wx.astype(F32)).reshape(B, T, LRU_WIDTH) + bx)
    log_a = LRU_C * rec_gate * jax.nn.log_sigmoid(lam.astype(F32))
    a = jnp.exp(log_a)
    mult = jnp.sqrt(-jnp.expm1(2.0 * log_a))
    mult = jnp.where((pos == 0)[None, :, None], 1.0, mult)
    bterm = mult * in_gate * uf
    bterm = bterm.at[:, 0].add(a[:, 0] * h0.astype(F32))

    def comb(left, right):
        a1, b1 = left
        a2, b2 = right
        return a1 * a2, a2 * b1 + b2

    _, h = lax.associative_scan(comb, (a, bterm), axis=1)
    return h.astype(u.dtype), h[:, -1].astype(h0.dtype)


def ret_chunk(S, q, k, v, log_g):
    L = q.shape[1]
    n = jnp.arange(L, dtype=F32)
    diff = n[:, None] - n[None, :]
    causal = diff >= 0
    decay = jnp.where(causal[None], jnp.exp(log_g[:, None, None] * jnp.where(causal, diff, 0.0)[None]), 0.0)
    scores = jnp.einsum('blhd,bmhd->bhlm', q, k) * decay[None]
    intra = jnp.einsum('bhlm,bmhv->blhv', scores, v)
    inter = jnp.einsum('blhd,bhdv->blhv', q, S) * jnp.exp((n + 1.0)[:, None] * log_g[None, :])[None, :, :, None]
    kd = k * jnp.exp((L - 1.0 - n)[:, None] * log_g[None, :])[None, :, :, None]
    S_new = jnp.exp(L * log_g)[None, :, None, None] * S + jnp.einsum('blhd,blhv->bhdv', kd, v)
    return S_new, intra + inter


def retention(q, k, v, S0, lead):
    B, T = q.shape[0], q.shape[1]
    log_g = jnp.log1p(-jnp.exp2(-5.0 - jnp.arange(RET_HEADS, dtype=F32)))
    q, k, v = q.astype(F32), k.astype(F32), v.astype(F32)
    S = S0.astype(F32)
    outs = []
    if lead > 0:
        S, o_lead = ret_chunk(S, q[:, :lead], k[:, :lead], v[:, :lead], log_g)
        outs.append(o_lead)
    rest = T - lead
    c = CHUNK if rest % CHUNK == 0 else rest
    nc = rest // c

    def to_chunks(t):
        return jnp.moveaxis(t[:, lead:].reshape(B, nc, c, t.shape[2], t.shape[3]), 1, 0)

    def step(S_c, xs):
        qc, kc, vc = xs
        return ret_chunk(S_c, qc, kc, vc, log_g)

    S, o = lax.scan(step, S, (to_chunks(q), to_chunks(k), to_chunks(v)))
    outs.append(jnp.moveaxis(o, 0, 1).reshape(B, rest, RET_HEADS, RET_DV))
    return jnp.concatenate(outs, axis=1), S.astype(S0.dtype)


def head_norm(o, g):
    mu = jnp.mean(o, axis=-1, keepdims=True)
    var = jnp.mean(jnp.square(o - mu), axis=-1, keepdims=True)
    return (o - mu) * lax.rsqrt(var + EPS) * g.astype(F32).reshape(RET_HEADS, RET_DV)


def mixer(h, pos, lead, conv_buf, lru_h0, ret_S0, w_in, conv_w, conv_b, lru_wa, lru_ba, lru_wx, lru_bx,
          lru_lam, ret_norm_g, p_a, p_b, w_out):
    B, T, _ = h.shape
    xa, ga, q, k, v, g, gate_a, gate_b = split_proj(h @ w_in)
    xc, new_buf = causal_conv(xa, conv_buf, conv_w, conv_b)
    y_lru, h_last = rglru(xc, lru_h0, pos, lru_wa, lru_ba, lru_wx, lru_bx, lru_lam)
    ya = y_lru * jax.nn.gelu(ga)
    qh = rope(q.reshape(B, T, RET_HEADS, RET_DK), pos)
    kh = rope(k.reshape(B, T, RET_HEADS, RET_DK), pos) * (RET_DK ** -0.5)
    vh = v.reshape(B, T, RET_HEADS, RET_DV)
    o, S_last = retention(qh, kh, vh, ret_S0, lead)
    o = head_norm(o, ret_norm_g).reshape(B, T, RET_WIDTH).astype(h.dtype)
    yb = jax.nn.silu(g) * o
    merged = jax.nn.sigmoid(gate_a) * (ya @ p_a) + jax.nn.sigmoid(gate_b) * (yb @ p_b)
    return merged @ w_out, new_buf, h_last, S_last


def trunk(x, pos, lead, conv_bufs, lru_hs, ret_Ss, norm_mix_g, w_in, conv_w, conv_b, lru_wa, lru_ba,
          lru_wx, lru_bx, lru_lam, ret_norm_g, p_a, p_b, w_out, norm_ffn_g, w_up, w_down, norm_f_g):
    new_conv, new_lru, new_ret = [], [], []
    for l in range(DEPTH):
        h = rmsnorm(x, norm_mix_g[l])
        mix, cb, lh, rs = mixer(h, pos, lead, conv_bufs[l], lru_hs[l], ret_Ss[l], w_in[l], conv_w[l],
                                conv_b[l], lru_wa[l], lru_ba[l], lru_wx[l], lru_bx[l], lru_lam[l],
                                ret_norm_g[l], p_a[l], p_b[l], w_out[l])
        x = x + mix
        h = rmsnorm(x, norm_ffn_g[l])
        x = x + jnp.square(jax.nn.relu(h @ w_up[l])) @ w_down[l]
        new_conv.append(cb)
        new_lru.append(lh)
        new_ret.append(rs)
    return rmsnorm(x, norm_f_g), jnp.stack(new_conv), jnp.stack(new_lru), jnp.stack(new_ret)


def setup_inputs(seed: int = 0) -> dict:
    key = jax.random.key(seed)
    ks = jax.random.split(key, 24)

    def nrm(k, shape, scale):
        return jax.random.normal(k, shape, F32) * scale

    a_c = jax.random.uniform(ks[14], (DEPTH, LRU_WIDTH), F32, 0.9, 0.999)
    s = a_c ** (1.0 / LRU_C)
    return {
        'x_prompt': nrm(ks[0], (BATCH, SEQ, D_MODEL), 1.0),
        'x_sample': nrm(ks[1], (DEC_BATCH, DEC_SEQ, D_MODEL), 1.0),
        'state_conv': nrm(ks[2], (DEPTH, DEC_BATCH, CONV_W - 1, LRU_WIDTH), 1.0),
        'state_lru': nrm(ks[3], (DEPTH, DEC_BATCH, LRU_WIDTH), 1.0),
        'state_ret': nrm(ks[4], (DEPTH, DEC_BATCH, RET_HEADS, RET_DK, RET_DV), 1.0),
        'meta_tokens': nrm(ks[5], (N_META, D_MODEL), 1.0),
        'norm_mix_g': 1.0 + nrm(ks[6], (DEPTH, D_MODEL), 0.02),
        'w_in': nrm(ks[7], (DEPTH, D_MODEL, D_PROJ), D_MODEL ** -0.5),
        'conv_w': nrm(ks[8], (DEPTH, CONV_W, LRU_WIDTH), CONV_W ** -0.5),
        'conv_b': nrm(ks[9], (DEPTH, LRU_WIDTH), 0.02),
        'lru_wa': nrm(ks[10], (DEPTH, LRU_BLOCKS, LRU_BLOCK, LRU_BLOCK), LRU_BLOCK ** -0.5),
        'lru_ba': nrm(ks[11], (DEPTH, LRU_WIDTH), 0.02),
        'lru_wx': nrm(ks[12], (DEPTH, LRU_BLOCKS, LRU_BLOCK, LRU_BLOCK), LRU_BLOCK ** -0.5),
        'lru_bx': nrm(ks[13], (DEPTH, LRU_WIDTH), 0.02),
        'lru_lam': jnp.log(s) - jnp.log1p(-s),
        'ret_norm_g': 1.0 + nrm(ks[15], (DEPTH, RET_WIDTH), 0.02),
        'p_a': nrm(ks[16], (DEPTH, LRU_WIDTH, D_MODEL), LRU_WIDTH ** -0.5),
        'p_b': nrm(ks[17], (DEPTH, RET_WIDTH, D_MODEL), RET_WIDTH ** -0.5),
        'w_out': nrm(ks[18], (DEPTH, D_MODEL, D_MODEL), D_MODEL ** -0.5),
        'norm_ffn_g': 1.0 + nrm(ks[19], (DEPTH, D_MODEL), 0.02),
        'w_up': nrm(ks[20], (DEPTH, D_MODEL, D_FF), D_MODEL ** -0.5),
        'w_down': nrm(ks[21], (DEPTH, D_FF, D_MODEL), D_FF ** -0.5),
        'norm_f_g': 1.0 + nrm(ks[22], (D_MODEL,), 0.02),
    }


def reference(x_prompt, x_sample, state_conv, state_lru, state_ret, meta_tokens, norm_mix_g, w_in, conv_w,
              conv_b, lru_wa, lru_ba, lru_wx, lru_bx, lru_lam, ret_norm_g, p_a, p_b, w_out, norm_ffn_g,
              w_up, w_down, norm_f_g):
    weights = (norm_mix_g, w_in, conv_w, conv_b, lru_wa, lru_ba, lru_wx, lru_bx, lru_lam, ret_norm_g,
               p_a, p_b, w_out, norm_ffn_g, w_up, w_down, norm_f_g)
    B = x_prompt.shape[0]
    meta = jnp.broadcast_to(meta_tokens.astype(x_prompt.dtype)[None], (B, N_META, D_MODEL))
    xp = jnp.concatenate([meta, x_prompt], axis=1)
    pos_p = jnp.arange(N_META + x_prompt.shape[1], dtype=jnp.int32)
    conv0 = jnp.zeros((DEPTH, B, CONV_W - 1, LRU_WIDTH), state_conv.dtype)
    lru0 = jnp.zeros((DEPTH, B, LRU_WIDTH), state_lru.dtype)
    ret0 = jnp.zeros((DEPTH, B, RET_HEADS, RET_DK, RET_DV), state_ret.dtype)
    yp, new_conv_prompt, new_lru_prompt, new_ret_prompt = trunk(xp, pos_p, N_META, conv0, lru0, ret0, *weights)
    y_prompt = yp[:, N_META:]
    pos_s = PAST_LEN + jnp.arange(x_sample.shape[1], dtype=jnp.int32)
    y_sample, new_conv_sample, new_lru_sample, new_ret_sample = trunk(
        x_sample, pos_s, 0, state_conv, state_lru, state_ret, *weights)
    return (y_prompt, y_sample, new_conv_prompt, new_lru_prompt, new_ret_prompt,
            new_conv_sample, new_lru_sample, new_ret_sample)
```

```python
import contextlib
import numpy as np
import concourse.bass as bass
import concourse.mybir as mybir
from concourse.bass_utils import run_bass_kernel_spmd

F32 = mybir.dt.float32
BF16 = mybir.dt.bfloat16
AF = mybir.ActivationFunctionType
ALU = mybir.AluOpType

D = 2048
TP = 1032
TS = 128
T = TP + TS
NH = 8
CH = 116
PT = [(CH * i, CH) for i in range(8)] + [(928, 104)]
TILES = PT + [(TP, TS)]
NTP = [(0, 344), (344, 344), (688, 344)]
NT = NTP + [(TP, TS)]
GAM = [1.0 - 2.0 ** (-5 - h) for h in range(NH)]
EPS = 1e-6
ENGS = ("pe", "act", "dve", "pool", "sp")


class Op:
    __slots__ = ("eng", "fn", "reads", "writes", "is_dma", "deps", "token", "need_inc", "idx", "tag",
                 "prev_same_sem")

    def __init__(self, eng, fn, reads, writes, is_dma, tag=None):
        self.eng = eng
        self.fn = fn
        self.reads = tuple(reads)
        self.writes = tuple(writes)
        self.is_dma = is_dma
        self.deps = set()
        self.token = None
        self.need_inc = False
        self.tag = tag
        self.prev_same_sem = None


class Prog:
    def __init__(self, nc, n_dma_sems=6):
        self.nc = nc
        self.ops = []
        self.n_dma_sems = n_dma_sems
        self.last_writer = {}
        self.readers = {}
        self.last_barrier = None
        self.last_on_eng = {}

    def op(self, eng, fn, reads=(), writes=(), tag=None):
        o = Op(eng, fn, reads, writes, False, tag)
        self._add(o)
        return o

    def dma(self, queue, out, in_, reads=(), writes=(), tag=None):
        n = out.shape[0]
        if n == in_.shape[0] and n > 16 and n % 16 != 0:
            n16 = n - n % 16
            o1 = self._dma1(queue, out[0:n16], in_[0:n16], reads, writes, tag, None)
            self._dma1(queue, out[n16:n], in_[n16:n], reads, writes, tag, o1)
            return o1
        return self._dma1(queue, out, in_, reads, writes, tag, None)

    def _dma1(self, queue, out, in_, reads, writes, tag, co):
        def fn(e, out=out, in_=in_):
            return e.dma_start(out=out, in_=in_)
        o = Op(queue, fn, reads, writes, True, tag)
        self._add(o, co)
        return o

    def barrier(self, fn, eng="dve"):
        o = Op(eng, fn, (), (), False, "barrier")
        o.idx = len(self.ops)
        start = self.last_barrier.idx if self.last_barrier is not None else 0
        for p in self.ops[start:]:
            if p.is_dma:
                o.deps.add(p.idx)
        for e, p in self.last_on_eng.items():
            o.deps.add(p.idx)
        self.ops.append(o)
        self.last_on_eng[eng] = o
        self.last_barrier = o
        self.last_writer.clear()
        self.readers.clear()
        return o

    def _add(self, o, co=None):
        o.idx = len(self.ops)
        lw, rd = self.last_writer, self.readers
        for k in o.reads:
            if k in lw:
                o.deps.update(lw[k])
        if co is not None:
            o.deps.update(d for d in co.deps)
        else:
            for k in o.writes:
                if k in lw:
                    o.deps.update(lw[k])
                if k in rd:
                    o.deps.update(rd[k])
        if self.last_barrier is not None:
            o.deps.add(self.last_barrier.idx)
        for k in o.writes:
            if co is not None:
                cur = lw.get(k, [])
                lw[k] = (cur if co.idx in cur else [co.idx]) + [o.idx]
            else:
                lw[k] = [o.idx]
                rd[k] = []
        for k in o.reads:
            lst = rd.setdefault(k, [])
            if not o.is_dma:
                lst[:] = [q for q in lst if self.ops[q].is_dma or self.ops[q].eng != o.eng]
            lst.append(o.idx)
        self.ops.append(o)
        self.last_on_eng[o.eng] = o

    @staticmethod
    def _skip(p, o):
        if p.is_dma:
            return False
        if p.eng != o.eng:
            return False
        if p.eng == "pe":
            return True
        if o.is_dma:
            return False
        if p.tag == "barrier":
            return False
        return not (set(p.writes) & set(o.reads))

    def emit(self):
        nc = self.nc
        ops = self.ops
        for o in ops:
            for d in o.deps:
                p = ops[d]
                if p.is_dma or self._skip(p, o):
                    continue
                p.need_inc = True
        with contextlib.ExitStack() as st:
            esem = {e: st.enter_context(nc.semaphore(f"s_{e}")) for e in ENGS[:4]}
            dsems = {}
            for q in ("sp", "act", "pool"):
                dsems[q] = [st.enter_context(nc.semaphore(f"d_{q}{j}")) for j in range(self.n_dma_sems)]
            tick = {e: 0 for e in ENGS}
            dcount = {q: 0 for q in dsems}
            dval = {q: [0] * self.n_dma_sems for q in dsems}
            prev_on_sem = {}
            for o in ops:
                if o.is_dma:
                    q = o.eng
                    j = dcount[q] % self.n_dma_sems
                    dcount[q] += 1
                    dval[q][j] += 16
                    key = ("d", q, j)
                    o.token = (key, dval[q][j])
                    o.prev_same_sem = prev_on_sem.get(key)
                    prev_on_sem[key] = o
                elif o.need_inc:
                    tick[o.eng] += 1
                    o.token = (("e", o.eng), tick[o.eng])

            def semof(key):
                return esem[key[1]] if key[0] == "e" else dsems[key[1]][key[2]]

            waited = {e: {} for e in ENGS}
            per_eng = {e: [] for e in ENGS}
            for o in ops:
                need = {}
                for d in o.deps:
                    p = ops[d]
                    if p.token is None or self._skip(p, o):
                        continue
                    k, v = p.token
                    if need.get(k, 0) < v:
                        need[k] = v
                if o.is_dma and o.prev_same_sem is not None:
                    k, v = o.prev_same_sem.token
                    if need.get(k, 0) < v:
                        need[k] = v
                w = waited[o.eng]
                waits = []
                for k, v in need.items():
                    if w.get(k, 0) < v:
                        w[k] = v
                        waits.append((k, v))
                per_eng[o.eng].append((o, waits))
            final_waits = []
            for q in dsems:
                for j in range(self.n_dma_sems):
                    if dval[q][j] > 0:
                        final_waits.append((("d", q, j), dval[q][j]))
            for e in ENGS[:4]:
                if tick[e] > 0:
                    final_waits.append((("e", e), tick[e]))
            self.stats = {e: len(per_eng[e]) for e in ENGS}
            self.stats["waits"] = sum(len(w) for e in ENGS for _, w in per_eng[e])
            self.stats["ticks"] = dict(tick)

            def run(ename, e):
                for o, waits in per_eng[ename]:
                    for k, v in waits:
                        e.wait_ge(semof(k), v)
                    ins = o.fn(e)
                    if o.is_dma:
                        ins.then_inc(semof(o.token[0]), 16)
                    elif o.token is not None:
                        ins.then_inc(semof(o.token[0]), 1)
                if ename == "sp":
                    for k, v in final_waits:
                        e.wait_ge(semof(k), v)

            with nc.Block() as block:
                @block.sync
                def _(e):
                    run("sp", e)

                @block.scalar
                def _(e):
                    run("act", e)

                @block.vector
                def _(e):
                    run("dve", e)

                @block.gpsimd
                def _(e):
                    run("pool", e)

                @block.tensor
                def _(e):
                    run("pe", e)


def tt_over(t0, n):
    return [i for i, (a, b) in enumerate(TILES) if a < t0 + n and t0 < a + b]


def nt_over(t0, n):
    return [i for i, (a, b) in enumerate(NT) if a < t0 + n and t0 < a + b]


def build_nc(debug=False):
    build_nc.marks = []
    nc = bass.Bass("TRN2", target_bir_lowering=False)

    def din(name, shape):
        return nc.dram_tensor(name, list(shape), F32, kind="ExternalInput").ap()

    def dout(name, shape):
        return nc.dram_tensor(name, list(shape), F32, kind="ExternalOutput").ap()

    xall = din("xall", [T, D])
    xpre = din("xpre", [TP, D])
    sconv = din("sconv", [48, 1024])
    slru = din("slru", [16, 1024])
    sret = din("sret", [16, NH, 128, 128])
    flags_d = din("flags", [128, 8])
    tab = din("tab", [4, T, 1024])
    tabpre = din("tabpre", [2, TP, 1024])
    masks_d = din("masks", [2, 128, 128])
    selm_d = din("selm", [128, 16])
    prm = din("prm", [8, 1024])
    gn3 = din("gn3", [3, D])
    rng = din("rng", [1024])
    w_in = din("w_in", [D, 10240])
    lwa = din("lwa", [16, 64, 64])
    lwx = din("lwx", [16, 64, 64])
    p_a = din("p_a", [1024, D])
    p_b = din("p_b", [1024, D])
    w_out = din("w_out", [D, D])
    w_up = din("w_up", [D, 8192])
    w_down = din("w_down", [8192, D])
    y_d = dout("y", [T, D])
    nconv_d = dout("nconv", [51, 1024])
    nlru_d = dout("nlru", [17, 1024])
    nretp_d = dout("nretp", [NH, 128, 128])
    nrets_d = dout("nrets", [16, NH, 128, 128])
    if debug:
        dbg_hT = nc.dram_tensor("dbg_hT", [128, 16, T], BF16, kind="ExternalOutput").ap()
        dbg_yaT = nc.dram_tensor("dbg_yaT", [128, 8, T], BF16, kind="ExternalOutput").ap()
        dbg_ybT = nc.dram_tensor("dbg_ybT", [128, 8, T], BF16, kind="ExternalOutput").ap()
        dbg_mT = nc.dram_tensor("dbg_mT", [128, 16, T], BF16, kind="ExternalOutput").ap()
        dbg_x1 = nc.dram_tensor("dbg_x1", [128, 10, D], F32, kind="ExternalOutput").ap()
        dbg_x2 = nc.dram_tensor("dbg_x2", [128, 10, D], F32, kind="ExternalOutput").ap()

    st = contextlib.ExitStack()
    with st:
        def sb(name, shape, dt):
            return st.enter_context(nc.sbuf_tensor("sb_" + name, list(shape), dt))

        identb = sb("identb", [128, 128], BF16)
        identf = sb("identf", [128, 128], F32)
        masks = sb("masks", [128, 2, 128], F32)
        selm = sb("selm", [128, 16], F32)
        flags = sb("flags", [128, 8], F32)
        prm_fm = sb("prm_fm", [128, 8, 8], F32)
        c8 = sb("c8", [128, 8], F32)
        c16 = sb("c16", [128, 8], F32)
        wabd = sb("wabd", [128, 8, 128], BF16)
        wxbd = sb("wxbd", [128, 8, 128], BF16)
        sc_fm = sb("sc_fm", [128, 8, 48], F32)
        h0_fm = sb("h0_fm", [128, 8, 16], F32)
        hl_fm = sb("hl_fm", [128, 8, 17], F32)
        xtail = sb("xtail", [128, 8, 3], F32)
        hin = sb("hin", [128, 8], F32)
        spre = sb("spre", [128, NH, 128], F32)
        sfin = sb("sfin", [128, NH, 128], F32)
        ss = sb("ss", [128, 16], F32)
        rs = sb("rs", [128, 16], F32)
        gbc = sb("gbc", [128, D], F32)
        gnbc = sb("gnbc", [128, 1024], F32)
        bnst = sb("bnst", [128, 2, 6], F32)
        mv = sb("mv", [128, 2, 2], F32)
        rsd = sb("rsd", [128, 2], F32)
        tmp16 = sb("tmp16", [128, 16], F32)
        tmp16b = sb("tmp16b", [128, 16], F32)
        tmp16s = [tmp16b, tmp16]
        WBW = 16 * 512
        wbs = [sb(f"wb{k}", [128, WBW], BF16) for k in range(2)]
        ARW = 35700
        arena = sb("arena", [128, ARW], F32)
        banks = [st.enter_context(nc.psum_tensor(f"bank{i}", [128, 512], F32)) for i in range(8)]

        wl_log = []
        wl_state = {"dry": True}

        def program():
            P = Prog(nc)
            bank_ctr = [0]

            def nb(lo=0, hi=6):
                i = lo + bank_ctr[0] % (hi - lo)
                bank_ctr[0] += 1
                return i

            def BK(i):
                return ("ps", i)

            class Arena:
                def __init__(self):
                    self.off = 0

                def at(self, off):
                    self.off = off

                def f32(self, *shape):
                    n = int(np.prod(shape))
                    v = arena[:, self.off:self.off + n]
                    self.off += n
                    assert self.off <= ARW, self.off
                    if len(shape) == 2:
                        return v.rearrange("p (a b) -> p a b", a=shape[0])
                    if len(shape) == 3:
                        return v.rearrange("p (a b c) -> p a b c", a=shape[0], b=shape[1])
                    return v

                def bf(self, *shape):
                    n = int(np.prod(shape))
                    w = (n + 1) // 2
                    v = arena[:, self.off:self.off + w].bitcast(BF16)
                    self.off += w
                    assert self.off <= ARW, self.off
                    v = v[:, 0:n]
                    if len(shape) == 2:
                        return v.rearrange("p (a b) -> p a b", a=shape[0])
                    if len(shape) == 3:
                        return v.rearrange("p (a b c) -> p a b c", a=shape[0], b=shape[1])
                    return v

            AR = Arena()

            def do_barrier():
                P.barrier(lambda e: e.memset(tmp16[:, 0:1], 0.0))
                build_nc.marks.append({e: sum(1 for o in P.ops if o.eng == e) for e in ENGS})

            AR.at(ARW - 3 * 1024)
            prm_tm = AR.f32(1024)
            sc_tm = AR.f32(1024)
            h0_tm = AR.f32(1024)
            P.op("pool", lambda e: e.memset(identf[:], 1.0), writes=["identf"])
            P.op("pool", lambda e: e.affine_select(out=identf[:], in_=identf[:], pattern=[[-1, 128]],
                                                    compare_op=ALU.is_equal, fill=0.0, base=0, channel_multiplier=1),
                 reads=["identf"], writes=["identf"])
            P.op("dve", lambda e: e.tensor_copy(out=identb[:], in_=identf[:]), reads=["identf"], writes=["identb"])
            P.dma("sp", masks[:], masks_d.rearrange("a p n -> p a n"), writes=["masks"])
            P.dma("sp", selm[:], selm_d, writes=["selm"])
            P.dma("sp", flags[:], flags_d, writes=["flags"])
            P.dma("sp", prm_tm[0:8, :], prm, writes=["prm_tm"])
            P.dma("sp", sc_tm[0:48, :], sconv, writes=["sc_tm"])
            P.dma("sp", h0_tm[0:16, :], slru, writes=["h0_tm"])
            P.dma("sp", gnbc[:], rng.partition_broadcast(128), writes=["gnbc"])
            P.op("pool", lambda e: e.memset(wabd[:], 0.0), writes=["wabd"])
            P.op("pool", lambda e: e.memset(wxbd[:], 0.0), writes=["wxbd"])
            for (wsrc, wdst, key) in ((lwa, wabd, "wabd"), (lwx, wxbd, "wxbd")):
                v = wsrc.rearrange("(j two) c d -> two c j d", two=2)
                P.dma("pool", wdst[0:64, :, 0:64], v[0], reads=[key], writes=[key])
                P.dma("pool", wdst[64:128, :, 64:128], v[1], reads=[key], writes=[key])
            for j in range(8):
                b = nb()
                P.op("pe", lambda e, j=j, b=b: e.transpose(banks[b][:, 0:8], prm_tm[0:8, 128 * j:128 * j + 128], identf[0:8, 0:8]),
                     reads=["prm_tm", "identf"], writes=[BK(b)])
                P.op("pe", lambda e, j=j, b=b: e.transpose(banks[b][:, 8:56], sc_tm[0:48, 128 * j:128 * j + 128], identf[0:48, 0:48]),
                     reads=["sc_tm", "identf"], writes=[BK(b)])
                P.op("pe", lambda e, j=j, b=b: e.transpose(banks[b][:, 56:72], h0_tm[0:16, 128 * j:128 * j + 128], identf[0:16, 0:16]),
                     reads=["h0_tm", "identf"], writes=[BK(b)])
                P.op("dve", lambda e, j=j, b=b: e.tensor_copy(out=prm_fm[:, j, :], in_=banks[b][:, 0:8]), reads=[BK(b)], writes=["prm_fm"])
                P.op("dve", lambda e, j=j, b=b: e.tensor_copy(out=sc_fm[:, j, :], in_=banks[b][:, 8:56]), reads=[BK(b)], writes=["sc_fm"])
                P.op("dve", lambda e, j=j, b=b: e.tensor_copy(out=h0_fm[:, j, :], in_=banks[b][:, 56:72]), reads=[BK(b)], writes=["h0_fm"])
            P.op("act", lambda e: e.activation(out=c8[:], in_=prm_fm[:, :, 7], func=AF.Sigmoid), reads=["prm_fm"], writes=["c8"])
            P.op("act", lambda e: e.activation(out=c8[:], in_=c8[:], func=AF.Ln), reads=["c8"], writes=["c8"])
            P.op("dve", lambda e: e.tensor_scalar_mul(out=c16[:], in0=c8[:], scalar1=16.0), reads=["c8"], writes=["c16"])
            P.op("dve", lambda e: e.tensor_scalar_mul(out=c8[:], in0=c8[:], scalar1=8.0), reads=["c8", "c16"], writes=["c8"])

            do_barrier()

            wslot = [0]
            issued = set()

            def _issue(c):
                if c in issued or c >= len(wl_log):
                    return
                issued.add(c)
                k = c % 2
                first = None
                for fn, src in wl_log[c]:
                    o = P._dma1("pool", fn(wbs[k]), src, (), [("wb", k)], None, first)
                    if first is None:
                        first = o

            def wload(parts, prefetch=True):
                c = wslot[0]
                wslot[0] += 1
                k = c % 2
                if wl_state["dry"]:
                    wl_log.append(parts)
                    return wbs[k], ("wb", k)
                _issue(c)
                if prefetch:
                    _issue(c + 1)
                return wbs[k], ("wb", k)

            def wprefetch(n=2):
                if wl_state["dry"]:
                    return
                for c in range(wslot[0], wslot[0] + n):
                    _issue(c)

            def wview(wb, nk, ncol, off=0):
                return wb[:, off:off + nk * ncol].rearrange("p (k n) -> p k n", k=nk)

            def wsrc(w, r0, nk, c0, ncol):
                return w[r0:r0 + 128 * nk, c0:c0 + ncol].rearrange("(k p) n -> p k n", p=128)

            def norm_T(src_tiles, tiles, grow, dstT, dkey, stage, hbs):
                P.dma("sp", gbc[:], gn3[grow].partition_broadcast(128), writes=["gbc"])
                P.op("dve", lambda e: e.memset(ss[:], 0.0), writes=["ss"])
                info = {}

                def N1(i):
                    t0, n = tiles[i]
                    src, skeys = src_tiles(i)
                    hb = hbs[i % len(hbs)]
                    hk = ("hb", stage, i % len(hbs))
                    info[i] = (src, skeys, hb, hk)
                    P.op("act", lambda e: e.activation(out=hb[0:n, :], in_=src, func=AF.Square, accum_out=ss[0:n, i:i + 1]),
                         reads=skeys + ["ss"], writes=[hk, ("ss", i)])
                    P.op("act", lambda e: e.activation(out=rs[0:n, i:i + 1], in_=ss[0:n, i:i + 1], func=AF.Sqrt, scale=1.0 / D, bias=EPS),
                         reads=[("ss", i)], writes=[("rs", i)])
                    P.op("dve", lambda e: e.reciprocal(out=rs[0:n, i:i + 1], in_=rs[0:n, i:i + 1]), reads=[("rs", i)], writes=[("rs", i)])
                    if dstT is not None:
                        P.op("dve", lambda e: e.scalar_tensor_tensor(out=hb[0:n, :], in0=src, scalar=rs[0:n, i:i + 1], in1=gbc[0:n, :], op0=ALU.mult, op1=ALU.mult),
                             reads=skeys + [("rs", i), "gbc", hk], writes=[hk])

                def N2(i):
                    t0, n = tiles[i]
                    src, skeys, hb, hk = info[i]
                    for half in range(2):
                        b = nb()
                        pv = banks[b][:].bitcast(BF16).rearrange("p (k n) -> p k n", k=8)
                        for kk in range(8):
                            kt = half * 8 + kk
                            P.op("pe", lambda e, pv=pv, kk=kk, kt=kt: e.transpose(pv[:, kk, 0:n], hb[0:n, kt * 128:(kt + 1) * 128], identb[0:n, 0:n]),
                                 reads=[hk, "identb"], writes=[BK(b)])
                        if half == 0:
                            P.op("act", lambda e, pv=pv, half=half: e.copy(out=dstT[:, half * 8:half * 8 + 8, t0:t0 + n], in_=pv[:, :, 0:n]),
                                 reads=[BK(b)], writes=[(dkey, i)])
                        else:
                            P.op("dve", lambda e, pv=pv, half=half: e.tensor_copy(out=dstT[:, half * 8:half * 8 + 8, t0:t0 + n], in_=pv[:, :, 0:n]),
                                 reads=[BK(b)], writes=[(dkey, i)])

                nt_ = len(tiles)
                if dstT is None:
                    for i in range(nt_):
                        N1(i)
                        yield i, info[i][2], info[i][3]
                    return
                N1(0)
                for i in range(nt_):
                    if i + 1 < nt_:
                        N1(i + 1)
                    N2(i)
                    yield i, info[i][2], info[i][3]

            def run_gen(g):
                for _ in g:
                    pass

            def fm_proj(wv, wkey, c0, nk, rhsT, rkey, ntiles, consumer, rkeys_fn=None):
                bl = []
                for (t0, n) in ntiles:
                    b = nb()
                    bl.append(b)
                for kt in range(nk):
                    for (t0, n), b in zip(ntiles, bl):
                        rk = [(rkey, i) for i in (rkeys_fn(t0, n) if rkeys_fn else tt_over(t0, n))]
                        P.op("pe", lambda e, b=b, kt=kt, t0=t0, n=n: e.matmul(
                            banks[b][:, 0:n], lhsT=wv[:, kt, c0:c0 + 128], rhs=rhsT[:, kt, t0:t0 + n],
                            start=(kt == 0), stop=(kt == nk - 1)), reads=[wkey] + rk, writes=[BK(b)])
                for ni, ((t0, n), b) in enumerate(zip(ntiles, bl)):
                    consumer(ni, t0, n, b)

            def tm_proj(wv, wkey, c0, ncol, nk, lhsT, lkeys, t0, n, b):
                for kt in range(nk):
                    P.op("pe", lambda e, kt=kt: e.matmul(
                        banks[b][0:n, 0:ncol], lhsT=lhsT[:, kt, t0:t0 + n], rhs=wv[:, kt, c0:c0 + ncol],
                        start=(kt == 0), stop=(kt == nk - 1)), reads=[wkey] + lkeys, writes=[BK(b)])

            def lru_tile(j, hT, hkey, ntl, Tn, has_s, L, wv, wkey, f1col, f0col):
                q = j % 2
                xa, xc, xcb, r_, i_, a_ = (L[q][k] for k in ("xa", "xc", "xcb", "r", "i", "a"))
                xas = L[q].get("xas")
                ggb = L[q].get("gg")
                t16 = tmp16s[q]
                pre = "m" if has_s else "q"

                def K(nm, ni=None):
                    return ("L", pre, q, nm, ni) if ni is not None else ("L", pre, q, nm)
                allnt = list(range(len(ntl)))
                xar = [K("xa", ni) for ni in allnt] + [K("xah")] + ([K("xash")] if has_s else [])

                def xa_cons(ni, t0, n, b):
                    if t0 < TP:
                        P.op("dve", lambda e: e.tensor_copy(out=xa[:, 3 + t0:3 + t0 + n], in_=banks[b][:, 0:n]),
                             reads=[BK(b)], writes=[K("xa", ni)])
                    else:
                        P.op("dve", lambda e: e.tensor_copy(out=xas[:, :, 3:11], in_=banks[b][:, 0:128].rearrange("p (s t) -> p s t", t=8)),
                             reads=[BK(b)], writes=[K("xa", ni)])
                def LAp():
                    fm_proj(wv, wkey, 0, 16, hT, hkey, ntl, xa_cons)

                def LA():
                    if not has_s:
                        P.op("pool", lambda e: e.tensor_copy(out=xtail[:, j, :], in_=xa[:, TP:TP + 3]), reads=xar, writes=["xtail"])
                    if has_s:
                        def ga_cons(ni, t0, n, b):
                            P.op("act", lambda e: e.activation(out=ggb[:, t0:t0 + n], in_=banks[b][:, 0:n], func=AF.Gelu_apprx_tanh),
                                 reads=[BK(b)], writes=[K("gg")])
                        fm_proj(wv, wkey, 128, 16, hT, hkey, ntl, ga_cons)
                        P.op("dve", lambda e: e.tensor_scalar_mul(out=xa[:, 0:3], in0=xtail[:, j, :], scalar1=flags[:, 0:1]),
                             reads=["xtail", "flags"], writes=[K("xah")])
                        P.op("dve", lambda e: e.tensor_copy(out=xas[:, :, 0:3], in_=sc_fm[:, j, :].rearrange("p (s t) -> p s t", t=3)),
                             reads=["sc_fm"], writes=[K("xash")])
                    else:
                        P.op("dve", lambda e: e.memset(xa[:, 0:3], 0.0), writes=[K("xah")])
                    if has_s:
                        P.op("pool", lambda e: e.tensor_copy(out=cv_fm[:, j, 0:48].rearrange("p (s t) -> p s t", t=3), in_=xas[:, :, 8:11]), reads=xar, writes=["cv_fm"])
                        P.op("pool", lambda e: e.tensor_copy(out=cv_fm[:, j, 48:51], in_=xa[:, TP:TP + 3]), reads=xar, writes=["cv_fm"])
                    views = [(xc[:, 0:TP], lambda k: xa[:, k:k + TP])]
                    if has_s:
                        views.append((xc[:, TP:T].rearrange("p (s t) -> p s t", t=8), lambda k: xas[:, :, k:k + 8]))
                    xbv = [xcb[:, 0:TP]] + ([xcb[:, TP:T].rearrange("p (s t) -> p s t", t=8)] if has_s else [])
                    for vi, (ov, iv) in enumerate(views):
                        P.op("dve", lambda e, ov=ov, iv=iv: e.tensor_scalar(out=ov, in0=iv(0), scalar1=prm_fm[:, j, 0:1], scalar2=prm_fm[:, j, 4:5],
                                                                            op0=ALU.mult, op1=ALU.add),
                             reads=xar + ["prm_fm"], writes=[K("xc")])
                        for k in range(1, 3):
                            P.op("dve", lambda e, ov=ov, iv=iv, k=k: e.scalar_tensor_tensor(out=ov, in0=iv(k), scalar=prm_fm[:, j, k:k + 1], in1=ov,
                                                                                             op0=ALU.mult, op1=ALU.add),
                                 reads=xar + ["prm_fm", K("xc")], writes=[K("xc")])
                        P.op("dve", lambda e, ov=ov, iv=iv, xb=xbv[vi]: e.scalar_tensor_tensor(out=xb, in0=iv(3), scalar=prm_fm[:, j, 3:4], in1=ov,
                                                                                            op0=ALU.mult, op1=ALU.add),
                             reads=xar + ["prm_fm", K("xc")], writes=[K("xcb")])
                        P.op("dve", lambda e, ov=ov, iv=iv: e.scalar_tensor_tensor(out=ov, in0=iv(3), scalar=prm_fm[:, j, 3:4], in1=ov,
                                                                                    op0=ALU.mult, op1=ALU.add),
                             reads=xar + ["prm_fm", K("xc"), K("xcb")], writes=[K("xc")])

                def LB():
                    for ni, (t0, n) in enumerate(ntl):
                        br, bi = nb(), nb()
                        P.op("pe", lambda e, br=br, t0=t0, n=n: e.matmul(banks[br][:, 0:n], lhsT=wabd[:, j, :], rhs=xcb[:, t0:t0 + n], start=True, stop=True),
                             reads=["wabd", K("xcb")], writes=[BK(br)])
                        P.op("pe", lambda e, bi=bi, t0=t0, n=n: e.matmul(banks[bi][:, 0:n], lhsT=wxbd[:, j, :], rhs=xcb[:, t0:t0 + n], start=True, stop=True),
                             reads=["wxbd", K("xcb")], writes=[BK(bi)])
                        P.op("act", lambda e, br=br, t0=t0, n=n: e.activation(out=r_[:, t0:t0 + n], in_=banks[br][:, 0:n], func=AF.Sigmoid, bias=prm_fm[:, j, 5:6]),
                             reads=[BK(br), "prm_fm"], writes=[K("r")])
                        P.op("act", lambda e, bi=bi, t0=t0, n=n: e.activation(out=i_[:, t0:t0 + n], in_=banks[bi][:, 0:n], func=AF.Sigmoid, bias=prm_fm[:, j, 6:7]),
                             reads=[BK(bi), "prm_fm"], writes=[K("i")])
                    P.op("pool", lambda e: e.tensor_tensor(out=i_[:, 0:Tn], in0=i_[:, 0:Tn], in1=xc[:, 0:Tn], op=ALU.mult), reads=[K("i"), K("xc")], writes=[K("i")])
                    P.op("act", lambda e: e.activation(out=a_[:, 0:Tn], in_=r_[:, 0:Tn], func=AF.Exp, scale=c8[:, j:j + 1]), reads=[K("r"), "c8"], writes=[K("a")])
                    P.op("act", lambda e: e.activation(out=r_[:, 0:Tn], in_=r_[:, 0:Tn], func=AF.Exp, scale=c16[:, j:j + 1]), reads=[K("r"), K("a"), "c16"], writes=[K("r")])
                    P.op("act", lambda e: e.activation(out=r_[:, 0:Tn], in_=r_[:, 0:Tn], func=AF.Relu, scale=-1.0, bias=1.0), reads=[K("r")], writes=[K("r")])
                    P.op("act", lambda e: e.activation(out=r_[:, 0:Tn], in_=r_[:, 0:Tn], func=AF.Sqrt), reads=[K("r")], writes=[K("r")])

                def LC():
                    P.op("dve", lambda e: e.tensor_scalar(out=r_[:, 0:1], in0=r_[:, 0:1], scalar1=flags[:, f0col:f0col + 1], scalar2=flags[:, f1col:f1col + 1],
                                                          op0=ALU.mult, op1=ALU.add), reads=[K("r"), "flags"], writes=[K("r")])
                    P.op("dve", lambda e: e.tensor_tensor(out=i_[:, 0:Tn], in0=r_[:, 0:Tn], in1=i_[:, 0:Tn], op=ALU.mult), reads=[K("r"), K("i")], writes=[K("i")])
                    if has_s:
                        a0 = a_[:, TP:T].rearrange("p (s t) -> p s t", t=8)[:, :, 0]
                        b0 = i_[:, TP:T].rearrange("p (s t) -> p s t", t=8)[:, :, 0]
                        P.op("dve", lambda e: e.tensor_tensor(out=t16[:], in0=a0, in1=h0_fm[:, j, :], op=ALU.mult), reads=[K("a"), "h0_fm"], writes=[("t16", q)])
                        P.op("dve", lambda e: e.tensor_tensor(out=b0, in0=b0, in1=t16[:], op=ALU.add), reads=[K("i"), ("t16", q)], writes=[K("i")])
                        P.op("dve", lambda e: e.memset(a0, 0.0), reads=[K("r"), ("t16", q)], writes=[K("a")])
                        P.op("dve", lambda e: e.tensor_tensor_scan(out=r_[:, 0:Tn], data0=a_[:, 0:Tn], data1=i_[:, 0:Tn], initial=hin[:, j:j + 1],
                                                                   op0=ALU.mult, op1=ALU.add), reads=[K("a"), K("i"), "hin"], writes=[K("r")])
                        P.op("pool", lambda e: e.tensor_copy(out=hl_fm[:, j, 0:16], in_=r_[:, TP:T].rearrange("p (s t) -> p s t", t=8)[:, :, 7]),
                             reads=[K("r")], writes=["hl_fm"])
                        P.op("pool", lambda e: e.tensor_copy(out=hl_fm[:, j, 16:17], in_=r_[:, TP - 1:TP]), reads=[K("r")], writes=["hl_fm"])
                        for ni, (t0, n) in enumerate(ntl):
                            eng = "dve" if ni % 2 == 0 else "pool"
                            P.op(eng, lambda e, t0=t0, n=n: e.tensor_tensor(out=yaT[:, j, t0:t0 + n], in0=r_[:, t0:t0 + n], in1=ggb[:, t0:t0 + n], op=ALU.mult),
                                 reads=[K("gg"), K("r")], writes=[("yaT", ni)])
                    else:
                        P.op("dve", lambda e: e.tensor_tensor_scan(out=r_[:, 0:Tn], data0=a_[:, 0:Tn], data1=i_[:, 0:Tn], initial=0.0,
                                                                   op0=ALU.mult, op1=ALU.add), reads=[K("a"), K("i")], writes=[K("r")])
                        P.op("dve", lambda e: e.tensor_scalar_mul(out=hin[:, j:j + 1], in0=r_[:, TP - 1:TP], scalar1=flags[:, 0:1]),
                             reads=[K("r"), "flags"], writes=["hin"])
                return LA, LB, LC, LAp

            def lru_bufs(Tn, has_s):
                Ls = []
                for q in range(2):
                    L = {}
                    L["xa"] = AR.f32(TP + 3 + 1)
                    if has_s:
                        L["xas"] = AR.f32(16, 11)
                        L["gg"] = AR.bf(Tn)
                    for k in ("xc", "r", "i", "a"):
                        L[k] = AR.f32(Tn)
                    L["xcb"] = AR.bf(Tn)
                    Ls.append(L)
                return Ls

            def rope(b, n, ct, st_, tkeys, out, okey, tmp, tk):
                x = banks[b][0:n, 0:256].rearrange("p (h two d) -> p h two d", h=2, two=2)
                x1, x2 = x[:, :, 0, :], x[:, :, 1, :]
                o = out.rearrange("p (h two d) -> p h two d", h=2, two=2)
                t1, t2, t3, t4 = (tmp[0:n, q, :].rearrange("p (h d) -> p h d", h=2) for q in range(4))
                P.op("dve", lambda e: e.tensor_tensor(out=t1, in0=x1, in1=ct, op=ALU.mult), reads=[BK(b)] + tkeys, writes=[(tk, 0)])
                P.op("dve", lambda e: e.tensor_tensor(out=t2, in0=x2, in1=st_, op=ALU.mult), reads=[BK(b)] + tkeys, writes=[(tk, 1)])
                P.op("dve", lambda e: e.tensor_tensor(out=t3, in0=x2, in1=ct, op=ALU.mult), reads=[BK(b)] + tkeys, writes=[(tk, 2)])
                P.op("dve", lambda e: e.tensor_tensor(out=t4, in0=x1, in1=st_, op=ALU.mult), reads=[BK(b)] + tkeys, writes=[(tk, 3)])
                P.op("pool", lambda e: e.tensor_tensor(out=o[:, :, 0, :], in0=t1, in1=t2, op=ALU.subtract), reads=[(tk, 0), (tk, 1)], writes=[okey])
                P.op("pool", lambda e: e.tensor_tensor(out=o[:, :, 1, :], in0=t3, in1=t4, op=ALU.add), reads=[(tk, 2), (tk, 3)], writes=[okey])

            AR.at(0)
            hTq = AR.bf(16, TP)
            xts = [AR.f32(D), AR.f32(D)]
            hbs = [AR.bf(D), AR.bf(D)]
            mark = AR.off

            def pre_src(i):
                t0, n = PT[i]
                xt = xts[i % 2]
                P.dma("sp", xt[0:n, :], xpre[t0:t0 + n, :], writes=[("xt", i % 2)])
                return xt[0:n, :], [("xt", i % 2)]
            run_gen(norm_T(pre_src, PT, 0, hTq, "hTq", "q", hbs))
            L = lru_bufs(TP, False)
            stages = {}

            def pre_make(j):
                wb, wkey = wload([(lambda w: wview(w, 16, 128), wsrc(w_in, 0, 16, 128 * j, 128))])
                stages[j] = lru_tile(j, hTq, "hTq", NTP, TP, False, L, wview(wb, 16, 128), wkey, 3, 4)
            for step in range(-2, 9):
                if 0 <= step - 1 < 8:
                    stages[step - 1][2]()
                if 0 <= step < 8:
                    stages[step][1]()
                if 0 <= step + 1 < 8:
                    stages[step + 1][0]()
                if 0 <= step + 2 < 8:
                    pre_make(step + 2)
                    stages[step + 2][3]()
            khat = AR.bf(9, 256)
            vtm = [AR.bf(256), AR.bf(256)]
            rtmp = [AR.f32(4, 128), AR.f32(4, 128)]
            tbl = [AR.f32(2, 256), AR.f32(2, 256)]
            for hg in range(4):
                wb, wkey = wload([(lambda w: wview(w, 16, 512)[:, :, 0:256], wsrc(w_in, 0, 16, 3072 + 256 * hg, 256)),
                                  (lambda w: wview(w, 16, 512)[:, :, 256:512], wsrc(w_in, 0, 16, 4096 + 256 * hg, 256))])
                wv = wview(wb, 16, 512)
                sb_ = 6 + hg % 2
                pend = [None]
                for i, (t0, n) in enumerate(PT):
                    tb = tbl[i % 2]
                    P.dma("sp", tb[0:n, :, :], tabpre[:, t0:t0 + n, 256 * hg:256 * hg + 256].rearrange("a p n -> p a n"), writes=[("tblq", i % 2)])
                    b = nb()
                    tm_proj(wv, wkey, 0, 256, 16, hTq, [("hTq", i)], t0, n, b)
                    rope(b, n, tb[0:n, 0, :].rearrange("p (h two d) -> p h two d", h=2, two=2)[:, :, 0, :],
                         tb[0:n, 1, :].rearrange("p (h two d) -> p h two d", h=2, two=2)[:, :, 0, :],
                         [("tblq", i % 2)], khat[0:n, i, :], ("khat", i), rtmp[i % 2], ("rtq", i % 2))
                    b2 = nb()
                    tm_proj(wv, wkey, 256, 256, 16, hTq, [("hTq", i)], t0, n, b2)
                    vt = vtm[i % 2]
                    P.op("act", lambda e, vt=vt, n=n, b2=b2: e.copy(out=vt[0:n, :], in_=banks[b2][0:n, 0:256]), reads=[BK(b2)], writes=[("vtq", i % 2)])
                    def s_acc(i=i, n=n, vt=vt):
                        for hh in range(2):
                            P.op("pe", lambda e, hh=hh: e.matmul(
                                banks[6 + hh][:, 0:128], lhsT=khat[0:n, i, hh * 128:hh * 128 + 128], rhs=vt[0:n, hh * 128:hh * 128 + 128],
                                start=(i == 0), stop=(i == 8)), reads=[("khat", i), ("vtq", i % 2)], writes=[BK(6 + hh)])
                    if pend[0] is not None:
                        pend[0]()
                    pend[0] = s_acc
                pend[0]()
                pend[0] = None
                for hh in range(2):
                    P.op("dve", lambda e, hg=hg, hh=hh: e.tensor_scalar_mul(out=spre[:, 2 * hg + hh, :], in0=banks[6 + hh][:, 0:128], scalar1=flags[:, 0:1]),
                         reads=[BK(6 + hh), "flags"], writes=[("spre", hg, hh)])
            do_barrier()

            AR.at(0)
            hT = AR.bf(16, T)
            yaT = AR.bf(8, T)
            ybT = AR.bf(8, T)
            XOFF = AR.off
            cv_fm = AR.f32(8, 51)
            cv_tm = AR.f32(1024)
            hl_tm = AR.f32(1024)
            mark = AR.off
            xts = [AR.f32(D), AR.f32(D)]
            hbs = [AR.bf(D), AR.bf(D)]

            def main_src(i):
                t0, n = TILES[i]
                xt = xts[i % 2]
                P.dma("sp", xt[0:n, :], xall[t0:t0 + n, :], writes=[("xt", i % 2)])
                return xt[0:n, :], [("xt", i % 2)]
            run_gen(norm_T(main_src, TILES, 0, hT, "hT", "m", hbs))
            do_barrier()
            AR.at(mark)
            L = lru_bufs(T, True)
            stages = {}

            def main_make(j):
                wb, wkey = wload([(lambda w: wview(w, 16, 256)[:, :, 0:128], wsrc(w_in, 0, 16, 128 * j, 128)),
                                  (lambda w: wview(w, 16, 256)[:, :, 128:256], wsrc(w_in, 0, 16, 1024 + 128 * j, 128))])
                stages[j] = lru_tile(j, hT, "hT", NT, T, True, L, wview(wb, 16, 256), wkey, 1, 2)
            for step in range(-2, 9):
                if 0 <= step - 1 < 8:
                    stages[step - 1][2]()
                if 0 <= step < 8:
                    stages[step][1]()
                if 0 <= step + 1 < 8:
                    stages[step + 1][0]()
                if 0 <= step + 2 < 8:
                    main_make(step + 2)
                    stages[step + 2][3]()
            for half in range(2):
                b = nb()
                for jj in range(4):
                    j = half * 4 + jj
                    P.op("pe", lambda e, j=j, jj=jj, b=b: e.transpose(banks[b][0:17, jj * 128:jj * 128 + 128], hl_fm[:, j, :], identf[:, :]),
                         reads=["hl_fm", "identf"], writes=[BK(b)])
                P.op("act", lambda e, half=half, b=b: e.copy(out=hl_tm[0:17, half * 512:half * 512 + 512], in_=banks[b][0:17, :]), reads=[BK(b)], writes=["hl_tm"])
            P.dma("sp", nlru_d, hl_tm[0:17, :], reads=["hl_tm"])
            for half in range(2):
                b = nb()
                for jj in range(4):
                    j = half * 4 + jj
                    P.op("pe", lambda e, j=j, jj=jj, b=b: e.transpose(banks[b][0:51, jj * 128:jj * 128 + 128], cv_fm[:, j, :], identf[:, :]),
                         reads=["cv_fm", "identf"], writes=[BK(b)])
                P.op("act", lambda e, half=half, b=b: e.copy(out=cv_tm[0:51, half * 512:half * 512 + 512], in_=banks[b][0:51, :]), reads=[BK(b)], writes=["cv_tm"])
            P.dma("sp", nconv_d, cv_tm[0:51, :], reads=["cv_tm"])
            do_barrier()

            AR.at(XOFF)
            qT = AR.bf(2, T)
            kT = AR.bf(2, T)
            kTM = AR.bf(10, 256)
            vTM = AR.bf(10, 256)
            gsT = AR.bf(10, 256)
            qtmp = [AR.bf(256), AR.bf(256)]
            rtmp = [AR.f32(4, 128), AR.f32(4, 128)]
            tbl = [AR.f32(4, 256), AR.f32(4, 256)]
            gtmp = [AR.f32(256), AR.f32(256)]
            scm = [[AR.bf(128), AR.bf(128)], [AR.bf(128), AR.bf(128)]]
            onb = [AR.f32(128), AR.f32(128)]
            ybt = [[AR.bf(128), AR.bf(128)], [AR.bf(128), AR.bf(128)]]
            s32 = [AR.f32(128), AR.f32(128)]
            s16 = [[AR.bf(128), AR.bf(128)], [AR.bf(128), AR.bf(128)]]
            scm_s = [AR.bf(128), AR.bf(128)]
            ybt_s = [AR.bf(128), AR.bf(128)]
            kvs = [AR.f32(128), AR.f32(128)]
            s0f = AR.f32(16, 128)
            s0b = AR.bf(16, 128)
            sout = s0f
            qm = AR.bf(2176)
            km = AR.bf(16, 128)
            P.op("pool", lambda e: e.memset(qm[:], 0.0), writes=["qm"])

            for hg in range(4):
                wb1, wk1 = wload([(lambda w: wview(w, 16, 512)[:, :, 0:256], wsrc(w_in, 0, 16, 2048 + 256 * hg, 256)),
                                  (lambda w: wview(w, 16, 512)[:, :, 256:512], wsrc(w_in, 0, 16, 3072 + 256 * hg, 256))])
                wb2, wk2 = wload([(lambda w: wview(w, 16, 512)[:, :, 0:256], wsrc(w_in, 0, 16, 4096 + 256 * hg, 256)),
                                  (lambda w: wview(w, 16, 512)[:, :, 256:512], wsrc(w_in, 0, 16, 5120 + 256 * hg, 256))], prefetch=False)
                wv1, wv2 = wview(wb1, 16, 512), wview(wb2, 16, 512)
                for i, (t0, n) in enumerate(TILES):
                    tb = tbl[i % 2]
                    P.dma("sp", tb[0:n, :, :], tab[:, t0:t0 + n, 256 * hg:256 * hg + 256].rearrange("a p n -> p a n"), writes=[("tbl", i % 2)])

                    def tv(a):
                        return tb[0:n, a, :].rearrange("p (h two d) -> p h two d", h=2, two=2)[:, :, 0, :]
                    b = nb()
                    tm_proj(wv1, wk1, 0, 256, 16, hT, [("hT", i)], t0, n, b)
                    qt = qtmp[i % 2]
                    rope(b, n, tv(0), tv(1), [("tbl", i % 2)], qt[0:n, :], ("qt", i % 2), rtmp[i % 2], ("rt", i % 2))
                    b = nb()
                    tm_proj(wv1, wk1, 256, 256, 16, hT, [("hT", i)], t0, n, b)
                    rope(b, n, tv(2), tv(3), [("tbl", i % 2)], kTM[0:n, i, :], ("kTM", i), rtmp[i % 2], ("rt", i % 2))
                    b = nb()
                    tm_proj(wv2, wk2, 0, 256, 16, hT, [("hT", i)], t0, n, b)
                    P.op("act", lambda e, i=i, n=n, b=b: e.copy(out=vTM[0:n, i, :], in_=banks[b][0:n, 0:256]), reads=[BK(b)], writes=[("vTM", i)])
                    b = nb()
                    tm_proj(wv2, wk2, 256, 256, 16, hT, [("hT", i)], t0, n, b)
                    gt = gtmp[i % 2]
                    P.op("act", lambda e, gt=gt, n=n, b=b: e.activation(out=gt[0:n, :], in_=banks[b][0:n, 0:256], func=AF.Silu), reads=[BK(b)], writes=[("gt", i % 2)])
                    P.op("pool", lambda e, gt=gt, i=i, n=n, hg=hg: e.tensor_tensor(out=gsT[0:n, i, :], in0=gt[0:n, :], in1=gnbc[0:n, 256 * hg:256 * hg + 256], op=ALU.mult),
                         reads=[("gt", i % 2), "gnbc"], writes=[("gsT", i)])
                    b = nb()
                    pv = banks[b][:].bitcast(BF16).rearrange("p (k n) -> p k n", k=8)
                    for hh in range(2):
                        P.op("pe", lambda e, pv=pv, hh=hh, qt=qt, n=n: e.transpose(pv[:, hh, 0:n], qt[0:n, hh * 128:hh * 128 + 128], identb[0:n, 0:n]),
                             reads=[("qt", i % 2), "identb"], writes=[BK(b)])
                        P.op("pe", lambda e, pv=pv, hh=hh, i=i, n=n: e.transpose(pv[:, 2 + hh, 0:n], kTM[0:n, i, hh * 128:hh * 128 + 128], identb[0:n, 0:n]),
                             reads=[("kTM", i), "identb"], writes=[BK(b)])
                    P.op("act", lambda e, pv=pv, t0=t0, n=n: e.copy(out=qT[:, :, t0:t0 + n], in_=pv[:, 0:2, 0:n]), reads=[BK(b)], writes=[("qT", i)])
                    P.op("act", lambda e, pv=pv, t0=t0, n=n: e.copy(out=kT[:, :, t0:t0 + n], in_=pv[:, 2:4, 0:n]), reads=[BK(b)], writes=[("kT", i)])
                for hh in range(2):
                    h = 2 * hg + hh
                    P.op("pool", lambda e, h=h, hh=hh: e.tensor_copy(out=s32[hh][:, :], in_=spre[:, h, :]), reads=[("spre", hg, hh)], writes=[("s32", hh)])
                    P.op("act", lambda e, hh=hh: e.copy(out=s16[hh][0][:, :], in_=s32[hh][:, :]), reads=[("s32", hh)], writes=[("s16", hh, 0)])
                wprefetch(2)
                items = [(i, hh) for i in range(10) for hh in range(2)]
                ctx = {}

                def S1a(k):
                    i, hh = items[k]
                    h = 2 * hg + hh
                    hs = slice(hh * 128, hh * 128 + 128)
                    P.dma("sp", s0f[:, :, :], sret[:, h, :, :].rearrange("s d v -> d s v"), writes=["s0f"])
                    P.op("act", lambda e: e.copy(out=s0b[:, :, :], in_=s0f[:, :, :]), reads=["s0f"], writes=["s0b"])
                    P.op("pool", lambda e: e.tensor_copy(out=qm[:, 0:2176].rearrange("p (s x) -> p s x", x=136)[:, :, 0:8],
                                                          in_=qT[:, hh, TP:T].rearrange("p (s t) -> p s t", t=8)),
                         reads=[("qT", 9), "qm"], writes=["qm"])
                    P.op("pool", lambda e: e.tensor_tensor(out=km[:, :, :], in0=kTM[:, 9, hs].unsqueeze(1).broadcast_to([128, 16, 128]),
                                                            in1=selm[:, :].unsqueeze(2).broadcast_to([128, 16, 128]), op=ALU.mult),
                         reads=[("kTM", 9), "selm"], writes=["km"])

                def S1(k):
                    i, hh = items[k]
                    t0, n = TILES[i]
                    bs = nb()
                    P.op("pe", lambda e: e.matmul(banks[bs][0:n, 0:n], lhsT=kT[:, hh, t0:t0 + n], rhs=qT[:, hh, t0:t0 + n], start=True, stop=True),
                         reads=[("kT", i), ("qT", i)], writes=[BK(bs)])
                    sm = scm_s[hh]
                    P.op("dve", lambda e: e.tensor_tensor(out=sm[0:n, 0:n], in0=banks[bs][0:n, 0:n], in1=masks[0:n, 1, 0:n], op=ALU.mult),
                         reads=[BK(bs), "masks"], writes=[("scm_s", hh)])

                def S2(k):
                    i, hh = items[k]
                    t0, n = TILES[i]
                    h = 2 * hg + hh
                    hs = slice(hh * 128, hh * 128 + 128)
                    samp = (i == 9)
                    sm = scm_s[hh]
                    bo = nb()
                    P.op("pe", lambda e: e.matmul(banks[bo][0:n, 0:128], lhsT=sm[0:n, 0:n], rhs=vTM[0:n, i, hs], start=True, stop=False),
                         reads=[("scm_s", hh), ("vTM", i)], writes=[BK(bo)])
                    if not samp:
                        P.op("pe", lambda e: e.matmul(banks[bo][0:n, 0:128], lhsT=qT[:, hh, t0:t0 + n], rhs=s16[hh][1][:, :], start=False, stop=True),
                             reads=[("qT", i), ("s16", hh, 1)], writes=[BK(bo)])
                    else:
                        for s_ in range(16):
                            P.op("pe", lambda e, s_=s_: e.matmul(banks[bo][0:128, 0:128], lhsT=qm[:, 128 * s_:128 * s_ + 128], rhs=s0b[:, s_, :], start=False, stop=(s_ == 15)),
                                 reads=["qm", "s0b"], writes=[BK(bo)])
                    P.op("dve", lambda e: e.bn_stats(out=bnst[0:n, hh, :], in_=banks[bo][0:n, 0:128]), reads=[BK(bo)], writes=[("bnst", hh)])
                    P.op("dve", lambda e: e.bn_aggr(out=mv[0:n, hh, :], in_=bnst[0:n, hh, :]), reads=[("bnst", hh)], writes=[("mv", hh)])
                    P.op("act", lambda e: e.activation(out=rsd[0:n, hh:hh + 1], in_=mv[0:n, hh, 1:2], func=AF.Sqrt, scale=1.0, bias=EPS),
                         reads=[("mv", hh)], writes=[("rsd", hh)])
                    P.op("dve", lambda e: e.reciprocal(out=rsd[0:n, hh:hh + 1], in_=rsd[0:n, hh:hh + 1]), reads=[("rsd", hh)], writes=[("rsd", hh)])
                    on = onb[hh]
                    P.op("dve", lambda e: e.tensor_scalar(out=on[0:n, :], in0=banks[bo][0:n, 0:128], scalar1=mv[0:n, hh, 0:1], scalar2=rsd[0:n, hh:hh + 1],
                                                          op0=ALU.subtract, op1=ALU.mult),
                         reads=[BK(bo), ("mv", hh), ("rsd", hh)], writes=[("on", hh)])
                    yb = ybt_s[hh]
                    P.op("pool", lambda e: e.tensor_tensor(out=yb[0:n, :], in0=on[0:n, :], in1=gsT[0:n, i, hs], op=ALU.mult),
                         reads=[("on", hh), ("gsT", i)], writes=[("ybt_s", hh)])

                def S3(k):
                    i, hh = items[k]
                    t0, n = TILES[i]
                    h = 2 * hg + hh
                    hs = slice(hh * 128, hh * 128 + 128)
                    samp = (i == 9)
                    yb = ybt_s[hh]
                    bt = nb()
                    pv = banks[bt][:].bitcast(BF16)
                    P.op("pe", lambda e: e.transpose(pv[:, 0:n], yb[0:n, :], identb[0:n, 0:n]), reads=[("ybt_s", hh), "identb"], writes=[BK(bt)])
                    P.op("act", lambda e: e.copy(out=ybT[:, h, t0:t0 + n], in_=pv[:, 0:n]), reads=[BK(bt)], writes=[("ybT", h, i)])
                    if not samp:
                        bk = nb()
                        P.op("pe", lambda e: e.matmul(banks[bk][:, 0:128], lhsT=kTM[0:n, i, hs], rhs=vTM[0:n, i, hs], start=True, stop=True),
                             reads=[("kTM", i), ("vTM", i)], writes=[BK(bk)])
                        gl = float(GAM[h] ** n)
                        last = (i == 8)
                        P.op("act", lambda e: e.mul(out=kvs[hh][:, :], in_=banks[bk][:, 0:128], mul=gl), reads=[BK(bk)], writes=[("kvs", hh)])
                        dst = sfin[:, h, :] if last else s32[hh][:, :]
                        P.op("dve", lambda e: e.scalar_tensor_tensor(out=dst, in0=s32[hh][:, :], scalar=gl, in1=kvs[hh][:, :], op0=ALU.mult, op1=ALU.add),
                             reads=[("kvs", hh), ("s32", hh)], writes=[("sfin", h)] if last else [("s32", hh)])
                        if not last:
                            P.op("pool", lambda e: e.tensor_copy(out=s16[hh][1][:, :], in_=s32[hh][:, :]), reads=[("s32", hh)], writes=[("s16", hh, 1)])
                    else:
                        g8 = float(GAM[h] ** 8)
                        for q4 in range(4):
                            bk = nb()
                            for s4 in range(4):
                                s_ = 4 * q4 + s4
                                P.op("pe", lambda e, bk=bk, s_=s_, s4=s4: e.matmul(banks[bk][:, 128 * s4:128 * s4 + 128], lhsT=km[:, s_, :], rhs=vTM[:, 9, hs], start=True, stop=True),
                                     reads=["km", ("vTM", 9)], writes=[BK(bk)])
                            P.op("dve", lambda e, bk=bk, q4=q4: e.tensor_tensor(out=sout[:, 4 * q4:4 * q4 + 4, :], in0=banks[bk][:, :].rearrange("p (s v) -> p s v", s=4),
                                                                               in1=s0f[:, 4 * q4:4 * q4 + 4, :], op=ALU.add),
                                 reads=[BK(bk), "s0f"], writes=["s0f"])
                            P.op("act", lambda e, q4=q4: e.mul(out=sout[:, 4 * q4:4 * q4 + 4, :], in_=sout[:, 4 * q4:4 * q4 + 4, :], mul=g8),
                                 reads=["s0f"], writes=["s0f"])
                        P.dma("sp", nrets_d[:, h, :, :].rearrange("s d v -> d s v"), sout[:, :, :], reads=["s0f"])

                def R1(i):
                    t0, n = TILES[i]
                    for hh in range(2):
                        bs = nb()
                        P.op("pe", lambda e, bs=bs, hh=hh: e.matmul(banks[bs][0:n, 0:n], lhsT=kT[:, hh, t0:t0 + n], rhs=qT[:, hh, t0:t0 + n], start=True, stop=True),
                             reads=[("kT", i), ("qT", i)], writes=[BK(bs)])
                        sm = scm[hh][i % 2]
                        P.op("dve", lambda e, bs=bs, sm=sm: e.tensor_tensor(out=sm[0:n, 0:n], in0=banks[bs][0:n, 0:n], in1=masks[0:n, 0, 0:n], op=ALU.mult),
                             reads=[BK(bs), "masks"], writes=[("scm", hh, i % 2)])

                def R2(i):
                    t0, n = TILES[i]
                    for hh in range(2):
                        hs = slice(hh * 128, hh * 128 + 128)
                        sm = scm[hh][i % 2]
                        bo = nb()
                        P.op("pe", lambda e, bo=bo, sm=sm, hs=hs: e.matmul(banks[bo][0:n, 0:128], lhsT=sm[0:n, 0:n], rhs=vTM[0:n, i, hs], start=True, stop=False),
                             reads=[("scm", hh, i % 2), ("vTM", i)], writes=[BK(bo)])
                        P.op("pe", lambda e, bo=bo, hh=hh: e.matmul(banks[bo][0:n, 0:128], lhsT=qT[:, hh, t0:t0 + n], rhs=s16[hh][i % 2][:, :], start=False, stop=True),
                             reads=[("qT", i), ("s16", hh, i % 2)], writes=[BK(bo)])
                        P.op("dve", lambda e, bo=bo, hh=hh: e.bn_stats(out=bnst[0:n, hh, :], in_=banks[bo][0:n, 0:128]), reads=[BK(bo)], writes=[("bnst", hh)])
                        P.op("dve", lambda e, hh=hh: e.bn_aggr(out=mv[0:n, hh, :], in_=bnst[0:n, hh, :]), reads=[("bnst", hh)], writes=[("mv", hh)])
                        P.op("act", lambda e, hh=hh: e.activation(out=rsd[0:n, hh:hh + 1], in_=mv[0:n, hh, 1:2], func=AF.Sqrt, scale=1.0, bias=EPS),
                             reads=[("mv", hh)], writes=[("rsd", hh)])
                        P.op("dve", lambda e, hh=hh: e.reciprocal(out=rsd[0:n, hh:hh + 1], in_=rsd[0:n, hh:hh + 1]), reads=[("rsd", hh)], writes=[("rsd", hh)])
                        on = onb[hh]
                        P.op("dve", lambda e, bo=bo, hh=hh, on=on: e.tensor_scalar(out=on[0:n, :], in0=banks[bo][0:n, 0:128], scalar1=mv[0:n, hh, 0:1], scalar2=rsd[0:n, hh:hh + 1],
                                                                                   op0=ALU.subtract, op1=ALU.mult),
                             reads=[BK(bo), ("mv", hh), ("rsd", hh)], writes=[("on", hh)])
                        yb = ybt[hh][i % 2]
                        P.op("pool", lambda e, on=on, yb=yb, hs=hs: e.tensor_tensor(out=yb[0:n, :], in0=on[0:n, :], in1=gsT[0:n, i, hs], op=ALU.mult),
                             reads=[("on", hh), ("gsT", i)], writes=[("ybt", hh, i % 2)])

                def RU(i):
                    t0, n = TILES[i]
                    for hh in range(2):
                        h = 2 * hg + hh
                        hs = slice(hh * 128, hh * 128 + 128)
                        bk = nb()
                        P.op("pe", lambda e, bk=bk, hs=hs: e.matmul(banks[bk][:, 0:128], lhsT=kTM[0:n, i, hs], rhs=vTM[0:n, i, hs], start=True, stop=True),
                             reads=[("kTM", i), ("vTM", i)], writes=[BK(bk)])
                        gl = float(GAM[h] ** n)
                        last = (i == 8)
                        P.op("act", lambda e, bk=bk, gl=gl, hh=hh: e.mul(out=kvs[hh][:, :], in_=banks[bk][:, 0:128], mul=gl), reads=[BK(bk)], writes=[("kvs", hh)])
                        dst = sfin[:, h, :] if last else s32[hh][:, :]
                        P.op("dve", lambda e, gl=gl, dst=dst, hh=hh: e.scalar_tensor_tensor(out=dst, in0=s32[hh][:, :], scalar=gl, in1=kvs[hh][:, :], op0=ALU.mult, op1=ALU.add),
                             reads=[("kvs", hh), ("s32", hh)], writes=[("sfin", h)] if last else [("s32", hh)])
                        if not last:
                            P.op("pool", lambda e, hh=hh: e.tensor_copy(out=s16[hh][(i + 1) % 2][:, :], in_=s32[hh][:, :]), reads=[("s32", hh)], writes=[("s16", hh, (i + 1) % 2)])

                def RT(i):
                    t0, n = TILES[i]
                    for hh in range(2):
                        h = 2 * hg + hh
                        yb = ybt[hh][i % 2]
                        bt = nb()
                        pv = banks[bt][:].bitcast(BF16)
                        P.op("pe", lambda e, pv=pv, yb=yb: e.transpose(pv[:, 0:n], yb[0:n, :], identb[0:n, 0:n]), reads=[("ybt", hh, i % 2), "identb"], writes=[BK(bt)])
                        P.op("act", lambda e, pv=pv, h=h: e.copy(out=ybT[:, h, t0:t0 + n], in_=pv[:, 0:n]), reads=[BK(bt)], writes=[("ybT", h, i)])

                samp_sched = {-2: [lambda: S1a(18)], 0: [lambda: S1(18)], 2: [lambda: S2(18)], 4: [lambda: S3(18), lambda: S1a(19)],
                              5: [lambda: S1(19)], 7: [lambda: S2(19)], 9: [lambda: S3(19)]}
                for t in range(-2, 10):
                    for f in samp_sched.get(t, []):
                        f()
                    if 0 <= t - 1 <= 8:
                        RT(t - 1)
                    if 0 <= t + 1 <= 8:
                        RU(t + 1)
                    if 0 <= t + 2 <= 8:
                        R1(t + 2)
                    if 0 <= t + 1 <= 8:
                        R2(t + 1)
            P.dma("sp", nretp_d.rearrange("h d v -> d h v"), sfin[:, :, :], reads=[("sfin", h) for h in range(8)])
            do_barrier()
            if debug:
                P.dma("sp", dbg_hT, hT[:, :, :])
                P.dma("sp", dbg_yaT, yaT[:, :, :])
                P.dma("sp", dbg_ybT, ybT[:, :, :])
                do_barrier()

            AR.at(ARW - 8 * T)
            mT = AR.bf(16, T)
            AR.at(XOFF)
            sga = [AR.f32(344), AR.f32(344)]
            sgb = [AR.f32(344), AR.f32(344)]
            t1b = [AR.f32(344), AR.f32(344)]
            t2b = [AR.f32(344), AR.f32(344)]
            assert AR.off <= ARW - 8 * T
            for m in range(16):
                wb, wkey = wload([(lambda w: w[:, 0:2048].rearrange("p (k n) -> p k n", k=16), wsrc(w_in, 0, 16, 6144 + 128 * m, 128)),
                                  (lambda w: w[:, 2048:4096].rearrange("p (k n) -> p k n", k=16), wsrc(w_in, 0, 16, 8192 + 128 * m, 128)),
                                  (lambda w: w[:, 4096:5120].rearrange("p (k n) -> p k n", k=8), wsrc(p_a, 0, 8, 128 * m, 128)),
                                  (lambda w: w[:, 5120:6144].rearrange("p (k n) -> p k n", k=8), wsrc(p_b, 0, 8, 128 * m, 128))])
                wga = wb[:, 0:2048].rearrange("p (k n) -> p k n", k=16)
                wgb = wb[:, 2048:4096].rearrange("p (k n) -> p k n", k=16)
                wpa = wb[:, 4096:5120].rearrange("p (k n) -> p k n", k=8)
                wpb = wb[:, 5120:6144].rearrange("p (k n) -> p k n", k=8)
                for ni, (t0, n) in enumerate(NT):
                    res = {}

                    def cons(name):
                        def c(ni_, t0_, n_, b):
                            res[name] = b
                        return c
                    fm_proj(wga, wkey, 0, 16, hT, "hT", [(t0, n)], cons("ga"))
                    fm_proj(wgb, wkey, 0, 16, hT, "hT", [(t0, n)], cons("gb"))
                    fm_proj(wpa, wkey, 0, 8, yaT, "yaT", [(t0, n)], cons("pa"), rkeys_fn=nt_over)
                    ybk = [("ybT", h, i) for h in range(8) for i in tt_over(t0, n)]
                    bpb = nb()
                    for kt in range(8):
                        P.op("pe", lambda e, kt=kt, bpb=bpb, t0=t0, n=n, wpb=wpb: e.matmul(banks[bpb][:, 0:n], lhsT=wpb[:, kt, :], rhs=ybT[:, kt, t0:t0 + n], start=(kt == 0), stop=(kt == 7)),
                             reads=[wkey] + [("ybT", kt, i) for i in tt_over(t0, n)], writes=[BK(bpb)])
                    q = (m * 4 + ni) % 2
                    P.op("act", lambda e, q=q, n=n, b=res["ga"]: e.activation(out=sga[q][:, 0:n], in_=banks[b][:, 0:n], func=AF.Sigmoid), reads=[BK(res["ga"])], writes=[("sga", q)])
                    P.op("act", lambda e, q=q, n=n, b=res["gb"]: e.activation(out=sgb[q][:, 0:n], in_=banks[b][:, 0:n], func=AF.Sigmoid), reads=[BK(res["gb"])], writes=[("sgb", q)])
                    P.op("dve", lambda e, q=q, n=n, b=res["pa"]: e.tensor_tensor(out=t1b[q][:, 0:n], in0=banks[b][:, 0:n], in1=sga[q][:, 0:n], op=ALU.mult),
                         reads=[BK(res["pa"]), ("sga", q)], writes=[("t1", q)])
                    P.op("dve", lambda e, q=q, n=n, b=bpb: e.tensor_tensor(out=t2b[q][:, 0:n], in0=banks[b][:, 0:n], in1=sgb[q][:, 0:n], op=ALU.mult),
                         reads=[BK(bpb), ("sgb", q)], writes=[("t2", q)])
                    P.op("pool", lambda e, q=q, n=n, m=m, t0=t0: e.tensor_tensor(out=mT[:, m, t0:t0 + n], in0=t1b[q][:, 0:n], in1=t2b[q][:, 0:n], op=ALU.add),
                         reads=[("t1", q), ("t2", q)], writes=[("mT", ni)])
            do_barrier()

            if debug:
                P.dma("sp", dbg_mT, mT[:, :, :])
                do_barrier()
            AR.at(0)
            acc = AR.f32(10, D)
            assert AR.off <= ARW - 8 * T
            for i, (t0, n) in enumerate(TILES):
                P.dma("sp", acc[0:n, i, :], xall[t0:t0 + n, :], writes=[("acc", i)])
            for cg in range(4):
                wb, wkey = wload([(lambda w: wview(w, 16, 512), wsrc(w_out, 0, 16, 512 * cg, 512))])
                wv = wview(wb, 16, 512)
                for i, (t0, n) in enumerate(TILES):
                    b = nb(0, 8)
                    tm_proj(wv, wkey, 0, 512, 16, mT, [("mT", q) for q in nt_over(t0, n)], t0, n, b)
                    P.op("dve", lambda e, i=i, n=n, b=b, cg=cg: e.tensor_tensor(out=acc[0:n, i, 512 * cg:512 * cg + 512], in0=banks[b][0:n, :], in1=acc[0:n, i, 512 * cg:512 * cg + 512], op=ALU.add),
                         reads=[BK(b), ("acc", i)], writes=[("acc", i)])
            do_barrier()

            if debug:
                P.dma("sp", dbg_x1, acc[:, :, :])
                do_barrier()
            AR.at(10 * D)
            h2T = AR.bf(16, T)
            uoff = AR.off
            uT = AR.bf(8, T)
            rtm = [AR.bf(344), AR.bf(344)]
            AR.at(uoff)
            hbs = [AR.bf(D), AR.bf(D)]

            def acc_src(i):
                t0, n = TILES[i]
                return acc[0:n, i, :], [("acc", i)]
            run_gen(norm_T(acc_src, TILES, 1, h2T, "h2T", "f", hbs))
            do_barrier()
            for fb in range(8):
                for sub in range(2):
                    wb, wkey = wload([(lambda w: wview(w, 16, 512), wsrc(w_up, 0, 16, 1024 * fb + 512 * sub, 512))])
                    wv = wview(wb, 16, 512)
                    for mm in range(4):
                        ft = 4 * sub + mm

                        def up_cons(ni, t0, n, b, ft=ft):
                            q = ni % 2
                            P.op("act", lambda e: e.activation(out=rtm[q][:, 0:n], in_=banks[b][:, 0:n], func=AF.Relu), reads=[BK(b)], writes=[("rtm", q)])
                            P.op("pool", lambda e: e.tensor_tensor(out=uT[:, ft, t0:t0 + n], in0=rtm[q][:, 0:n], in1=rtm[q][:, 0:n], op=ALU.mult),
                                 reads=[("rtm", q)], writes=[("uT", ft, ni)])
                        fm_proj(wv, wkey, 128 * mm, 16, h2T, "h2T", NT, up_cons)
                for cgp in range(2):
                    wb, wkey = wload([(lambda w: wview(w, 8, 1024), wsrc(w_down, 1024 * fb, 8, 1024 * cgp, 1024))])
                    wv = wview(wb, 8, 1024)
                    for c2 in range(2):
                        cg = 2 * cgp + c2
                        for i, (t0, n) in enumerate(TILES):
                            b = nb(0, 8)
                            for kt in range(8):
                                P.op("pe", lambda e, kt=kt, b=b, t0=t0, n=n, c2=c2, wv=wv: e.matmul(banks[b][0:n, :], lhsT=uT[:, kt, t0:t0 + n], rhs=wv[:, kt, 512 * c2:512 * c2 + 512],
                                                                                         start=(kt == 0), stop=(kt == 7)),
                                     reads=[wkey] + [("uT", kt, q) for q in nt_over(t0, n)], writes=[BK(b)])
                            P.op("dve", lambda e, i=i, n=n, b=b, cg=cg: e.tensor_tensor(out=acc[0:n, i, 512 * cg:512 * cg + 512], in0=banks[b][0:n, :], in1=acc[0:n, i, 512 * cg:512 * cg + 512], op=ALU.add),
                                 reads=[BK(b), ("acc", i)], writes=[("acc", i)])
            do_barrier()
            if debug:
                P.dma("sp", dbg_x2, acc[:, :, :])
                do_barrier()
            for i, hb, hk in norm_T(acc_src, TILES, 2, None, None, "z", hbs):
                t0, n = TILES[i]
                P.op("dve", lambda e, i=i, n=n: e.scalar_tensor_tensor(out=acc[0:n, i, :], in0=acc[0:n, i, :], scalar=rs[0:n, i:i + 1], in1=gbc[0:n, :], op0=ALU.mult, op1=ALU.mult),
                     reads=[("acc", i), ("rs", i), "gbc"], writes=[("acc", i)])
                P.dma("sp", y_d[t0:t0 + n, :], acc[0:n, i, :], reads=[("acc", i)])

            return P

        program()
        wl_state["dry"] = False
        build_nc.marks = []
        P = program()
        P.emit()
        build_nc.stats = P.stats
    return nc


def _tables(core):
    half = core % 2
    f32 = np.float32
    inv = (f32(10000.0) ** (-(np.arange(0, 128, 2, dtype=f32)) / f32(128))).astype(f32)
    t = np.arange(T)
    pos = np.where(t < TP, half * TP + t, 16384 + (t - TP) % 8).astype(f32)
    l = np.where(t < 928, t % CH, np.where(t < TP, t - 928, (t - TP) % 8)).astype(np.float64)
    ang = (pos[:, None] * inv[None, :]).astype(f32).astype(np.float64)
    cos, sin = np.cos(ang), np.sin(ang)
    gam = np.array(GAM, np.float64)
    dq = gam[None, :] ** (l[:, None] + 1.0)
    dk = gam[None, :] ** (-(l[:, None] + 1.0)) * (128.0 ** -0.5)
    tab = np.zeros((4, T, NH, 64), f32)
    tab[0] = cos[:, None, :] * dq[:, :, None]
    tab[1] = sin[:, None, :] * dq[:, :, None]
    tab[2] = cos[:, None, :] * dk[:, :, None]
    tab[3] = sin[:, None, :] * dk[:, :, None]
    tab = np.repeat(tab[:, :, :, None, :], 2, axis=3).reshape(4, T, 1024)
    tp = np.arange(TP)
    posq = tp.astype(f32)
    angq = (posq[:, None] * inv[None, :]).astype(f32).astype(np.float64)
    wq = gam[None, :] ** (TP - 1.0 - tp[:, None]) * (128.0 ** -0.5)
    tabq = np.zeros((2, TP, NH, 64), f32)
    tabq[0] = np.cos(angq)[:, None, :] * wq[:, :, None]
    tabq[1] = np.sin(angq)[:, None, :] * wq[:, :, None]
    tabq = np.repeat(tabq[:, :, :, None, :], 2, axis=3).reshape(2, TP, 1024)
    return tab, tabq


_NC_CACHE = {}


def kernel(x_prompt, x_sample, state_conv, state_lru, state_ret, meta_tokens, norm_mix_g, w_in, conv_w,
           conv_b, lru_wa, lru_ba, lru_wx, lru_bx, lru_lam, ret_norm_g, p_a, p_b, w_out, norm_ffn_g,
           w_up, w_down, norm_f_g):
    f32 = np.float32
    A = lambda a: np.ascontiguousarray(np.asarray(a, dtype=f32))
    x_prompt, x_sample = A(x_prompt), A(x_sample)
    meta = A(meta_tokens)
    dbg = bool(_NC_CACHE.get("debug"))
    if ("nc", dbg) not in _NC_CACHE:
        _NC_CACHE[("nc", dbg)] = build_nc(debug=dbg)
    nc = _NC_CACHE[("nc", dbg)]
    pp = np.arange(128)
    maskp = (pp[None, :] >= pp[:, None]).astype(f32)
    masks_ = maskp * (pp[None, :] // 8 == pp[:, None] // 8).astype(f32)
    masks = np.stack([maskp, masks_])
    selm = (pp[:, None] // 8 == np.arange(16)[None, :]).astype(f32)
    prm = np.concatenate([A(conv_w)[0], A(conv_b), A(lru_ba), A(lru_bx), A(lru_lam)], axis=0)
    gn3 = np.stack([A(norm_mix_g)[0], A(norm_ffn_g)[0], A(norm_f_g)])
    shared = dict(masks=masks, selm=selm, prm=A(prm), gn3=A(gn3), rng=A(ret_norm_g)[0], w_in=A(w_in)[0],
                  lwa=A(lru_wa)[0], lwx=A(lru_wx)[0], p_a=A(p_a)[0], p_b=A(p_b)[0], w_out=A(w_out)[0],
                  w_up=A(w_up)[0], w_down=A(w_down)[0])
    in_maps = []
    tabs = [_tables(0), _tables(1)]
    for c in range(8):
        b, half = c // 2, c % 2
        seq = np.concatenate([meta, x_prompt[b]], axis=0)
        own = seq[half * TP:(half + 1) * TP]
        pre = seq[0:TP]
        xs = x_sample[16 * c:16 * c + 16].reshape(128, D)
        fl = np.zeros((128, 8), f32)
        fl[:, 0] = half
        fl[:, 1] = 1 - half
        fl[:, 2] = half
        fl[:, 3] = 1.0
        fl[:, 4] = 0.0
        m = dict(shared)
        m.update(xall=A(np.concatenate([own, xs], axis=0)), xpre=A(pre),
                 sconv=A(state_conv[0, 16 * c:16 * c + 16].reshape(48, 1024)),
                 slru=A(state_lru[0, 16 * c:16 * c + 16]), sret=A(state_ret[0, 16 * c:16 * c + 16]),
                 flags=fl, tab=tabs[half][0], tabpre=tabs[half][1])
        in_maps.append(m)
    res = run_bass_kernel_spmd(nc, in_maps, core_ids=list(range(8)))
    R = res.results
    if dbg:
        _NC_CACHE["raw"] = R
    y_prompt = np.zeros((4, 2048, D), f32)
    y_sample = np.zeros((128, 8, D), f32)
    ncp = np.zeros((1, 4, 3, 1024), f32)
    nlp = np.zeros((1, 4, 1024), f32)
    nrp = np.zeros((1, 4, NH, 128, 128), f32)
    ncs = np.zeros((1, 128, 3, 1024), f32)
    nls = np.zeros((1, 128, 1024), f32)
    nrs = np.zeros((1, 128, NH, 128, 128), f32)
    for c in range(8):
        b, half = c // 2, c % 2
        r = R[c]
        y = np.asarray(r["y"])
        if half == 0:
            y_prompt[b, 0:TP - 16] = y[16:TP]
        else:
            y_prompt[b, TP - 16:] = y[0:TP]
            ncp[0, b] = np.asarray(r["nconv"])[48:51]
            nlp[0, b] = np.asarray(r["nlru"])[16]
            nrp[0, b] = np.asarray(r["nretp"])
        y_sample[16 * c:16 * c + 16] = y[TP:T].reshape(16, 8, D)
        ncs[0, 16 * c:16 * c + 16] = np.asarray(r["nconv"])[0:48].reshape(16, 3, 1024)
        nls[0, 16 * c:16 * c + 16] = np.asarray(r["nlru"])[0:16]
        nrs[0, 16 * c:16 * c + 16] = np.asarray(r["nrets"])
    return (y_prompt, y_sample, ncp, nlp, nrp, ncs, nls, nrs)
```

```python
import contextlib
import numpy as np
import concourse.bass as bass
import concourse.mybir as mybir
from concourse.bass_utils import run_bass_kernel_spmd

F32 = mybir.dt.float32
BF16 = mybir.dt.bfloat16
AF = mybir.ActivationFunctionType
ALU = mybir.AluOpType

D = 2048
TP = 1032
TS = 128
T = TP + TS
NH = 8
CH = 116
PT = [(CH * i, CH) for i in range(8)] + [(928, 104)]
TILES = PT + [(TP, TS)]
NTP = [(0, 344), (344, 344), (688, 344)]
NT = NTP + [(TP, TS)]
GAM = [1.0 - 2.0 ** (-5 - h) for h in range(NH)]
EPS = 1e-6
ENGS = ("pe", "act", "dve", "pool", "sp")


class Op:
    __slots__ = ("eng", "fn", "reads", "writes", "is_dma", "deps", "token", "need_inc", "idx", "tag",
                 "prev_same_sem")

    def __init__(self, eng, fn, reads, writes, is_dma, tag=None):
        self.eng = eng
        self.fn = fn
        self.reads = tuple(reads)
        self.writes = tuple(writes)
        self.is_dma = is_dma
        self.deps = set()
        self.token = None
        self.need_inc = False
        self.tag = tag
        self.prev_same_sem = None


class Prog:
    def __init__(self, nc, n_dma_sems=6):
        self.nc = nc
        self.ops = []
        self.n_dma_sems = n_dma_sems
        self.last_writer = {}
        self.readers = {}
        self.last_barrier = None
        self.last_on_eng = {}

    def op(self, eng, fn, reads=(), writes=(), tag=None):
        o = Op(eng, fn, reads, writes, False, tag)
        self._add(o)
        return o

    def dma(self, queue, out, in_, reads=(), writes=(), tag=None):
        n = out.shape[0]
        if n == in_.shape[0] and n > 16 and n % 16 != 0:
            n16 = n - n % 16
            o1 = self._dma1(queue, out[0:n16], in_[0:n16], reads, writes, tag, None)
            self._dma1(queue, out[n16:n], in_[n16:n], reads, writes, tag, o1)
            return o1
        return self._dma1(queue, out, in_, reads, writes, tag, None)

    def _dma1(self, queue, out, in_, reads, writes, tag, co):
        def fn(e, out=out, in_=in_):
            return e.dma_start(out=out, in_=in_)
        o = Op(queue, fn, reads, writes, True, tag)
        self._add(o, co)
        return o

    def barrier(self, fn, eng="dve"):
        o = Op(eng, fn, (), (), False, "barrier")
        o.idx = len(self.ops)
        start = self.last_barrier.idx if self.last_barrier is not None else 0
        for p in self.ops[start:]:
            if p.is_dma:
                o.deps.add(p.idx)
        for e, p in self.last_on_eng.items():
            o.deps.add(p.idx)
        self.ops.append(o)
        self.last_on_eng[eng] = o
        self.last_barrier = o
        self.last_writer.clear()
        self.readers.clear()
        return o

    def _add(self, o, co=None):
        o.idx = len(self.ops)
        lw, rd = self.last_writer, self.readers
        for k in o.reads:
            if k in lw:
                o.deps.update(lw[k])
        if co is not None:
            o.deps.update(d for d in co.deps)
        else:
            for k in o.writes:
                if k in lw:
                    o.deps.update(lw[k])
                if k in rd:
                    o.deps.update(rd[k])
        if self.last_barrier is not None:
            o.deps.add(self.last_barrier.idx)
        for k in o.writes:
            if co is not None:
                cur = lw.get(k, [])
                lw[k] = (cur if co.idx in cur else [co.idx]) + [o.idx]
            else:
                lw[k] = [o.idx]
                rd[k] = []
        for k in o.reads:
            lst = rd.setdefault(k, [])
            if not o.is_dma:
                lst[:] = [q for q in lst if self.ops[q].is_dma or self.ops[q].eng != o.eng]
            lst.append(o.idx)
        self.ops.append(o)
        self.last_on_eng[o.eng] = o

    @staticmethod
    def _skip(p, o):
        if p.is_dma:
            return False
        if p.eng != o.eng:
            return False
        if p.eng == "pe":
            return True
        if o.is_dma:
            return False
        if p.tag == "barrier":
            return False
        return not (set(p.writes) & set(o.reads))

    def emit(self):
        nc = self.nc
        ops = self.ops
        for o in ops:
            for d in o.deps:
                p = ops[d]
                if p.is_dma or self._skip(p, o):
                    continue
                p.need_inc = True
        with contextlib.ExitStack() as st:
            esem = {e: st.enter_context(nc.semaphore(f"s_{e}")) for e in ENGS[:4]}
            dsems = {}
            for q in ("sp", "act", "pool"):
                dsems[q] = [st.enter_context(nc.semaphore(f"d_{q}{j}")) for j in range(self.n_dma_sems)]
            tick = {e: 0 for e in ENGS}
            dcount = {q: 0 for q in dsems}
            dval = {q: [0] * self.n_dma_sems for q in dsems}
            prev_on_sem = {}
            for o in ops:
                if o.is_dma:
                    q = o.eng
                    j = dcount[q] % self.n_dma_sems
                    dcount[q] += 1
                    dval[q][j] += 16
                    key = ("d", q, j)
                    o.token = (key, dval[q][j])
                    o.prev_same_sem = prev_on_sem.get(key)
                    prev_on_sem[key] = o
                elif o.need_inc:
                    tick[o.eng] += 1
                    o.token = (("e", o.eng), tick[o.eng])

            def semof(key):
                return esem[key[1]] if key[0] == "e" else dsems[key[1]][key[2]]

            waited = {e: {} for e in ENGS}
            per_eng = {e: [] for e in ENGS}
            for o in ops:
                need = {}
                for d in o.deps:
                    p = ops[d]
                    if p.token is None or self._skip(p, o):
                        continue
                    k, v = p.token
                    if need.get(k, 0) < v:
                        need[k] = v
                if o.is_dma and o.prev_same_sem is not None:
                    k, v = o.prev_same_sem.token
                    if need.get(k, 0) < v:
                        need[k] = v
                w = waited[o.eng]
                waits = []
                for k, v in need.items():
                    if w.get(k, 0) < v:
                        w[k] = v
                        waits.append((k, v))
                per_eng[o.eng].append((o, waits))
            final_waits = []
            for q in dsems:
                for j in range(self.n_dma_sems):
                    if dval[q][j] > 0:
                        final_waits.append((("d", q, j), dval[q][j]))
            for e in ENGS[:4]:
                if tick[e] > 0:
                    final_waits.append((("e", e), tick[e]))
            self.stats = {e: len(per_eng[e]) for e in ENGS}
            self.stats["waits"] = sum(len(w) for e in ENGS for _, w in per_eng[e])
            self.stats["ticks"] = dict(tick)

            def run(ename, e):
                for o, waits in per_eng[ename]:
                    for k, v in waits:
                        e.wait_ge(semof(k), v)
                    ins = o.fn(e)
                    if o.is_dma:
                        ins.then_inc(semof(o.token[0]), 16)
                    elif o.token is not None:
                        ins.then_inc(semof(o.token[0]), 1)
                if ename == "sp":
                    for k, v in final_waits:
                        e.wait_ge(semof(k), v)

            with nc.Block() as block:
                @block.sync
                def _(e):
                    run("sp", e)

                @block.scalar
                def _(e):
                    run("act", e)

                @block.vector
                def _(e):
                    run("dve", e)

                @block.gpsimd
                def _(e):
                    run("pool", e)

                @block.tensor
                def _(e):
                    run("pe", e)


def tt_over(t0, n):
    return [i for i, (a, b) in enumerate(TILES) if a < t0 + n and t0 < a + b]


def nt_over(t0, n):
    return [i for i, (a, b) in enumerate(NT) if a < t0 + n and t0 < a + b]


def build_nc(debug=False):
    build_nc.marks = []
    nc = bass.Bass("TRN2", target_bir_lowering=False)

    def din(name, shape):
        return nc.dram_tensor(name, list(shape), F32, kind="ExternalInput").ap()

    def dout(name, shape):
        return nc.dram_tensor(name, list(shape), F32, kind="ExternalOutput").ap()

    xall = din("xall", [T, D])
    xpre = din("xpre", [TP, D])
    sconv = din("sconv", [48, 1024])
    slru = din("slru", [16, 1024])
    sret = din("sret", [16, NH, 128, 128])
    flags_d = din("flags", [128, 8])
    tab = din("tab", [4, T, 1024])
    tabpre = din("tabpre", [2, TP, 1024])
    masks_d = din("masks", [2, 128, 128])
    selm_d = din("selm", [128, 16])
    prm = din("prm", [8, 1024])
    gn3 = din("gn3", [3, D])
    rng = din("rng", [1024])
    w_in = din("w_in", [D, 10240])
    lwa = din("lwa", [16, 64, 64])
    lwx = din("lwx", [16, 64, 64])
    p_a = din("p_a", [1024, D])
    p_b = din("p_b", [1024, D])
    w_out = din("w_out", [D, D])
    w_up = din("w_up", [D, 8192])
    w_down = din("w_down", [8192, D])
    y_d = dout("y", [T, D])
    nconv_d = dout("nconv", [51, 1024])
    nlru_d = dout("nlru", [17, 1024])
    nretp_d = dout("nretp", [NH, 128, 128])
    nrets_d = dout("nrets", [16, NH, 128, 128])
    if debug:
        dbg_hT = nc.dram_tensor("dbg_hT", [128, 16, T], BF16, kind="ExternalOutput").ap()
        dbg_yaT = nc.dram_tensor("dbg_yaT", [128, 8, T], BF16, kind="ExternalOutput").ap()
        dbg_ybT = nc.dram_tensor("dbg_ybT", [128, 8, T], BF16, kind="ExternalOutput").ap()
        dbg_mT = nc.dram_tensor("dbg_mT", [128, 16, T], BF16, kind="ExternalOutput").ap()
        dbg_x1 = nc.dram_tensor("dbg_x1", [128, 10, D], F32, kind="ExternalOutput").ap()
        dbg_x2 = nc.dram_tensor("dbg_x2", [128, 10, D], F32, kind="ExternalOutput").ap()

    st = contextlib.ExitStack()
    with st:
        def sb(name, shape, dt):
            return st.enter_context(nc.sbuf_tensor("sb_" + name, list(shape), dt))

        identb = sb("identb", [128, 128], BF16)
        identf = sb("identf", [128, 128], F32)
        masks = sb("masks", [128, 2, 128], F32)
        selm = sb("selm", [128, 16], F32)
        flags = sb("flags", [128, 8], F32)
        prm_fm = sb("prm_fm", [128, 8, 8], F32)
        c8 = sb("c8", [128, 8], F32)
        c16 = sb("c16", [128, 8], F32)
        wabd = sb("wabd", [128, 8, 128], BF16)
        wxbd = sb("wxbd", [128, 8, 128], BF16)
        sc_fm = sb("sc_fm", [128, 8, 48], F32)
        h0_fm = sb("h0_fm", [128, 8, 16], F32)
        hl_fm = sb("hl_fm", [128, 8, 17], F32)
        xtail = sb("xtail", [128, 8, 3], F32)
        hin = sb("hin", [128, 8], F32)
        spre = sb("spre", [128, NH, 128], F32)
        sfin = sb("sfin", [128, NH, 128], F32)
        ss = sb("ss", [128, 16], F32)
        rs = sb("rs", [128, 16], F32)
        gbc = sb("gbc", [128, D], F32)
        gnbc = sb("gnbc", [128, 1024], F32)
        bnst = sb("bnst", [128, 2, 6], F32)
        mv = sb("mv", [128, 2, 2], F32)
        rsd = sb("rsd", [128, 2], F32)
        nbias = sb("nbias", [128, 2], F32)
        tmp16 = sb("tmp16", [128, 16], F32)
        tmp16b = sb("tmp16b", [128, 16], F32)
        tmp16s = [tmp16b, tmp16]
        WBW = 16 * 512
        wbs = [sb(f"wb{k}", [128, WBW], BF16) for k in range(2)]
        ARW = 35700
        arena = sb("arena", [128, ARW], F32)
        banks = [st.enter_context(nc.psum_tensor(f"bank{i}", [128, 512], F32)) for i in range(8)]

        wl_log = []
        wl_state = {"dry": True}

        def program():
            P = Prog(nc)
            bank_ctr = [0]

            def nb(lo=0, hi=6):
                i = lo + bank_ctr[0] % (hi - lo)
                bank_ctr[0] += 1
                return i

            def BK(i):
                return ("ps", i)

            class Arena:
                def __init__(self):
                    self.off = 0

                def at(self, off):
                    self.off = off

                def f32(self, *shape):
                    n = int(np.prod(shape))
                    v = arena[:, self.off:self.off + n]
                    self.off += n
                    assert self.off <= ARW, self.off
                    if len(shape) == 2:
                        return v.rearrange("p (a b) -> p a b", a=shape[0])
                    if len(shape) == 3:
                        return v.rearrange("p (a b c) -> p a b c", a=shape[0], b=shape[1])
                    return v

                def bf(self, *shape):
                    n = int(np.prod(shape))
                    w = (n + 1) // 2
                    v = arena[:, self.off:self.off + w].bitcast(BF16)
                    self.off += w
                    assert self.off <= ARW, self.off
                    v = v[:, 0:n]
                    if len(shape) == 2:
                        return v.rearrange("p (a b) -> p a b", a=shape[0])
                    if len(shape) == 3:
                        return v.rearrange("p (a b c) -> p a b c", a=shape[0], b=shape[1])
                    return v

            AR = Arena()

            def do_barrier():
                P.barrier(lambda e: e.memset(tmp16[:, 0:1], 0.0))
                build_nc.marks.append({e: sum(1 for o in P.ops if o.eng == e) for e in ENGS})

            AR.at(ARW - 3 * 1024)
            prm_tm = AR.f32(1024)
            sc_tm = AR.f32(1024)
            h0_tm = AR.f32(1024)
            P.op("pool", lambda e: e.memset(identf[:], 1.0), writes=["identf"])
            P.op("pool", lambda e: e.affine_select(out=identf[:], in_=identf[:], pattern=[[-1, 128]],
                                                    compare_op=ALU.is_equal, fill=0.0, base=0, channel_multiplier=1),
                 reads=["identf"], writes=["identf"])
            P.op("dve", lambda e: e.tensor_copy(out=identb[:], in_=identf[:]), reads=["identf"], writes=["identb"])
            P.dma("sp", masks[:], masks_d.rearrange("a p n -> p a n"), writes=["masks"])
            P.dma("sp", selm[:], selm_d, writes=["selm"])
            P.dma("sp", flags[:], flags_d, writes=["flags"])
            P.dma("sp", prm_tm[0:8, :], prm, writes=["prm_tm"])
            P.dma("sp", sc_tm[0:48, :], sconv, writes=["sc_tm"])
            P.dma("sp", h0_tm[0:16, :], slru, writes=["h0_tm"])
            P.dma("sp", gnbc[:], rng.partition_broadcast(128), writes=["gnbc"])
            P.op("pool", lambda e: e.memset(wabd[:], 0.0), writes=["wabd"])
            P.op("pool", lambda e: e.memset(wxbd[:], 0.0), writes=["wxbd"])
            for (wsrc, wdst, key) in ((lwa, wabd, "wabd"), (lwx, wxbd, "wxbd")):
                v = wsrc.rearrange("(j two) c d -> two c j d", two=2)
                P.dma("pool", wdst[0:64, :, 0:64], v[0], reads=[key], writes=[key])
                P.dma("pool", wdst[64:128, :, 64:128], v[1], reads=[key], writes=[key])
            for j in range(8):
                b = nb()
                P.op("pe", lambda e, j=j, b=b: e.transpose(banks[b][:, 0:8], prm_tm[0:8, 128 * j:128 * j + 128], identf[0:8, 0:8]),
                     reads=["prm_tm", "identf"], writes=[BK(b)])
                P.op("pe", lambda e, j=j, b=b: e.transpose(banks[b][:, 8:56], sc_tm[0:48, 128 * j:128 * j + 128], identf[0:48, 0:48]),
                     reads=["sc_tm", "identf"], writes=[BK(b)])
                P.op("pe", lambda e, j=j, b=b: e.transpose(banks[b][:, 56:72], h0_tm[0:16, 128 * j:128 * j + 128], identf[0:16, 0:16]),
                     reads=["h0_tm", "identf"], writes=[BK(b)])
                P.op("dve", lambda e, j=j, b=b: e.tensor_copy(out=prm_fm[:, j, :], in_=banks[b][:, 0:8]), reads=[BK(b)], writes=["prm_fm"])
                P.op("dve", lambda e, j=j, b=b: e.tensor_copy(out=sc_fm[:, j, :], in_=banks[b][:, 8:56]), reads=[BK(b)], writes=["sc_fm"])
                P.op("dve", lambda e, j=j, b=b: e.tensor_copy(out=h0_fm[:, j, :], in_=banks[b][:, 56:72]), reads=[BK(b)], writes=["h0_fm"])
            P.op("act", lambda e: e.activation(out=c8[:], in_=prm_fm[:, :, 7], func=AF.Sigmoid), reads=["prm_fm"], writes=["c8"])
            P.op("act", lambda e: e.activation(out=c8[:], in_=c8[:], func=AF.Ln), reads=["c8"], writes=["c8"])
            P.op("dve", lambda e: e.tensor_scalar_mul(out=c16[:], in0=c8[:], scalar1=16.0), reads=["c8"], writes=["c16"])
            P.op("dve", lambda e: e.tensor_scalar_mul(out=c8[:], in0=c8[:], scalar1=8.0), reads=["c8", "c16"], writes=["c8"])

            do_barrier()

            wslot = [0]
            issued = set()

            def _issue(c):
                if c in issued or c >= len(wl_log):
                    return
                issued.add(c)
                k = c % 2
                first = None
                for fn, src in wl_log[c]:
                    o = P._dma1("pool", fn(wbs[k]), src, (), [("wb", k)], None, first)
                    if first is None:
                        first = o

            def wload(parts, prefetch=True):
                c = wslot[0]
                wslot[0] += 1
                k = c % 2
                if wl_state["dry"]:
                    wl_log.append(parts)
                    return wbs[k], ("wb", k)
                _issue(c)
                if prefetch:
                    _issue(c + 1)
                return wbs[k], ("wb", k)

            def wprefetch(n=2):
                if wl_state["dry"]:
                    return
                for c in range(wslot[0], wslot[0] + n):
                    _issue(c)

            def wview(wb, nk, ncol, off=0):
                return wb[:, off:off + nk * ncol].rearrange("p (k n) -> p k n", k=nk)

            def wsrc(w, r0, nk, c0, ncol):
                return w[r0:r0 + 128 * nk, c0:c0 + ncol].rearrange("(k p) n -> p k n", p=128)

            def norm_T(src_tiles, tiles, grow, dstT, dkey, stage, hbs):
                P.dma("sp", gbc[:], gn3[grow].partition_broadcast(128), writes=["gbc"])
                P.op("dve", lambda e: e.memset(ss[:], 0.0), writes=["ss"])
                info = {}

                def N1(i):
                    t0, n = tiles[i]
                    src, skeys = src_tiles(i)
                    hb = hbs[i % len(hbs)]
                    hk = ("hb", stage, i % len(hbs))
                    info[i] = (src, skeys, hb, hk)
                    P.op("act", lambda e: e.activation(out=hb[0:n, :], in_=src, func=AF.Square, accum_out=ss[0:n, i:i + 1]),
                         reads=skeys + ["ss"], writes=[hk, ("ss", i)])
                    P.op("act", lambda e: e.activation(out=rs[0:n, i:i + 1], in_=ss[0:n, i:i + 1], func=AF.Sqrt, scale=1.0 / D, bias=EPS),
                         reads=[("ss", i)], writes=[("rs", i)])
                    P.op("dve", lambda e: e.reciprocal(out=rs[0:n, i:i + 1], in_=rs[0:n, i:i + 1]), reads=[("rs", i)], writes=[("rs", i)])
                    if dstT is not None:
                        P.op("dve", lambda e: e.scalar_tensor_tensor(out=hb[0:n, :], in0=src, scalar=rs[0:n, i:i + 1], in1=gbc[0:n, :], op0=ALU.mult, op1=ALU.mult),
                             reads=skeys + [("rs", i), "gbc", hk], writes=[hk])

                def N2(i):
                    t0, n = tiles[i]
                    src, skeys, hb, hk = info[i]
                    for half in range(2):
                        b = nb()
                        pv = banks[b][:].bitcast(BF16).rearrange("p (k n) -> p k n", k=8)
                        for kk in range(8):
                            kt = half * 8 + kk
                            P.op("pe", lambda e, pv=pv, kk=kk, kt=kt: e.transpose(pv[:, kk, 0:n], hb[0:n, kt * 128:(kt + 1) * 128], identb[0:n, 0:n]),
                                 reads=[hk, "identb"], writes=[BK(b)])
                        if half == 0:
                            P.op("act", lambda e, pv=pv, half=half: e.copy(out=dstT[:, half * 8:half * 8 + 8, t0:t0 + n], in_=pv[:, :, 0:n]),
                                 reads=[BK(b)], writes=[(dkey, i)])
                        else:
                            P.op("dve", lambda e, pv=pv, half=half: e.tensor_copy(out=dstT[:, half * 8:half * 8 + 8, t0:t0 + n], in_=pv[:, :, 0:n]),
                                 reads=[BK(b)], writes=[(dkey, i)])

                nt_ = len(tiles)
                if dstT is None:
                    for i in range(nt_):
                        N1(i)
                        yield i, info[i][2], info[i][3]
                    return
                N1(0)
                for i in range(nt_):
                    if i + 1 < nt_:
                        N1(i + 1)
                    N2(i)
                    yield i, info[i][2], info[i][3]

            def run_gen(g):
                for _ in g:
                    pass

            def fm_proj(wv, wkey, c0, nk, rhsT, rkey, ntiles, consumer, rkeys_fn=None):
                bl = []
                for (t0, n) in ntiles:
                    b = nb()
                    bl.append(b)
                for kt in range(nk):
                    for (t0, n), b in zip(ntiles, bl):
                        rk = [(rkey, i) for i in (rkeys_fn(t0, n) if rkeys_fn else tt_over(t0, n))]
                        P.op("pe", lambda e, b=b, kt=kt, t0=t0, n=n: e.matmul(
                            banks[b][:, 0:n], lhsT=wv[:, kt, c0:c0 + 128], rhs=rhsT[:, kt, t0:t0 + n],
                            start=(kt == 0), stop=(kt == nk - 1)), reads=[wkey] + rk, writes=[BK(b)])
                for ni, ((t0, n), b) in enumerate(zip(ntiles, bl)):
                    consumer(ni, t0, n, b)

            def tm_proj(wv, wkey, c0, ncol, nk, lhsT, lkeys, t0, n, b):
                for kt in range(nk):
                    P.op("pe", lambda e, kt=kt: e.matmul(
                        banks[b][0:n, 0:ncol], lhsT=lhsT[:, kt, t0:t0 + n], rhs=wv[:, kt, c0:c0 + ncol],
                        start=(kt == 0), stop=(kt == nk - 1)), reads=[wkey] + lkeys, writes=[BK(b)])

            def lru_tile(j, hT, hkey, ntl, Tn, has_s, L, wv, wkey, f1col, f0col):
                q = j % 2
                xa, xc, xcb, r_, i_, a_ = (L[q][k] for k in ("xa", "xc", "xcb", "r", "i", "a"))
                xas = L[q].get("xas")
                ggb = L[q].get("gg")
                t16 = tmp16s[q]
                pre = "m" if has_s else "q"

                def K(nm, ni=None):
                    return ("L", pre, q, nm, ni) if ni is not None else ("L", pre, q, nm)
                allnt = list(range(len(ntl)))
                xar = [K("xa", ni) for ni in allnt] + [K("xah")] + ([K("xash")] if has_s else [])

                def xa_cons(ni, t0, n, b):
                    if t0 < TP:
                        P.op("dve", lambda e: e.tensor_copy(out=xa[:, 3 + t0:3 + t0 + n], in_=banks[b][:, 0:n]),
                             reads=[BK(b)], writes=[K("xa", ni)])
                    else:
                        P.op("dve", lambda e: e.tensor_copy(out=xas[:, :, 3:11], in_=banks[b][:, 0:128].rearrange("p (s t) -> p s t", t=8)),
                             reads=[BK(b)], writes=[K("xa", ni)])
                def LAp():
                    fm_proj(wv, wkey, 0, 16, hT, hkey, ntl, xa_cons)

                def LA():
                    if not has_s:
                        P.op("pool", lambda e: e.tensor_copy(out=xtail[:, j, :], in_=xa[:, TP:TP + 3]), reads=xar, writes=["xtail"])
                    if has_s:
                        def ga_cons(ni, t0, n, b):
                            P.op("act", lambda e: e.activation(out=ggb[:, t0:t0 + n], in_=banks[b][:, 0:n], func=AF.Gelu_apprx_tanh),
                                 reads=[BK(b)], writes=[K("gg")])
                        fm_proj(wv, wkey, 128, 16, hT, hkey, ntl, ga_cons)
                        P.op("dve", lambda e: e.tensor_scalar_mul(out=xa[:, 0:3], in0=xtail[:, j, :], scalar1=flags[:, 0:1]),
                             reads=["xtail", "flags"], writes=[K("xah")])
                        P.op("dve", lambda e: e.tensor_copy(out=xas[:, :, 0:3], in_=sc_fm[:, j, :].rearrange("p (s t) -> p s t", t=3)),
                             reads=["sc_fm"], writes=[K("xash")])
                    else:
                        P.op("dve", lambda e: e.memset(xa[:, 0:3], 0.0), writes=[K("xah")])
                    if has_s:
                        P.op("pool", lambda e: e.tensor_copy(out=cv_fm[:, j, 0:48].rearrange("p (s t) -> p s t", t=3), in_=xas[:, :, 8:11]), reads=xar, writes=["cv_fm"])
                        P.op("pool", lambda e: e.tensor_copy(out=cv_fm[:, j, 48:51], in_=xa[:, TP:TP + 3]), reads=xar, writes=["cv_fm"])
                    views = [(xc[:, 0:TP], lambda k: xa[:, k:k + TP])]
                    if has_s:
                        views.append((xc[:, TP:T].rearrange("p (s t) -> p s t", t=8), lambda k: xas[:, :, k:k + 8]))
                    xbv = [xcb[:, 0:TP]] + ([xcb[:, TP:T].rearrange("p (s t) -> p s t", t=8)] if has_s else [])
                    for vi, (ov, iv) in enumerate(views):
                        P.op("dve", lambda e, ov=ov, iv=iv: e.tensor_scalar(out=ov, in0=iv(0), scalar1=prm_fm[:, j, 0:1], scalar2=prm_fm[:, j, 4:5],
                                                                            op0=ALU.mult, op1=ALU.add),
                             reads=xar + ["prm_fm"], writes=[K("xc")])
                        for k in range(1, 3):
                            P.op("dve", lambda e, ov=ov, iv=iv, k=k: e.scalar_tensor_tensor(out=ov, in0=iv(k), scalar=prm_fm[:, j, k:k + 1], in1=ov,
                                                                                             op0=ALU.mult, op1=ALU.add),
                                 reads=xar + ["prm_fm", K("xc")], writes=[K("xc")])
                        P.op("dve", lambda e, ov=ov, iv=iv, xb=xbv[vi]: e.scalar_tensor_tensor(out=xb, in0=iv(3), scalar=prm_fm[:, j, 3:4], in1=ov,
                                                                                            op0=ALU.mult, op1=ALU.add),
                             reads=xar + ["prm_fm", K("xc")], writes=[K("xcb")])
                        P.op("dve", lambda e, ov=ov, iv=iv: e.scalar_tensor_tensor(out=ov, in0=iv(3), scalar=prm_fm[:, j, 3:4], in1=ov,
                                                                                    op0=ALU.mult, op1=ALU.add),
                             reads=xar + ["prm_fm", K("xc"), K("xcb")], writes=[K("xc")])

                def LB():
                    for ni, (t0, n) in enumerate(ntl):
                        br, bi = nb(), nb()
                        P.op("pe", lambda e, br=br, t0=t0, n=n: e.matmul(banks[br][:, 0:n], lhsT=wabd[:, j, :], rhs=xcb[:, t0:t0 + n], start=True, stop=True),
                             reads=["wabd", K("xcb")], writes=[BK(br)])
                        P.op("pe", lambda e, bi=bi, t0=t0, n=n: e.matmul(banks[bi][:, 0:n], lhsT=wxbd[:, j, :], rhs=xcb[:, t0:t0 + n], start=True, stop=True),
                             reads=["wxbd", K("xcb")], writes=[BK(bi)])
                        P.op("act", lambda e, br=br, t0=t0, n=n: e.activation(out=r_[:, t0:t0 + n], in_=banks[br][:, 0:n], func=AF.Sigmoid, bias=prm_fm[:, j, 5:6]),
                             reads=[BK(br), "prm_fm"], writes=[K("r")])
                        P.op("act", lambda e, bi=bi, t0=t0, n=n: e.activation(out=i_[:, t0:t0 + n], in_=banks[bi][:, 0:n], func=AF.Sigmoid, bias=prm_fm[:, j, 6:7]),
                             reads=[BK(bi), "prm_fm"], writes=[K("i")])
                    P.op("pool", lambda e: e.tensor_tensor(out=i_[:, 0:Tn], in0=i_[:, 0:Tn], in1=xc[:, 0:Tn], op=ALU.mult), reads=[K("i"), K("xc")], writes=[K("i")])
                    P.op("act", lambda e: e.activation(out=a_[:, 0:Tn], in_=r_[:, 0:Tn], func=AF.Exp, scale=c8[:, j:j + 1]), reads=[K("r"), "c8"], writes=[K("a")])
                    P.op("act", lambda e: e.activation(out=r_[:, 0:Tn], in_=r_[:, 0:Tn], func=AF.Exp, scale=c16[:, j:j + 1]), reads=[K("r"), K("a"), "c16"], writes=[K("r")])
                    P.op("act", lambda e: e.activation(out=r_[:, 0:Tn], in_=r_[:, 0:Tn], func=AF.Relu, scale=-1.0, bias=1.0), reads=[K("r")], writes=[K("r")])
                    P.op("act", lambda e: e.activation(out=r_[:, 0:Tn], in_=r_[:, 0:Tn], func=AF.Sqrt), reads=[K("r")], writes=[K("r")])

                def LC():
                    P.op("dve", lambda e: e.tensor_scalar(out=r_[:, 0:1], in0=r_[:, 0:1], scalar1=flags[:, f0col:f0col + 1], scalar2=flags[:, f1col:f1col + 1],
                                                          op0=ALU.mult, op1=ALU.add), reads=[K("r"), "flags"], writes=[K("r")])
                    P.op("dve", lambda e: e.tensor_tensor(out=i_[:, 0:Tn], in0=r_[:, 0:Tn], in1=i_[:, 0:Tn], op=ALU.mult), reads=[K("r"), K("i")], writes=[K("i")])
                    if has_s:
                        a0 = a_[:, TP:T].rearrange("p (s t) -> p s t", t=8)[:, :, 0]
                        b0 = i_[:, TP:T].rearrange("p (s t) -> p s t", t=8)[:, :, 0]
                        P.op("dve", lambda e: e.tensor_tensor(out=t16[:], in0=a0, in1=h0_fm[:, j, :], op=ALU.mult), reads=[K("a"), "h0_fm"], writes=[("t16", q)])
                        P.op("dve", lambda e: e.tensor_tensor(out=b0, in0=b0, in1=t16[:], op=ALU.add), reads=[K("i"), ("t16", q)], writes=[K("i")])
                        P.op("dve", lambda e: e.memset(a0, 0.0), reads=[K("r"), ("t16", q)], writes=[K("a")])
                        P.op("dve", lambda e: e.tensor_tensor_scan(out=r_[:, 0:Tn], data0=a_[:, 0:Tn], data1=i_[:, 0:Tn], initial=hin[:, j:j + 1],
                                                                   op0=ALU.mult, op1=ALU.add), reads=[K("a"), K("i"), "hin"], writes=[K("r")])
                        P.op("pool", lambda e: e.tensor_copy(out=hl_fm[:, j, 0:16], in_=r_[:, TP:T].rearrange("p (s t) -> p s t", t=8)[:, :, 7]),
                             reads=[K("r")], writes=["hl_fm"])
                        P.op("pool", lambda e: e.tensor_copy(out=hl_fm[:, j, 16:17], in_=r_[:, TP - 1:TP]), reads=[K("r")], writes=["hl_fm"])
                        for ni, (t0, n) in enumerate(ntl):
                            eng = "dve" if ni % 2 == 0 else "pool"
                            P.op(eng, lambda e, t0=t0, n=n: e.tensor_tensor(out=yaT[:, j, t0:t0 + n], in0=r_[:, t0:t0 + n], in1=ggb[:, t0:t0 + n], op=ALU.mult),
                                 reads=[K("gg"), K("r")], writes=[("yaT", ni)])
                    else:
                        P.op("dve", lambda e: e.tensor_tensor_scan(out=r_[:, 0:Tn], data0=a_[:, 0:Tn], data1=i_[:, 0:Tn], initial=0.0,
                                                                   op0=ALU.mult, op1=ALU.add), reads=[K("a"), K("i")], writes=[K("r")])
                        P.op("dve", lambda e: e.tensor_scalar_mul(out=hin[:, j:j + 1], in0=r_[:, TP - 1:TP], scalar1=flags[:, 0:1]),
                             reads=[K("r"), "flags"], writes=["hin"])
                return LA, LB, LC, LAp

            def lru_bufs(Tn, has_s):
                Ls = []
                for q in range(2):
                    L = {}
                    L["xa"] = AR.f32(TP + 3 + 1)
                    if has_s:
                        L["xas"] = AR.f32(16, 11)
                        L["gg"] = AR.bf(Tn)
                    for k in ("xc", "r", "i", "a"):
                        L[k] = AR.f32(Tn)
                    L["xcb"] = AR.bf(Tn)
                    Ls.append(L)
                return Ls

            def rope(b, n, ct, st_, tkeys, out, okey, tmp, tk):
                x = banks[b][0:n, 0:256].rearrange("p (h two d) -> p h two d", h=2, two=2)
                x1, x2 = x[:, :, 0, :], x[:, :, 1, :]
                o = out.rearrange("p (h two d) -> p h two d", h=2, two=2)
                t1, t2, t3, t4 = (tmp[0:n, q, :].rearrange("p (h d) -> p h d", h=2) for q in range(4))
                P.op("dve", lambda e: e.tensor_tensor(out=t1, in0=x1, in1=ct, op=ALU.mult), reads=[BK(b)] + tkeys, writes=[(tk, 0)])
                P.op("dve", lambda e: e.tensor_tensor(out=t2, in0=x2, in1=st_, op=ALU.mult), reads=[BK(b)] + tkeys, writes=[(tk, 1)])
                P.op("dve", lambda e: e.tensor_tensor(out=t3, in0=x2, in1=ct, op=ALU.mult), reads=[BK(b)] + tkeys, writes=[(tk, 2)])
                P.op("dve", lambda e: e.tensor_tensor(out=t4, in0=x1, in1=st_, op=ALU.mult), reads=[BK(b)] + tkeys, writes=[(tk, 3)])
                P.op("pool", lambda e: e.tensor_tensor(out=o[:, :, 0, :], in0=t1, in1=t2, op=ALU.subtract), reads=[(tk, 0), (tk, 1)], writes=[okey])
                P.op("pool", lambda e: e.tensor_tensor(out=o[:, :, 1, :], in0=t3, in1=t4, op=ALU.add), reads=[(tk, 2), (tk, 3)], writes=[okey])

            AR.at(0)
            hTq = AR.bf(16, TP)
            xts = [AR.f32(D), AR.f32(D)]
            hbs = [AR.bf(D), AR.bf(D)]
            mark = AR.off

            def pre_src(i):
                t0, n = PT[i]
                xt = xts[i % 2]
                P.dma("sp", xt[0:n, :], xpre[t0:t0 + n, :], writes=[("xt", i % 2)])
                return xt[0:n, :], [("xt", i % 2)]
            run_gen(norm_T(pre_src, PT, 0, hTq, "hTq", "q", hbs))
            L = lru_bufs(TP, False)
            stages = {}

            def pre_make(j):
                wb, wkey = wload([(lambda w: wview(w, 16, 128), wsrc(w_in, 0, 16, 128 * j, 128))])
                stages[j] = lru_tile(j, hTq, "hTq", NTP, TP, False, L, wview(wb, 16, 128), wkey, 3, 4)
            for step in range(-2, 9):
                if 0 <= step - 1 < 8:
                    stages[step - 1][2]()
                if 0 <= step < 8:
                    stages[step][1]()
                if 0 <= step + 1 < 8:
                    stages[step + 1][0]()
                if 0 <= step + 2 < 8:
                    pre_make(step + 2)
                    stages[step + 2][3]()
            khat = AR.bf(9, 256)
            vtm = [AR.bf(256), AR.bf(256)]
            rtmp = [AR.f32(4, 128), AR.f32(4, 128)]
            tbl = [AR.f32(2, 256), AR.f32(2, 256)]
            for hg in range(4):
                wb, wkey = wload([(lambda w: wview(w, 16, 512)[:, :, 0:256], wsrc(w_in, 0, 16, 3072 + 256 * hg, 256)),
                                  (lambda w: wview(w, 16, 512)[:, :, 256:512], wsrc(w_in, 0, 16, 4096 + 256 * hg, 256))])
                wv = wview(wb, 16, 512)
                sb_ = 6 + hg % 2
                pend = [None]
                for i, (t0, n) in enumerate(PT):
                    tb = tbl[i % 2]
                    P.dma("sp", tb[0:n, :, :], tabpre[:, t0:t0 + n, 256 * hg:256 * hg + 256].rearrange("a p n -> p a n"), writes=[("tblq", i % 2)])
                    b = nb()
                    tm_proj(wv, wkey, 0, 256, 16, hTq, [("hTq", i)], t0, n, b)
                    rope(b, n, tb[0:n, 0, :].rearrange("p (h two d) -> p h two d", h=2, two=2)[:, :, 0, :],
                         tb[0:n, 1, :].rearrange("p (h two d) -> p h two d", h=2, two=2)[:, :, 0, :],
                         [("tblq", i % 2)], khat[0:n, i, :], ("khat", i), rtmp[i % 2], ("rtq", i % 2))
                    b2 = nb()
                    tm_proj(wv, wkey, 256, 256, 16, hTq, [("hTq", i)], t0, n, b2)
                    vt = vtm[i % 2]
                    P.op("act", lambda e, vt=vt, n=n, b2=b2: e.copy(out=vt[0:n, :], in_=banks[b2][0:n, 0:256]), reads=[BK(b2)], writes=[("vtq", i % 2)])
                    def s_acc(i=i, n=n, vt=vt):
                        for hh in range(2):
                            P.op("pe", lambda e, hh=hh: e.matmul(
                                banks[6 + hh][:, 0:128], lhsT=khat[0:n, i, hh * 128:hh * 128 + 128], rhs=vt[0:n, hh * 128:hh * 128 + 128],
                                start=(i == 0), stop=(i == 8)), reads=[("khat", i), ("vtq", i % 2)], writes=[BK(6 + hh)])
                    if pend[0] is not None:
                        pend[0]()
                    pend[0] = s_acc
                pend[0]()
                pend[0] = None
                for hh in range(2):
                    P.op("dve", lambda e, hg=hg, hh=hh: e.tensor_scalar_mul(out=spre[:, 2 * hg + hh, :], in0=banks[6 + hh][:, 0:128], scalar1=flags[:, 0:1]),
                         reads=[BK(6 + hh), "flags"], writes=[("spre", hg, hh)])
            do_barrier()

            AR.at(0)
            hT = AR.bf(16, T)
            yaT = AR.bf(8, T)
            ybT = AR.bf(8, T)
            XOFF = AR.off
            cv_fm = AR.f32(8, 51)
            cv_tm = AR.f32(1024)
            hl_tm = AR.f32(1024)
            mark = AR.off
            xts = [AR.f32(D), AR.f32(D)]
            hbs = [AR.bf(D), AR.bf(D)]

            def main_src(i):
                t0, n = TILES[i]
                xt = xts[i % 2]
                P.dma("sp", xt[0:n, :], xall[t0:t0 + n, :], writes=[("xt", i % 2)])
                return xt[0:n, :], [("xt", i % 2)]
            run_gen(norm_T(main_src, TILES, 0, hT, "hT", "m", hbs))
            do_barrier()
            AR.at(mark)
            L = lru_bufs(T, True)
            stages = {}

            def main_make(j):
                wb, wkey = wload([(lambda w: wview(w, 16, 256)[:, :, 0:128], wsrc(w_in, 0, 16, 128 * j, 128)),
                                  (lambda w: wview(w, 16, 256)[:, :, 128:256], wsrc(w_in, 0, 16, 1024 + 128 * j, 128))])
                stages[j] = lru_tile(j, hT, "hT", NT, T, True, L, wview(wb, 16, 256), wkey, 1, 2)
            for step in range(-2, 9):
                if 0 <= step - 1 < 8:
                    stages[step - 1][2]()
                if 0 <= step < 8:
                    stages[step][1]()
                if 0 <= step + 1 < 8:
                    stages[step + 1][0]()
                if 0 <= step + 2 < 8:
                    main_make(step + 2)
                    stages[step + 2][3]()
            for half in range(2):
                b = nb()
                for jj in range(4):
                    j = half * 4 + jj
                    P.op("pe", lambda e, j=j, jj=jj, b=b: e.transpose(banks[b][0:17, jj * 128:jj * 128 + 128], hl_fm[:, j, :], identf[:, :]),
                         reads=["hl_fm", "identf"], writes=[BK(b)])
                P.op("act", lambda e, half=half, b=b: e.copy(out=hl_tm[0:17, half * 512:half * 512 + 512], in_=banks[b][0:17, :]), reads=[BK(b)], writes=["hl_tm"])
            P.dma("sp", nlru_d, hl_tm[0:17, :], reads=["hl_tm"])
            for half in range(2):
                b = nb()
                for jj in range(4):
                    j = half * 4 + jj
                    P.op("pe", lambda e, j=j, jj=jj, b=b: e.transpose(banks[b][0:51, jj * 128:jj * 128 + 128], cv_fm[:, j, :], identf[:, :]),
                         reads=["cv_fm", "identf"], writes=[BK(b)])
                P.op("act", lambda e, half=half, b=b: e.copy(out=cv_tm[0:51, half * 512:half * 512 + 512], in_=banks[b][0:51, :]), reads=[BK(b)], writes=["cv_tm"])
            P.dma("sp", nconv_d, cv_tm[0:51, :], reads=["cv_tm"])
            do_barrier()

            AR.at(XOFF)
            qT = AR.bf(2, T)
            kT = AR.bf(2, T)
            kTM = AR.bf(10, 256)
            vTM = AR.bf(10, 256)
            gsT = AR.bf(10, 256)
            qtmp = [AR.bf(256), AR.bf(256)]
            rtmp = [AR.f32(4, 128), AR.f32(4, 128)]
            tbl = [AR.f32(4, 256), AR.f32(4, 256)]
            gtmp = [AR.f32(256), AR.f32(256)]
            scm = [[AR.bf(128), AR.bf(128)], [AR.bf(128), AR.bf(128)]]
            onb = [AR.f32(128), AR.f32(128)]
            ybt = [[AR.bf(128), AR.bf(128)], [AR.bf(128), AR.bf(128)]]
            s32 = [AR.f32(128), AR.f32(128)]
            s16 = [[AR.bf(128), AR.bf(128)], [AR.bf(128), AR.bf(128)]]
            scm_s = [AR.bf(128), AR.bf(128)]
            ybt_s = [AR.bf(128), AR.bf(128)]
            kvs = [AR.f32(128), AR.f32(128)]
            s0f = AR.f32(16, 128)
            s0b = AR.bf(16, 128)
            sout = s0f
            qm = AR.bf(2176)
            km = AR.bf(16, 128)
            P.op("pool", lambda e: e.memset(qm[:], 0.0), writes=["qm"])

            for hg in range(4):
                wb1, wk1 = wload([(lambda w: wview(w, 16, 512)[:, :, 0:256], wsrc(w_in, 0, 16, 2048 + 256 * hg, 256)),
                                  (lambda w: wview(w, 16, 512)[:, :, 256:512], wsrc(w_in, 0, 16, 3072 + 256 * hg, 256))])
                wb2, wk2 = wload([(lambda w: wview(w, 16, 512)[:, :, 0:256], wsrc(w_in, 0, 16, 4096 + 256 * hg, 256)),
                                  (lambda w: wview(w, 16, 512)[:, :, 256:512], wsrc(w_in, 0, 16, 5120 + 256 * hg, 256))], prefetch=False)
                wv1, wv2 = wview(wb1, 16, 512), wview(wb2, 16, 512)
                for i, (t0, n) in enumerate(TILES):
                    tb = tbl[i % 2]
                    P.dma("sp", tb[0:n, :, :], tab[:, t0:t0 + n, 256 * hg:256 * hg + 256].rearrange("a p n -> p a n"), writes=[("tbl", i % 2)])

                    def tv(a):
                        return tb[0:n, a, :].rearrange("p (h two d) -> p h two d", h=2, two=2)[:, :, 0, :]
                    b = nb()
                    tm_proj(wv1, wk1, 0, 256, 16, hT, [("hT", i)], t0, n, b)
                    qt = qtmp[i % 2]
                    rope(b, n, tv(0), tv(1), [("tbl", i % 2)], qt[0:n, :], ("qt", i % 2), rtmp[i % 2], ("rt", i % 2))
                    b = nb()
                    tm_proj(wv1, wk1, 256, 256, 16, hT, [("hT", i)], t0, n, b)
                    rope(b, n, tv(2), tv(3), [("tbl", i % 2)], kTM[0:n, i, :], ("kTM", i), rtmp[i % 2], ("rt", i % 2))
                    b = nb()
                    tm_proj(wv2, wk2, 0, 256, 16, hT, [("hT", i)], t0, n, b)
                    P.op("act", lambda e, i=i, n=n, b=b: e.copy(out=vTM[0:n, i, :], in_=banks[b][0:n, 0:256]), reads=[BK(b)], writes=[("vTM", i)])
                    b = nb()
                    tm_proj(wv2, wk2, 256, 256, 16, hT, [("hT", i)], t0, n, b)
                    gt = gtmp[i % 2]
                    P.op("act", lambda e, gt=gt, n=n, b=b: e.activation(out=gt[0:n, :], in_=banks[b][0:n, 0:256], func=AF.Silu), reads=[BK(b)], writes=[("gt", i % 2)])
                    P.op("pool", lambda e, gt=gt, i=i, n=n, hg=hg: e.tensor_tensor(out=gsT[0:n, i, :], in0=gt[0:n, :], in1=gnbc[0:n, 256 * hg:256 * hg + 256], op=ALU.mult),
                         reads=[("gt", i % 2), "gnbc"], writes=[("gsT", i)])
                    b = nb()
                    pv = banks[b][:].bitcast(BF16).rearrange("p (k n) -> p k n", k=8)
                    for hh in range(2):
                        P.op("pe", lambda e, pv=pv, hh=hh, qt=qt, n=n: e.transpose(pv[:, hh, 0:n], qt[0:n, hh * 128:hh * 128 + 128], identb[0:n, 0:n]),
                             reads=[("qt", i % 2), "identb"], writes=[BK(b)])
                        P.op("pe", lambda e, pv=pv, hh=hh, i=i, n=n: e.transpose(pv[:, 2 + hh, 0:n], kTM[0:n, i, hh * 128:hh * 128 + 128], identb[0:n, 0:n]),
                             reads=[("kTM", i), "identb"], writes=[BK(b)])
                    P.op("act", lambda e, pv=pv, t0=t0, n=n: e.copy(out=qT[:, :, t0:t0 + n], in_=pv[:, 0:2, 0:n]), reads=[BK(b)], writes=[("qT", i)])
                    P.op("act", lambda e, pv=pv, t0=t0, n=n: e.copy(out=kT[:, :, t0:t0 + n], in_=pv[:, 2:4, 0:n]), reads=[BK(b)], writes=[("kT", i)])
                for hh in range(2):
                    h = 2 * hg + hh
                    P.op("pool", lambda e, h=h, hh=hh: e.tensor_copy(out=s32[hh][:, :], in_=spre[:, h, :]), reads=[("spre", hg, hh)], writes=[("s32", hh)])
                    P.op("act", lambda e, hh=hh: e.copy(out=s16[hh][0][:, :], in_=s32[hh][:, :]), reads=[("s32", hh)], writes=[("s16", hh, 0)])
                wprefetch(2)
                items = [(i, hh) for i in range(10) for hh in range(2)]
                ctx = {}

                def S1a(k):
                    i, hh = items[k]
                    h = 2 * hg + hh
                    hs = slice(hh * 128, hh * 128 + 128)
                    P.dma("sp", s0f[:, :, :], sret[:, h, :, :].rearrange("s d v -> d s v"), writes=["s0f"])
                    P.op("act", lambda e: e.copy(out=s0b[:, :, :], in_=s0f[:, :, :]), reads=["s0f"], writes=["s0b"])
                    P.op("pool", lambda e: e.tensor_copy(out=qm[:, 0:2176].rearrange("p (s x) -> p s x", x=136)[:, :, 0:8],
                                                          in_=qT[:, hh, TP:T].rearrange("p (s t) -> p s t", t=8)),
                         reads=[("qT", 9), "qm"], writes=["qm"])
                    P.op("pool", lambda e: e.tensor_tensor(out=km[:, :, :], in0=kTM[:, 9, hs].unsqueeze(1).broadcast_to([128, 16, 128]),
                                                            in1=selm[:, :].unsqueeze(2).broadcast_to([128, 16, 128]), op=ALU.mult),
                         reads=[("kTM", 9), "selm"], writes=["km"])

                def S1(k):
                    i, hh = items[k]
                    t0, n = TILES[i]
                    bs = nb()
                    P.op("pe", lambda e: e.matmul(banks[bs][0:n, 0:n], lhsT=kT[:, hh, t0:t0 + n], rhs=qT[:, hh, t0:t0 + n], start=True, stop=True),
                         reads=[("kT", i), ("qT", i)], writes=[BK(bs)])
                    sm = scm_s[hh]
                    P.op("dve", lambda e: e.tensor_tensor(out=sm[0:n, 0:n], in0=banks[bs][0:n, 0:n], in1=masks[0:n, 1, 0:n], op=ALU.mult),
                         reads=[BK(bs), "masks"], writes=[("scm_s", hh)])

                def S2(k):
                    i, hh = items[k]
                    t0, n = TILES[i]
                    h = 2 * hg + hh
                    hs = slice(hh * 128, hh * 128 + 128)
                    samp = (i == 9)
                    sm = scm_s[hh]
                    bo = nb()
                    P.op("pe", lambda e: e.matmul(banks[bo][0:n, 0:128], lhsT=sm[0:n, 0:n], rhs=vTM[0:n, i, hs], start=True, stop=False),
                         reads=[("scm_s", hh), ("vTM", i)], writes=[BK(bo)])
                    if not samp:
                        P.op("pe", lambda e: e.matmul(banks[bo][0:n, 0:128], lhsT=qT[:, hh, t0:t0 + n], rhs=s16[hh][1][:, :], start=False, stop=True),
                             reads=[("qT", i), ("s16", hh, 1)], writes=[BK(bo)])
                    else:
                        for s_ in range(16):
                            P.op("pe", lambda e, s_=s_: e.matmul(banks[bo][0:128, 0:128], lhsT=qm[:, 128 * s_:128 * s_ + 128], rhs=s0b[:, s_, :], start=False, stop=(s_ == 15)),
                                 reads=["qm", "s0b"], writes=[BK(bo)])
                    P.op("dve", lambda e: e.bn_stats(out=bnst[0:n, hh, :], in_=banks[bo][0:n, 0:128]), reads=[BK(bo)], writes=[("bnst", hh)])
                    P.op("dve", lambda e: e.bn_aggr(out=mv[0:n, hh, :], in_=bnst[0:n, hh, :]), reads=[("bnst", hh)], writes=[("mv", hh)])
                    P.op("act", lambda e: e.activation(out=rsd[0:n, hh:hh + 1], in_=mv[0:n, hh, 1:2], func=AF.Sqrt, scale=1.0, bias=EPS),
                         reads=[("mv", hh)], writes=[("rsd", hh)])
                    P.op("dve", lambda e: e.reciprocal(out=rsd[0:n, hh:hh + 1], in_=rsd[0:n, hh:hh + 1]), reads=[("rsd", hh)], writes=[("rsd", hh)])
                    on = onb[hh]
                    P.op("dve", lambda e: e.tensor_scalar(out=on[0:n, :], in0=banks[bo][0:n, 0:128], scalar1=mv[0:n, hh, 0:1], scalar2=rsd[0:n, hh:hh + 1],
                                                          op0=ALU.subtract, op1=ALU.mult),
                         reads=[BK(bo), ("mv", hh), ("rsd", hh)], writes=[("on", hh)])
                    yb = ybt_s[hh]
                    P.op("pool", lambda e: e.tensor_tensor(out=yb[0:n, :], in0=on[0:n, :], in1=gsT[0:n, i, hs], op=ALU.mult),
                         reads=[("on", hh), ("gsT", i)], writes=[("ybt_s", hh)])

                def S3(k):
                    i, hh = items[k]
                    t0, n = TILES[i]
                    h = 2 * hg + hh
                    hs = slice(hh * 128, hh * 128 + 128)
                    samp = (i == 9)
                    yb = ybt_s[hh]
                    bt = nb()
                    pv = banks[bt][:].bitcast(BF16)
                    P.op("pe", lambda e: e.transpose(pv[:, 0:n], yb[0:n, :], identb[0:n, 0:n]), reads=[("ybt_s", hh), "identb"], writes=[BK(bt)])
                    P.op("act", lambda e: e.copy(out=ybT[:, h, t0:t0 + n], in_=pv[:, 0:n]), reads=[BK(bt)], writes=[("ybT", h, i)])
                    if not samp:
                        bk = nb()
                        P.op("pe", lambda e: e.matmul(banks[bk][:, 0:128], lhsT=kTM[0:n, i, hs], rhs=vTM[0:n, i, hs], start=True, stop=True),
                             reads=[("kTM", i), ("vTM", i)], writes=[BK(bk)])
                        gl = float(GAM[h] ** n)
                        last = (i == 8)
                        P.op("act", lambda e: e.mul(out=kvs[hh][:, :], in_=banks[bk][:, 0:128], mul=gl), reads=[BK(bk)], writes=[("kvs", hh)])
                        dst = sfin[:, h, :] if last else s32[hh][:, :]
                        P.op("dve", lambda e: e.scalar_tensor_tensor(out=dst, in0=s32[hh][:, :], scalar=gl, in1=kvs[hh][:, :], op0=ALU.mult, op1=ALU.add),
                             reads=[("kvs", hh), ("s32", hh)], writes=[("sfin", h)] if last else [("s32", hh)])
                        if not last:
                            P.op("pool", lambda e: e.tensor_copy(out=s16[hh][1][:, :], in_=s32[hh][:, :]), reads=[("s32", hh)], writes=[("s16", hh, 1)])
                    else:
                        g8 = float(GAM[h] ** 8)
                        for q4 in range(4):
                            bk = nb()
                            for s4 in range(4):
                                s_ = 4 * q4 + s4
                                P.op("pe", lambda e, bk=bk, s_=s_, s4=s4: e.matmul(banks[bk][:, 128 * s4:128 * s4 + 128], lhsT=km[:, s_, :], rhs=vTM[:, 9, hs], start=True, stop=True),
                                     reads=["km", ("vTM", 9)], writes=[BK(bk)])
                            P.op("dve", lambda e, bk=bk, q4=q4: e.tensor_tensor(out=sout[:, 4 * q4:4 * q4 + 4, :], in0=banks[bk][:, :].rearrange("p (s v) -> p s v", s=4),
                                                                               in1=s0f[:, 4 * q4:4 * q4 + 4, :], op=ALU.add),
                                 reads=[BK(bk), "s0f"], writes=["s0f"])
                            P.op("act", lambda e, q4=q4: e.mul(out=sout[:, 4 * q4:4 * q4 + 4, :], in_=sout[:, 4 * q4:4 * q4 + 4, :], mul=g8),
                                 reads=["s0f"], writes=["s0f"])
                        P.dma("sp", nrets_d[:, h, :, :].rearrange("s d v -> d s v"), sout[:, :, :], reads=["s0f"])

                def R1(i):
                    t0, n = TILES[i]
                    for hh in range(2):
                        bs = nb()
                        P.op("pe", lambda e, bs=bs, hh=hh: e.matmul(banks[bs][0:n, 0:n], lhsT=kT[:, hh, t0:t0 + n], rhs=qT[:, hh, t0:t0 + n], start=True, stop=True),
                             reads=[("kT", i), ("qT", i)], writes=[BK(bs)])
                        sm = scm[hh][i % 2]
                        P.op("dve", lambda e, bs=bs, sm=sm: e.tensor_tensor(out=sm[0:n, 0:n], in0=banks[bs][0:n, 0:n], in1=masks[0:n, 0, 0:n], op=ALU.mult),
                             reads=[BK(bs), "masks"], writes=[("scm", hh, i % 2)])

                def R2(i):
                    t0, n = TILES[i]
                    for hh in range(2):
                        hs = slice(hh * 128, hh * 128 + 128)
                        sm = scm[hh][i % 2]
                        bo = nb()
                        P.op("pe", lambda e, bo=bo, sm=sm, hs=hs: e.matmul(banks[bo][0:n, 0:128], lhsT=sm[0:n, 0:n], rhs=vTM[0:n, i, hs], start=True, stop=False),
                             reads=[("scm", hh, i % 2), ("vTM", i)], writes=[BK(bo)])
                        P.op("pe", lambda e, bo=bo, hh=hh: e.matmul(banks[bo][0:n, 0:128], lhsT=qT[:, hh, t0:t0 + n], rhs=s16[hh][i % 2][:, :], start=False, stop=True),
                             reads=[("qT", i), ("s16", hh, i % 2)], writes=[BK(bo)])
                        P.op("dve", lambda e, bo=bo, hh=hh: e.bn_stats(out=bnst[0:n, hh, :], in_=banks[bo][0:n, 0:128]), reads=[BK(bo)], writes=[("bnst", hh)])
                        P.op("dve", lambda e, hh=hh: e.bn_aggr(out=mv[0:n, hh, :], in_=bnst[0:n, hh, :]), reads=[("bnst", hh)], writes=[("mv", hh)])
                        P.op("act", lambda e, hh=hh: e.activation(out=rsd[0:n, hh:hh + 1], in_=mv[0:n, hh, 1:2], func=AF.Sqrt, scale=1.0, bias=EPS),
                             reads=[("mv", hh)], writes=[("rsd", hh)])
                        P.op("dve", lambda e, hh=hh: e.reciprocal(out=rsd[0:n, hh:hh + 1], in_=rsd[0:n, hh:hh + 1]), reads=[("rsd", hh)], writes=[("rsd", hh)])
                        on = onb[hh]
                        P.op("dve", lambda e, hh=hh: e.scalar_tensor_tensor(out=nbias[0:n, hh:hh + 1], in0=mv[0:n, hh, 0:1], scalar=-1.0, in1=rsd[0:n, hh:hh + 1],
                                                                            op0=ALU.mult, op1=ALU.mult),
                             reads=[("mv", hh), ("rsd", hh)], writes=[("nb", hh)])
                        P.op("act", lambda e, bo=bo, hh=hh, on=on: e.activation(out=on[0:n, :], in_=banks[bo][0:n, 0:128], func=AF.Identity,
                                                                                scale=rsd[0:n, hh:hh + 1], bias=nbias[0:n, hh:hh + 1]),
                             reads=[BK(bo), ("rsd", hh), ("nb", hh)], writes=[("on", hh)])
                        yb = ybt[hh][i % 2]
                        P.op("pool", lambda e, on=on, yb=yb, hs=hs: e.tensor_tensor(out=yb[0:n, :], in0=on[0:n, :], in1=gsT[0:n, i, hs], op=ALU.mult),
                             reads=[("on", hh), ("gsT", i)], writes=[("ybt", hh, i % 2)])

                def RU(i):
                    t0, n = TILES[i]
                    for hh in range(2):
                        h = 2 * hg + hh
                        hs = slice(hh * 128, hh * 128 + 128)
                        bk = nb()
                        P.op("pe", lambda e, bk=bk, hs=hs: e.matmul(banks[bk][:, 0:128], lhsT=kTM[0:n, i, hs], rhs=vTM[0:n, i, hs], start=True, stop=True),
                             reads=[("kTM", i), ("vTM", i)], writes=[BK(bk)])
                        gl = float(GAM[h] ** n)
                        last = (i == 8)
                        P.op("act", lambda e, bk=bk, gl=gl, hh=hh: e.mul(out=kvs[hh][:, :], in_=banks[bk][:, 0:128], mul=gl), reads=[BK(bk)], writes=[("kvs", hh)])
                        dst = sfin[:, h, :] if last else s32[hh][:, :]
                        P.op("dve", lambda e, gl=gl, dst=dst, hh=hh: e.scalar_tensor_tensor(out=dst, in0=s32[hh][:, :], scalar=gl, in1=kvs[hh][:, :], op0=ALU.mult, op1=ALU.add),
                             reads=[("kvs", hh), ("s32", hh)], writes=[("sfin", h)] if last else [("s32", hh)])
                        if not last:
                            P.op("pool", lambda e, hh=hh: e.tensor_copy(out=s16[hh][(i + 1) % 2][:, :], in_=s32[hh][:, :]), reads=[("s32", hh)], writes=[("s16", hh, (i + 1) % 2)])

                def RT(i):
                    t0, n = TILES[i]
                    for hh in range(2):
                        h = 2 * hg + hh
                        yb = ybt[hh][i % 2]
                        bt = nb()
                        pv = banks[bt][:].bitcast(BF16)
                        P.op("pe", lambda e, pv=pv, yb=yb: e.transpose(pv[:, 0:n], yb[0:n, :], identb[0:n, 0:n]), reads=[("ybt", hh, i % 2), "identb"], writes=[BK(bt)])
                        P.op("act", lambda e, pv=pv, h=h: e.copy(out=ybT[:, h, t0:t0 + n], in_=pv[:, 0:n]), reads=[BK(bt)], writes=[("ybT", h, i)])

                samp_sched = {-2: [lambda: S1a(18)], 0: [lambda: S1(18)], 2: [lambda: S2(18)], 4: [lambda: S3(18), lambda: S1a(19)],
                              5: [lambda: S1(19)], 7: [lambda: S2(19)], 9: [lambda: S3(19)]}
                for t in range(-2, 10):
                    for f in samp_sched.get(t, []):
                        f()
                    if 0 <= t - 1 <= 8:
                        RT(t - 1)
                    if 0 <= t + 1 <= 8:
                        RU(t + 1)
                    if 0 <= t + 2 <= 8:
                        R1(t + 2)
                    if 0 <= t + 1 <= 8:
                        R2(t + 1)
            P.dma("sp", nretp_d.rearrange("h d v -> d h v"), sfin[:, :, :], reads=[("sfin", h) for h in range(8)])
            do_barrier()
            if debug:
                P.dma("sp", dbg_hT, hT[:, :, :])
                P.dma("sp", dbg_yaT, yaT[:, :, :])
                P.dma("sp", dbg_ybT, ybT[:, :, :])
                do_barrier()

            AR.at(ARW - 8 * T)
            mT = AR.bf(16, T)
            AR.at(XOFF)
            sga = [AR.f32(344), AR.f32(344)]
            sgb = [AR.f32(344), AR.f32(344)]
            t1b = [AR.f32(344), AR.f32(344)]
            t2b = [AR.f32(344), AR.f32(344)]
            assert AR.off <= ARW - 8 * T
            for m in range(16):
                wb, wkey = wload([(lambda w: w[:, 0:2048].rearrange("p (k n) -> p k n", k=16), wsrc(w_in, 0, 16, 6144 + 128 * m, 128)),
                                  (lambda w: w[:, 2048:4096].rearrange("p (k n) -> p k n", k=16), wsrc(w_in, 0, 16, 8192 + 128 * m, 128)),
                                  (lambda w: w[:, 4096:5120].rearrange("p (k n) -> p k n", k=8), wsrc(p_a, 0, 8, 128 * m, 128)),
                                  (lambda w: w[:, 5120:6144].rearrange("p (k n) -> p k n", k=8), wsrc(p_b, 0, 8, 128 * m, 128))])
                wga = wb[:, 0:2048].rearrange("p (k n) -> p k n", k=16)
                wgb = wb[:, 2048:4096].rearrange("p (k n) -> p k n", k=16)
                wpa = wb[:, 4096:5120].rearrange("p (k n) -> p k n", k=8)
                wpb = wb[:, 5120:6144].rearrange("p (k n) -> p k n", k=8)
                for ni, (t0, n) in enumerate(NT):
                    res = {}

                    def cons(name):
                        def c(ni_, t0_, n_, b):
                            res[name] = b
                        return c
                    fm_proj(wga, wkey, 0, 16, hT, "hT", [(t0, n)], cons("ga"))
                    fm_proj(wgb, wkey, 0, 16, hT, "hT", [(t0, n)], cons("gb"))
                    fm_proj(wpa, wkey, 0, 8, yaT, "yaT", [(t0, n)], cons("pa"), rkeys_fn=nt_over)
                    ybk = [("ybT", h, i) for h in range(8) for i in tt_over(t0, n)]
                    bpb = nb()
                    for kt in range(8):
                        P.op("pe", lambda e, kt=kt, bpb=bpb, t0=t0, n=n, wpb=wpb: e.matmul(banks[bpb][:, 0:n], lhsT=wpb[:, kt, :], rhs=ybT[:, kt, t0:t0 + n], start=(kt == 0), stop=(kt == 7)),
                             reads=[wkey] + [("ybT", kt, i) for i in tt_over(t0, n)], writes=[BK(bpb)])
                    q = (m * 4 + ni) % 2
                    P.op("act", lambda e, q=q, n=n, b=res["ga"]: e.activation(out=sga[q][:, 0:n], in_=banks[b][:, 0:n], func=AF.Sigmoid), reads=[BK(res["ga"])], writes=[("sga", q)])
                    P.op("act", lambda e, q=q, n=n, b=res["gb"]: e.activation(out=sgb[q][:, 0:n], in_=banks[b][:, 0:n], func=AF.Sigmoid), reads=[BK(res["gb"])], writes=[("sgb", q)])
                    P.op("dve", lambda e, q=q, n=n, b=res["pa"]: e.tensor_tensor(out=t1b[q][:, 0:n], in0=banks[b][:, 0:n], in1=sga[q][:, 0:n], op=ALU.mult),
                         reads=[BK(res["pa"]), ("sga", q)], writes=[("t1", q)])
                    P.op("dve", lambda e, q=q, n=n, b=bpb: e.tensor_tensor(out=t2b[q][:, 0:n], in0=banks[b][:, 0:n], in1=sgb[q][:, 0:n], op=ALU.mult),
                         reads=[BK(bpb), ("sgb", q)], writes=[("t2", q)])
                    P.op("pool", lambda e, q=q, n=n, m=m, t0=t0: e.tensor_tensor(out=mT[:, m, t0:t0 + n], in0=t1b[q][:, 0:n], in1=t2b[q][:, 0:n], op=ALU.add),
                         reads=[("t1", q), ("t2", q)], writes=[("mT", ni)])
            do_barrier()

            if debug:
                P.dma("sp", dbg_mT, mT[:, :, :])
                do_barrier()
            AR.at(0)
            acc = AR.f32(10, D)
            assert AR.off <= ARW - 8 * T
            for i, (t0, n) in enumerate(TILES):
                P.dma("sp", acc[0:n, i, :], xall[t0:t0 + n, :], writes=[("acc", i)])
            for cg in range(4):
                wb, wkey = wload([(lambda w: wview(w, 16, 512), wsrc(w_out, 0, 16, 512 * cg, 512))])
                wv = wview(wb, 16, 512)
                for i, (t0, n) in enumerate(TILES):
                    b = nb(0, 8)
                    tm_proj(wv, wkey, 0, 512, 16, mT, [("mT", q) for q in nt_over(t0, n)], t0, n, b)
                    P.op("dve", lambda e, i=i, n=n, b=b, cg=cg: e.tensor_tensor(out=acc[0:n, i, 512 * cg:512 * cg + 512], in0=banks[b][0:n, :], in1=acc[0:n, i, 512 * cg:512 * cg + 512], op=ALU.add),
                         reads=[BK(b), ("acc", i)], writes=[("acc", i)])
            do_barrier()

            if debug:
                P.dma("sp", dbg_x1, acc[:, :, :])
                do_barrier()
            AR.at(10 * D)
            h2T = AR.bf(16, T)
            uoff = AR.off
            uT = AR.bf(8, T)
            rtm = [AR.bf(344), AR.bf(344)]
            AR.at(uoff)
            hbs = [AR.bf(D), AR.bf(D)]

            def acc_src(i):
                t0, n = TILES[i]
                return acc[0:n, i, :], [("acc", i)]
            run_gen(norm_T(acc_src, TILES, 1, h2T, "h2T", "f", hbs))
            do_barrier()
            for fb in range(8):
                for sub in range(2):
                    wb, wkey = wload([(lambda w: wview(w, 16, 512), wsrc(w_up, 0, 16, 1024 * fb + 512 * sub, 512))])
                    wv = wview(wb, 16, 512)
                    for mm in range(4):
                        ft = 4 * sub + mm

                        def up_cons(ni, t0, n, b, ft=ft):
                            q = ni % 2
                            P.op("act", lambda e: e.activation(out=rtm[q][:, 0:n], in_=banks[b][:, 0:n], func=AF.Relu), reads=[BK(b)], writes=[("rtm", q)])
                            P.op("pool", lambda e: e.tensor_tensor(out=uT[:, ft, t0:t0 + n], in0=rtm[q][:, 0:n], in1=rtm[q][:, 0:n], op=ALU.mult),
                                 reads=[("rtm", q)], writes=[("uT", ft, ni)])
                        fm_proj(wv, wkey, 128 * mm, 16, h2T, "h2T", NT, up_cons)
                for cgp in range(2):
                    wb, wkey = wload([(lambda w: wview(w, 8, 1024), wsrc(w_down, 1024 * fb, 8, 1024 * cgp, 1024))])
                    wv = wview(wb, 8, 1024)
                    for c2 in range(2):
                        cg = 2 * cgp + c2
                        for i, (t0, n) in enumerate(TILES):
                            b = nb(0, 8)
                            for kt in range(8):
                                P.op("pe", lambda e, kt=kt, b=b, t0=t0, n=n, c2=c2, wv=wv: e.matmul(banks[b][0:n, :], lhsT=uT[:, kt, t0:t0 + n], rhs=wv[:, kt, 512 * c2:512 * c2 + 512],
                                                                                         start=(kt == 0), stop=(kt == 7)),
                                     reads=[wkey] + [("uT", kt, q) for q in nt_over(t0, n)], writes=[BK(b)])
                            P.op("dve", lambda e, i=i, n=n, b=b, cg=cg: e.tensor_tensor(out=acc[0:n, i, 512 * cg:512 * cg + 512], in0=banks[b][0:n, :], in1=acc[0:n, i, 512 * cg:512 * cg + 512], op=ALU.add),
                                 reads=[BK(b), ("acc", i)], writes=[("acc", i)])
            do_barrier()
            if debug:
                P.dma("sp", dbg_x2, acc[:, :, :])
                do_barrier()
            for i, hb, hk in norm_T(acc_src, TILES, 2, None, None, "z", hbs):
                t0, n = TILES[i]
                P.op("dve", lambda e, i=i, n=n: e.scalar_tensor_tensor(out=acc[0:n, i, :], in0=acc[0:n, i, :], scalar=rs[0:n, i:i + 1], in1=gbc[0:n, :], op0=ALU.mult, op1=ALU.mult),
                     reads=[("acc", i), ("rs", i), "gbc"], writes=[("acc", i)])
                P.dma("sp", y_d[t0:t0 + n, :], acc[0:n, i, :], reads=[("acc", i)])

            return P

        program()
        wl_state["dry"] = False
        build_nc.marks = []
        P = program()
        P.emit()
        build_nc.stats = P.stats
    return nc


def _tables(core):
    half = core % 2
    f32 = np.float32
    inv = (f32(10000.0) ** (-(np.arange(0, 128, 2, dtype=f32)) / f32(128))).astype(f32)
    t = np.arange(T)
    pos = np.where(t < TP, half * TP + t, 16384 + (t - TP) % 8).astype(f32)
    l = np.where(t < 928, t % CH, np.where(t < TP, t - 928, (t - TP) % 8)).astype(np.float64)
    ang = (pos[:, None] * inv[None, :]).astype(f32).astype(np.float64)
    cos, sin = np.cos(ang), np.sin(ang)
    gam = np.array(GAM, np.float64)
    dq = gam[None, :] ** (l[:, None] + 1.0)
    dk = gam[None, :] ** (-(l[:, None] + 1.0)) * (128.0 ** -0.5)
    tab = np.zeros((4, T, NH, 64), f32)
    tab[0] = cos[:, None, :] * dq[:, :, None]
    tab[1] = sin[:, None, :] * dq[:, :, None]
    tab[2] = cos[:, None, :] * dk[:, :, None]
    tab[3] = sin[:, None, :] * dk[:, :, None]
    tab = np.repeat(tab[:, :, :, None, :], 2, axis=3).reshape(4, T, 1024)
    tp = np.arange(TP)
    posq = tp.astype(f32)
    angq = (posq[:, None] * inv[None, :]).astype(f32).astype(np.float64)
    wq = gam[None, :] ** (TP - 1.0 - tp[:, None]) * (128.0 ** -0.5)
    tabq = np.zeros((2, TP, NH, 64), f32)
    tabq[0] = np.cos(angq)[:, None, :] * wq[:, :, None]
    tabq[1] = np.sin(angq)[:, None, :] * wq[:, :, None]
    tabq = np.repeat(tabq[:, :, :, None, :], 2, axis=3).reshape(2, TP, 1024)
    return tab, tabq


_NC_CACHE = {}


def kernel(x_prompt, x_sample, state_conv, state_lru, state_ret, meta_tokens, norm_mix_g, w_in, conv_w,
           conv_b, lru_wa, lru_ba, lru_wx, lru_bx, lru_lam, ret_norm_g, p_a, p_b, w_out, norm_ffn_g,
           w_up, w_down, norm_f_g):
    f32 = np.float32
    A = lambda a: np.ascontiguousarray(np.asarray(a, dtype=f32))
    x_prompt, x_sample = A(x_prompt), A(x_sample)
    meta = A(meta_tokens)
    dbg = bool(_NC_CACHE.get("debug"))
    if ("nc", dbg) not in _NC_CACHE:
        _NC_CACHE[("nc", dbg)] = build_nc(debug=dbg)
    nc = _NC_CACHE[("nc", dbg)]
    pp = np.arange(128)
    maskp = (pp[None, :] >= pp[:, None]).astype(f32)
    masks_ = maskp * (pp[None, :] // 8 == pp[:, None] // 8).astype(f32)
    masks = np.stack([maskp, masks_])
    selm = (pp[:, None] // 8 == np.arange(16)[None, :]).astype(f32)
    prm = np.concatenate([A(conv_w)[0], A(conv_b), A(lru_ba), A(lru_bx), A(lru_lam)], axis=0)
    gn3 = np.stack([A(norm_mix_g)[0], A(norm_ffn_g)[0], A(norm_f_g)])
    shared = dict(masks=masks, selm=selm, prm=A(prm), gn3=A(gn3), rng=A(ret_norm_g)[0], w_in=A(w_in)[0],
                  lwa=A(lru_wa)[0], lwx=A(lru_wx)[0], p_a=A(p_a)[0], p_b=A(p_b)[0], w_out=A(w_out)[0],
                  w_up=A(w_up)[0], w_down=A(w_down)[0])
    in_maps = []
    tabs = [_tables(0), _tables(1)]
    for c in range(8):
        b, half = c // 2, c % 2
        seq = np.concatenate([meta, x_prompt[b]], axis=0)
        own = seq[half * TP:(half + 1) * TP]
        pre = seq[0:TP]
        xs = x_sample[16 * c:16 * c + 16].reshape(128, D)
        fl = np.zeros((128, 8), f32)
        fl[:, 0] = half
        fl[:, 1] = 1 - half
        fl[:, 2] = half
        fl[:, 3] = 1.0
        fl[:, 4] = 0.0
        m = dict(shared)
        m.update(xall=A(np.concatenate([own, xs], axis=0)), xpre=A(pre),
                 sconv=A(state_conv[0, 16 * c:16 * c + 16].reshape(48, 1024)),
                 slru=A(state_lru[0, 16 * c:16 * c + 16]), sret=A(state_ret[0, 16 * c:16 * c + 16]),
                 flags=fl, tab=tabs[half][0], tabpre=tabs[half][1])
        in_maps.append(m)
    res = run_bass_kernel_spmd(nc, in_maps, core_ids=list(range(8)))
    R = res.results
    if dbg:
        _NC_CACHE["raw"] = R
    y_prompt = np.zeros((4, 2048, D), f32)
    y_sample = np.zeros((128, 8, D), f32)
    ncp = np.zeros((1, 4, 3, 1024), f32)
    nlp = np.zeros((1, 4, 1024), f32)
    nrp = np.zeros((1, 4, NH, 128, 128), f32)
    ncs = np.zeros((1, 128, 3, 1024), f32)
    nls = np.zeros((1, 128, 1024), f32)
    nrs = np.zeros((1, 128, NH, 128, 128), f32)
    for c in range(8):
        b, half = c // 2, c % 2
        r = R[c]
        y = np.asarray(r["y"])
        if half == 0:
            y_prompt[b, 0:TP - 16] = y[16:TP]
        else:
            y_prompt[b, TP - 16:] = y[0:TP]
            ncp[0, b] = np.asarray(r["nconv"])[48:51]
            nlp[0, b] = np.asarray(r["nlru"])[16]
            nrp[0, b] = np.asarray(r["nretp"])
        y_sample[16 * c:16 * c + 16] = y[TP:T].reshape(16, 8, D)
        ncs[0, 16 * c:16 * c + 16] = np.asarray(r["nconv"])[0:48].reshape(16, 3, 1024)
        nls[0, 16 * c:16 * c + 16] = np.asarray(r["nlru"])[0:16]
        nrs[0, 16 * c:16 * c + 16] = np.asarray(r["nrets"])
    return (y_prompt, y_sample, ncp, nlp, nrp, ncs, nls, nrs)
```

```python
import contextlib
import numpy as np
import concourse.bass as bass
import concourse.mybir as mybir
from concourse.bass_utils import run_bass_kernel_spmd

F32 = mybir.dt.float32
BF16 = mybir.dt.bfloat16
AF = mybir.ActivationFunctionType
ALU = mybir.AluOpType

D = 2048
TP = 1032
TS = 128
T = TP + TS
NH = 8
CH = 116
PT = [(CH * i, CH) for i in range(8)] + [(928, 104)]
TILES = PT + [(TP, TS)]
NTP = [(0, 344), (344, 344), (688, 344)]
NT = NTP + [(TP, TS)]
GAM = [1.0 - 2.0 ** (-5 - h) for h in range(NH)]
EPS = 1e-6
ENGS = ("pe", "act", "dve", "pool", "sp")


class Op:
    __slots__ = ("eng", "fn", "reads", "writes", "is_dma", "deps", "token", "need_inc", "idx", "tag",
                 "prev_same_sem")

    def __init__(self, eng, fn, reads, writes, is_dma, tag=None):
        self.eng = eng
        self.fn = fn
        self.reads = tuple(reads)
        self.writes = tuple(writes)
        self.is_dma = is_dma
        self.deps = set()
        self.token = None
        self.need_inc = False
        self.tag = tag
        self.prev_same_sem = None


class Prog:
    def __init__(self, nc, n_dma_sems=6):
        self.nc = nc
        self.ops = []
        self.n_dma_sems = n_dma_sems
        self.last_writer = {}
        self.readers = {}
        self.last_barrier = None
        self.last_on_eng = {}

    def op(self, eng, fn, reads=(), writes=(), tag=None):
        o = Op(eng, fn, reads, writes, False, tag)
        self._add(o)
        return o

    def dma(self, queue, out, in_, reads=(), writes=(), tag=None):
        n = out.shape[0]
        if n == in_.shape[0] and n > 16 and n % 16 != 0:
            n16 = n - n % 16
            o1 = self._dma1(queue, out[0:n16], in_[0:n16], reads, writes, tag, None)
            self._dma1(queue, out[n16:n], in_[n16:n], reads, writes, tag, o1)
            return o1
        return self._dma1(queue, out, in_, reads, writes, tag, None)

    def _dma1(self, queue, out, in_, reads, writes, tag, co):
        def fn(e, out=out, in_=in_):
            return e.dma_start(out=out, in_=in_)
        o = Op(queue, fn, reads, writes, True, tag)
        self._add(o, co)
        return o

    def barrier(self, fn, eng="dve"):
        o = Op(eng, fn, (), (), False, "barrier")
        o.idx = len(self.ops)
        start = self.last_barrier.idx if self.last_barrier is not None else 0
        for p in self.ops[start:]:
            if p.is_dma:
                o.deps.add(p.idx)
        for e, p in self.last_on_eng.items():
            o.deps.add(p.idx)
        self.ops.append(o)
        self.last_on_eng[eng] = o
        self.last_barrier = o
        self.last_writer.clear()
        self.readers.clear()
        return o

    def _add(self, o, co=None):
        o.idx = len(self.ops)
        lw, rd = self.last_writer, self.readers
        for k in o.reads:
            if k in lw:
                o.deps.update(lw[k])
        if co is not None:
            o.deps.update(d for d in co.deps)
        else:
            for k in o.writes:
                if k in lw:
                    o.deps.update(lw[k])
                if k in rd:
                    o.deps.update(rd[k])
        if self.last_barrier is not None:
            o.deps.add(self.last_barrier.idx)
        for k in o.writes:
            if co is not None:
                cur = lw.get(k, [])
                lw[k] = (cur if co.idx in cur else [co.idx]) + [o.idx]
            else:
                lw[k] = [o.idx]
                rd[k] = []
        for k in o.reads:
            lst = rd.setdefault(k, [])
            if not o.is_dma:
                lst[:] = [q for q in lst if self.ops[q].is_dma or self.ops[q].eng != o.eng]
            lst.append(o.idx)
        self.ops.append(o)
        self.last_on_eng[o.eng] = o

    @staticmethod
    def _skip(p, o):
        if p.is_dma:
            return False
        if p.eng != o.eng:
            return False
        if p.eng == "pe":
            return True
        if o.is_dma:
            return False
        if p.tag == "barrier":
            return False
        return not (set(p.writes) & set(o.reads))

    def emit(self):
        nc = self.nc
        ops = self.ops
        for o in ops:
            for d in o.deps:
                p = ops[d]
                if p.is_dma or self._skip(p, o):
                    continue
                p.need_inc = True
        with contextlib.ExitStack() as st:
            esem = {e: st.enter_context(nc.semaphore(f"s_{e}")) for e in ENGS[:4]}
            dsems = {}
            for q in ("sp", "act", "pool"):
                dsems[q] = [st.enter_context(nc.semaphore(f"d_{q}{j}")) for j in range(self.n_dma_sems)]
            tick = {e: 0 for e in ENGS}
            dcount = {q: 0 for q in dsems}
            dval = {q: [0] * self.n_dma_sems for q in dsems}
            prev_on_sem = {}
            for o in ops:
                if o.is_dma:
                    q = o.eng
                    j = dcount[q] % self.n_dma_sems
                    dcount[q] += 1
                    dval[q][j] += 16
                    key = ("d", q, j)
                    o.token = (key, dval[q][j])
                    o.prev_same_sem = prev_on_sem.get(key)
                    prev_on_sem[key] = o
                elif o.need_inc:
                    tick[o.eng] += 1
                    o.token = (("e", o.eng), tick[o.eng])

            def semof(key):
                return esem[key[1]] if key[0] == "e" else dsems[key[1]][key[2]]

            waited = {e: {} for e in ENGS}
            per_eng = {e: [] for e in ENGS}
            for o in ops:
                need = {}
                for d in o.deps:
                    p = ops[d]
                    if p.token is None or self._skip(p, o):
                        continue
                    k, v = p.token
                    if need.get(k, 0) < v:
                        need[k] = v
                if o.is_dma and o.prev_same_sem is not None:
                    k, v = o.prev_same_sem.token
                    if need.get(k, 0) < v:
                        need[k] = v
                w = waited[o.eng]
                waits = []
                for k, v in need.items():
                    if w.get(k, 0) < v:
                        w[k] = v
                        waits.append((k, v))
                per_eng[o.eng].append((o, waits))
            final_waits = []
            for q in dsems:
                for j in range(self.n_dma_sems):
                    if dval[q][j] > 0:
                        final_waits.append((("d", q, j), dval[q][j]))
            for e in ENGS[:4]:
                if tick[e] > 0:
                    final_waits.append((("e", e), tick[e]))
            self.stats = {e: len(per_eng[e]) for e in ENGS}
            self.stats["waits"] = sum(len(w) for e in ENGS for _, w in per_eng[e])
            self.stats["ticks"] = dict(tick)

            def run(ename, e):
                for o, waits in per_eng[ename]:
                    for k, v in waits:
                        e.wait_ge(semof(k), v)
                    ins = o.fn(e)
                    if o.is_dma:
                        ins.then_inc(semof(o.token[0]), 16)
                    elif o.token is not None:
                        ins.then_inc(semof(o.token[0]), 1)
                if ename == "sp":
                    for k, v in final_waits:
                        e.wait_ge(semof(k), v)

            with nc.Block() as block:
                @block.sync
                def _(e):
                    run("sp", e)

                @block.scalar
                def _(e):
                    run("act", e)

                @block.vector
                def _(e):
                    run("dve", e)

                @block.gpsimd
                def _(e):
                    run("pool", e)

                @block.tensor
                def _(e):
                    run("pe", e)


def tt_over(t0, n):
    return [i for i, (a, b) in enumerate(TILES) if a < t0 + n and t0 < a + b]


def nt_over(t0, n):
    return [i for i, (a, b) in enumerate(NT) if a < t0 + n and t0 < a + b]


def build_nc(debug=False):
    build_nc.marks = []
    nc = bass.Bass("TRN2", target_bir_lowering=False)

    def din(name, shape):
        return nc.dram_tensor(name, list(shape), F32, kind="ExternalInput").ap()

    def dout(name, shape):
        return nc.dram_tensor(name, list(shape), F32, kind="ExternalOutput").ap()

    xall = din("xall", [T, D])
    xpre = din("xpre", [TP, D])
    sconv = din("sconv", [48, 1024])
    slru = din("slru", [16, 1024])
    sret = din("sret", [16, NH, 128, 128])
    flags_d = din("flags", [128, 8])
    tab = din("tab", [4, T, 1024])
    tabpre = din("tabpre", [2, TP, 1024])
    masks_d = din("masks", [2, 128, 128])
    selm_d = din("selm", [128, 16])
    prm = din("prm", [8, 1024])
    gn3 = din("gn3", [3, D])
    rng = din("rng", [1024])
    w_in = din("w_in", [D, 10240])
    lwa = din("lwa", [16, 64, 64])
    lwx = din("lwx", [16, 64, 64])
    p_a = din("p_a", [1024, D])
    p_b = din("p_b", [1024, D])
    w_out = din("w_out", [D, D])
    w_up = din("w_up", [D, 8192])
    w_down = din("w_down", [8192, D])
    y_d = dout("y", [T, D])
    nconv_d = dout("nconv", [51, 1024])
    nlru_d = dout("nlru", [17, 1024])
    nretp_d = dout("nretp", [NH, 128, 128])
    nrets_d = dout("nrets", [16, NH, 128, 128])
    if debug:
        dbg_hT = nc.dram_tensor("dbg_hT", [128, 16, T], BF16, kind="ExternalOutput").ap()
        dbg_yaT = nc.dram_tensor("dbg_yaT", [128, 8, T], BF16, kind="ExternalOutput").ap()
        dbg_ybT = nc.dram_tensor("dbg_ybT", [128, 8, T], BF16, kind="ExternalOutput").ap()
        dbg_mT = nc.dram_tensor("dbg_mT", [128, 16, T], BF16, kind="ExternalOutput").ap()
        dbg_x1 = nc.dram_tensor("dbg_x1", [128, 10, D], F32, kind="ExternalOutput").ap()
        dbg_x2 = nc.dram_tensor("dbg_x2", [128, 10, D], F32, kind="ExternalOutput").ap()

    st = contextlib.ExitStack()
    with st:
        def sb(name, shape, dt):
            return st.enter_context(nc.sbuf_tensor("sb_" + name, list(shape), dt))

        identb = sb("identb", [128, 128], BF16)
        identf = sb("identf", [128, 128], F32)
        masks = sb("masks", [128, 2, 128], F32)
        selm = sb("selm", [128, 16], F32)
        flags = sb("flags", [128, 8], F32)
        prm_fm = sb("prm_fm", [128, 8, 8], F32)
        c8 = sb("c8", [128, 8], F32)
        c16 = sb("c16", [128, 8], F32)
        wabd = sb("wabd", [128, 8, 128], BF16)
        wxbd = sb("wxbd", [128, 8, 128], BF16)
        sc_fm = sb("sc_fm", [128, 8, 48], F32)
        h0_fm = sb("h0_fm", [128, 8, 16], F32)
        hl_fm = sb("hl_fm", [128, 8, 17], F32)
        xtail = sb("xtail", [128, 8, 3], F32)
        hin = sb("hin", [128, 8], F32)
        spre = sb("spre", [128, NH, 128], F32)
        sfin = sb("sfin", [128, NH, 128], F32)
        ss = sb("ss", [128, 16], F32)
        rs = sb("rs", [128, 16], F32)
        gbc = sb("gbc", [128, D], F32)
        gnbc = sb("gnbc", [128, 1024], F32)
        bnst = sb("bnst", [128, 2, 6], F32)
        mv = sb("mv", [128, 2, 2], F32)
        rsd = sb("rsd", [128, 2], F32)
        tmp16 = sb("tmp16", [128, 16], F32)
        tmp16b = sb("tmp16b", [128, 16], F32)
        tmp16s = [tmp16b, tmp16]
        WBW = 16 * 512
        wbs = [sb(f"wb{k}", [128, WBW], BF16) for k in range(2)]
        ARW = 35700
        arena = sb("arena", [128, ARW], F32)
        banks = [st.enter_context(nc.psum_tensor(f"bank{i}", [128, 512], F32)) for i in range(8)]

        wl_log = []
        wl_state = {"dry": True}

        def program():
            P = Prog(nc)
            bank_ctr = [0]

            def nb(lo=0, hi=6):
                i = lo + bank_ctr[0] % (hi - lo)
                bank_ctr[0] += 1
                return i

            def BK(i):
                return ("ps", i)

            class Arena:
                def __init__(self):
                    self.off = 0

                def at(self, off):
                    self.off = off

                def f32(self, *shape):
                    n = int(np.prod(shape))
                    v = arena[:, self.off:self.off + n]
                    self.off += n
                    assert self.off <= ARW, self.off
                    if len(shape) == 2:
                        return v.rearrange("p (a b) -> p a b", a=shape[0])
                    if len(shape) == 3:
                        return v.rearrange("p (a b c) -> p a b c", a=shape[0], b=shape[1])
                    return v

                def bf(self, *shape):
                    n = int(np.prod(shape))
                    w = (n + 1) // 2
                    v = arena[:, self.off:self.off + w].bitcast(BF16)
                    self.off += w
                    assert self.off <= ARW, self.off
                    v = v[:, 0:n]
                    if len(shape) == 2:
                        return v.rearrange("p (a b) -> p a b", a=shape[0])
                    if len(shape) == 3:
                        return v.rearrange("p (a b c) -> p a b c", a=shape[0], b=shape[1])
                    return v

            AR = Arena()

            def do_barrier():
                P.barrier(lambda e: e.memset(tmp16[:, 0:1], 0.0))
                build_nc.marks.append({e: sum(1 for o in P.ops if o.eng == e) for e in ENGS})

            AR.at(ARW - 3 * 1024)
            prm_tm = AR.f32(1024)
            sc_tm = AR.f32(1024)
            h0_tm = AR.f32(1024)
            P.op("pool", lambda e: e.memset(identf[:], 1.0), writes=["identf"])
            P.op("pool", lambda e: e.affine_select(out=identf[:], in_=identf[:], pattern=[[-1, 128]],
                                                    compare_op=ALU.is_equal, fill=0.0, base=0, channel_multiplier=1),
                 reads=["identf"], writes=["identf"])
            P.op("dve", lambda e: e.tensor_copy(out=identb[:], in_=identf[:]), reads=["identf"], writes=["identb"])
            P.dma("sp", masks[:], masks_d.rearrange("a p n -> p a n"), writes=["masks"])
            P.dma("sp", selm[:], selm_d, writes=["selm"])
            P.dma("sp", flags[:], flags_d, writes=["flags"])
            P.dma("sp", prm_tm[0:8, :], prm, writes=["prm_tm"])
            P.dma("sp", sc_tm[0:48, :], sconv, writes=["sc_tm"])
            P.dma("sp", h0_tm[0:16, :], slru, writes=["h0_tm"])
            P.dma("sp", gnbc[:], rng.partition_broadcast(128), writes=["gnbc"])
            P.op("pool", lambda e: e.memset(wabd[:], 0.0), writes=["wabd"])
            P.op("pool", lambda e: e.memset(wxbd[:], 0.0), writes=["wxbd"])
            for (wsrc, wdst, key) in ((lwa, wabd, "wabd"), (lwx, wxbd, "wxbd")):
                v = wsrc.rearrange("(j two) c d -> two c j d", two=2)
                P.dma("pool", wdst[0:64, :, 0:64], v[0], reads=[key], writes=[key])
                P.dma("pool", wdst[64:128, :, 64:128], v[1], reads=[key], writes=[key])
            for j in range(8):
                b = nb()
                P.op("pe", lambda e, j=j, b=b: e.transpose(banks[b][:, 0:8], prm_tm[0:8, 128 * j:128 * j + 128], identf[0:8, 0:8]),
                     reads=["prm_tm", "identf"], writes=[BK(b)])
                P.op("pe", lambda e, j=j, b=b: e.transpose(banks[b][:, 8:56], sc_tm[0:48, 128 * j:128 * j + 128], identf[0:48, 0:48]),
                     reads=["sc_tm", "identf"], writes=[BK(b)])
                P.op("pe", lambda e, j=j, b=b: e.transpose(banks[b][:, 56:72], h0_tm[0:16, 128 * j:128 * j + 128], identf[0:16, 0:16]),
                     reads=["h0_tm", "identf"], writes=[BK(b)])
                P.op("dve", lambda e, j=j, b=b: e.tensor_copy(out=prm_fm[:, j, :], in_=banks[b][:, 0:8]), reads=[BK(b)], writes=["prm_fm"])
                P.op("dve", lambda e, j=j, b=b: e.tensor_copy(out=sc_fm[:, j, :], in_=banks[b][:, 8:56]), reads=[BK(b)], writes=["sc_fm"])
                P.op("dve", lambda e, j=j, b=b: e.tensor_copy(out=h0_fm[:, j, :], in_=banks[b][:, 56:72]), reads=[BK(b)], writes=["h0_fm"])
            P.op("act", lambda e: e.activation(out=c8[:], in_=prm_fm[:, :, 7], func=AF.Sigmoid), reads=["prm_fm"], writes=["c8"])
            P.op("act", lambda e: e.activation(out=c8[:], in_=c8[:], func=AF.Ln), reads=["c8"], writes=["c8"])
            P.op("dve", lambda e: e.tensor_scalar_mul(out=c16[:], in0=c8[:], scalar1=16.0), reads=["c8"], writes=["c16"])
            P.op("dve", lambda e: e.tensor_scalar_mul(out=c8[:], in0=c8[:], scalar1=8.0), reads=["c8", "c16"], writes=["c8"])

            do_barrier()

            wslot = [0]
            issued = set()

            def _issue(c):
                if c in issued or c >= len(wl_log):
                    return
                issued.add(c)
                k = c % 2
                first = None
                for fn, src in wl_log[c]:
                    o = P._dma1("pool", fn(wbs[k]), src, (), [("wb", k)], None, first)
                    if first is None:
                        first = o

            def wload(parts, prefetch=True):
                c = wslot[0]
                wslot[0] += 1
                k = c % 2
                if wl_state["dry"]:
                    wl_log.append(parts)
                    return wbs[k], ("wb", k)
                _issue(c)
                if prefetch:
                    _issue(c + 1)
                return wbs[k], ("wb", k)

            def wprefetch(n=2):
                if wl_state["dry"]:
                    return
                for c in range(wslot[0], wslot[0] + n):
                    _issue(c)

            def wview(wb, nk, ncol, off=0):
                return wb[:, off:off + nk * ncol].rearrange("p (k n) -> p k n", k=nk)

            def wsrc(w, r0, nk, c0, ncol):
                return w[r0:r0 + 128 * nk, c0:c0 + ncol].rearrange("(k p) n -> p k n", p=128)

            def norm_T(src_tiles, tiles, grow, dstT, dkey, stage, hbs):
                P.dma("sp", gbc[:], gn3[grow].partition_broadcast(128), writes=["gbc"])
                P.op("dve", lambda e: e.memset(ss[:], 0.0), writes=["ss"])
                info = {}

                def N1(i):
                    t0, n = tiles[i]
                    src, skeys = src_tiles(i)
                    hb = hbs[i % len(hbs)]
                    hk = ("hb", stage, i % len(hbs))
                    info[i] = (src, skeys, hb, hk)
                    P.op("act", lambda e: e.activation(out=hb[0:n, :], in_=src, func=AF.Square, accum_out=ss[0:n, i:i + 1]),
                         reads=skeys + ["ss"], writes=[hk, ("ss", i)])
                    P.op("act", lambda e: e.activation(out=rs[0:n, i:i + 1], in_=ss[0:n, i:i + 1], func=AF.Sqrt, scale=1.0 / D, bias=EPS),
                         reads=[("ss", i)], writes=[("rs", i)])
                    P.op("dve", lambda e: e.reciprocal(out=rs[0:n, i:i + 1], in_=rs[0:n, i:i + 1]), reads=[("rs", i)], writes=[("rs", i)])
                    if dstT is not None:
                        P.op("dve", lambda e: e.scalar_tensor_tensor(out=hb[0:n, :], in0=src, scalar=rs[0:n, i:i + 1], in1=gbc[0:n, :], op0=ALU.mult, op1=ALU.mult),
                             reads=skeys + [("rs", i), "gbc", hk], writes=[hk])

                def N2(i):
                    t0, n = tiles[i]
                    src, skeys, hb, hk = info[i]
                    for half in range(2):
                        b = nb()
                        pv = banks[b][:].bitcast(BF16).rearrange("p (k n) -> p k n", k=8)
                        for kk in range(8):
                            kt = half * 8 + kk
                            P.op("pe", lambda e, pv=pv, kk=kk, kt=kt: e.transpose(pv[:, kk, 0:n], hb[0:n, kt * 128:(kt + 1) * 128], identb[0:n, 0:n]),
                                 reads=[hk, "identb"], writes=[BK(b)])
                        if half == 0:
                            P.op("act", lambda e, pv=pv, half=half: e.copy(out=dstT[:, half * 8:half * 8 + 8, t0:t0 + n], in_=pv[:, :, 0:n]),
                                 reads=[BK(b)], writes=[(dkey, i)])
                        else:
                            P.op("dve", lambda e, pv=pv, half=half: e.tensor_copy(out=dstT[:, half * 8:half * 8 + 8, t0:t0 + n], in_=pv[:, :, 0:n]),
                                 reads=[BK(b)], writes=[(dkey, i)])

                nt_ = len(tiles)
                if dstT is None:
                    for i in range(nt_):
                        N1(i)
                        yield i, info[i][2], info[i][3]
                    return
                N1(0)
                for i in range(nt_):
                    if i + 1 < nt_:
                        N1(i + 1)
                    N2(i)
                    yield i, info[i][2], info[i][3]

            def run_gen(g):
                for _ in g:
                    pass

            def fm_proj(wv, wkey, c0, nk, rhsT, rkey, ntiles, consumer, rkeys_fn=None):
                bl = []
                for (t0, n) in ntiles:
                    b = nb()
                    bl.append(b)
                for kt in range(nk):
                    for (t0, n), b in zip(ntiles, bl):
                        rk = [(rkey, i) for i in (rkeys_fn(t0, n) if rkeys_fn else tt_over(t0, n))]
                        P.op("pe", lambda e, b=b, kt=kt, t0=t0, n=n: e.matmul(
                            banks[b][:, 0:n], lhsT=wv[:, kt, c0:c0 + 128], rhs=rhsT[:, kt, t0:t0 + n],
                            start=(kt == 0), stop=(kt == nk - 1)), reads=[wkey] + rk, writes=[BK(b)])
                for ni, ((t0, n), b) in enumerate(zip(ntiles, bl)):
                    consumer(ni, t0, n, b)

            def tm_proj(wv, wkey, c0, ncol, nk, lhsT, lkeys, t0, n, b):
                for kt in range(nk):
                    P.op("pe", lambda e, kt=kt: e.matmul(
                        banks[b][0:n, 0:ncol], lhsT=lhsT[:, kt, t0:t0 + n], rhs=wv[:, kt, c0:c0 + ncol],
                        start=(kt == 0), stop=(kt == nk - 1)), reads=[wkey] + lkeys, writes=[BK(b)])

            def lru_tile(j, hT, hkey, ntl, Tn, has_s, L, wv, wkey, f1col, f0col):
                q = j % 2
                xa, xc, xcb, r_, i_, a_ = (L[q][k] for k in ("xa", "xc", "xcb", "r", "i", "a"))
                xas = L[q].get("xas")
                ggb = L[q].get("gg")
                t16 = tmp16s[q]
                pre = "m" if has_s else "q"

                def K(nm, ni=None):
                    return ("L", pre, q, nm, ni) if ni is not None else ("L", pre, q, nm)
                allnt = list(range(len(ntl)))
                xar = [K("xa", ni) for ni in allnt] + [K("xah")] + ([K("xash")] if has_s else [])

                def xa_cons(ni, t0, n, b):
                    if t0 < TP:
                        P.op("dve", lambda e: e.tensor_copy(out=xa[:, 3 + t0:3 + t0 + n], in_=banks[b][:, 0:n]),
                             reads=[BK(b)], writes=[K("xa", ni)])
                    else:
                        P.op("dve", lambda e: e.tensor_copy(out=xas[:, :, 3:11], in_=banks[b][:, 0:128].rearrange("p (s t) -> p s t", t=8)),
                             reads=[BK(b)], writes=[K("xa", ni)])
                def LAp():
                    fm_proj(wv, wkey, 0, 16, hT, hkey, ntl, xa_cons)

                def LA():
                    if not has_s:
                        P.op("pool", lambda e: e.tensor_copy(out=xtail[:, j, :], in_=xa[:, TP:TP + 3]), reads=xar, writes=["xtail"])
                    if has_s:
                        def ga_cons(ni, t0, n, b):
                            P.op("act", lambda e: e.activation(out=ggb[:, t0:t0 + n], in_=banks[b][:, 0:n], func=AF.Gelu_apprx_tanh),
                                 reads=[BK(b)], writes=[K("gg")])
                        fm_proj(wv, wkey, 128, 16, hT, hkey, ntl, ga_cons)
                        P.op("dve", lambda e: e.tensor_scalar_mul(out=xa[:, 0:3], in0=xtail[:, j, :], scalar1=flags[:, 0:1]),
                             reads=["xtail", "flags"], writes=[K("xah")])
                        P.op("dve", lambda e: e.tensor_copy(out=xas[:, :, 0:3], in_=sc_fm[:, j, :].rearrange("p (s t) -> p s t", t=3)),
                             reads=["sc_fm"], writes=[K("xash")])
                    else:
                        P.op("dve", lambda e: e.memset(xa[:, 0:3], 0.0), writes=[K("xah")])
                    if has_s:
                        P.op("pool", lambda e: e.tensor_copy(out=cv_fm[:, j, 0:48].rearrange("p (s t) -> p s t", t=3), in_=xas[:, :, 8:11]), reads=xar, writes=["cv_fm"])
                        P.op("pool", lambda e: e.tensor_copy(out=cv_fm[:, j, 48:51], in_=xa[:, TP:TP + 3]), reads=xar, writes=["cv_fm"])
                    views = [(xc[:, 0:TP], lambda k: xa[:, k:k + TP])]
                    if has_s:
                        views.append((xc[:, TP:T].rearrange("p (s t) -> p s t", t=8), lambda k: xas[:, :, k:k + 8]))
                    xbv = [xcb[:, 0:TP]] + ([xcb[:, TP:T].rearrange("p (s t) -> p s t", t=8)] if has_s else [])
                    for vi, (ov, iv) in enumerate(views):
                        P.op("dve", lambda e, ov=ov, iv=iv: e.tensor_scalar(out=ov, in0=iv(0), scalar1=prm_fm[:, j, 0:1], scalar2=prm_fm[:, j, 4:5],
                                                                            op0=ALU.mult, op1=ALU.add),
                             reads=xar + ["prm_fm"], writes=[K("xc")])
                        for k in range(1, 3):
                            P.op("dve", lambda e, ov=ov, iv=iv, k=k: e.scalar_tensor_tensor(out=ov, in0=iv(k), scalar=prm_fm[:, j, k:k + 1], in1=ov,
                                                                                             op0=ALU.mult, op1=ALU.add),
                                 reads=xar + ["prm_fm", K("xc")], writes=[K("xc")])
                        P.op("dve", lambda e, ov=ov, iv=iv, xb=xbv[vi]: e.scalar_tensor_tensor(out=xb, in0=iv(3), scalar=prm_fm[:, j, 3:4], in1=ov,
                                                                                            op0=ALU.mult, op1=ALU.add),
                             reads=xar + ["prm_fm", K("xc")], writes=[K("xcb")])
                        P.op("dve", lambda e, ov=ov, iv=iv: e.scalar_tensor_tensor(out=ov, in0=iv(3), scalar=prm_fm[:, j, 3:4], in1=ov,
                                                                                    op0=ALU.mult, op1=ALU.add),
                             reads=xar + ["prm_fm", K("xc"), K("xcb")], writes=[K("xc")])

                def LB():
                    for ni, (t0, n) in enumerate(ntl):
                        br, bi = nb(), nb()
                        P.op("pe", lambda e, br=br, t0=t0, n=n: e.matmul(banks[br][:, 0:n], lhsT=wabd[:, j, :], rhs=xcb[:, t0:t0 + n], start=True, stop=True),
                             reads=["wabd", K("xcb")], writes=[BK(br)])
                        P.op("pe", lambda e, bi=bi, t0=t0, n=n: e.matmul(banks[bi][:, 0:n], lhsT=wxbd[:, j, :], rhs=xcb[:, t0:t0 + n], start=True, stop=True),
                             reads=["wxbd", K("xcb")], writes=[BK(bi)])
                        P.op("act", lambda e, br=br, t0=t0, n=n: e.activation(out=r_[:, t0:t0 + n], in_=banks[br][:, 0:n], func=AF.Sigmoid, bias=prm_fm[:, j, 5:6]),
                             reads=[BK(br), "prm_fm"], writes=[K("r")])
                        P.op("act", lambda e, bi=bi, t0=t0, n=n: e.activation(out=i_[:, t0:t0 + n], in_=banks[bi][:, 0:n], func=AF.Sigmoid, bias=prm_fm[:, j, 6:7]),
                             reads=[BK(bi), "prm_fm"], writes=[K("i")])
                    P.op("pool", lambda e: e.tensor_tensor(out=i_[:, 0:Tn], in0=i_[:, 0:Tn], in1=xc[:, 0:Tn], op=ALU.mult), reads=[K("i"), K("xc")], writes=[K("i")])
                    P.op("act", lambda e: e.activation(out=a_[:, 0:Tn], in_=r_[:, 0:Tn], func=AF.Exp, scale=c8[:, j:j + 1]), reads=[K("r"), "c8"], writes=[K("a")])
                    P.op("act", lambda e: e.activation(out=r_[:, 0:Tn], in_=r_[:, 0:Tn], func=AF.Exp, scale=c16[:, j:j + 1]), reads=[K("r"), K("a"), "c16"], writes=[K("r")])
                    P.op("act", lambda e: e.activation(out=r_[:, 0:Tn], in_=r_[:, 0:Tn], func=AF.Relu, scale=-1.0, bias=1.0), reads=[K("r")], writes=[K("r")])
                    P.op("act", lambda e: e.activation(out=r_[:, 0:Tn], in_=r_[:, 0:Tn], func=AF.Sqrt), reads=[K("r")], writes=[K("r")])

                def LC():
                    P.op("dve", lambda e: e.tensor_scalar(out=r_[:, 0:1], in0=r_[:, 0:1], scalar1=flags[:, f0col:f0col + 1], scalar2=flags[:, f1col:f1col + 1],
                                                          op0=ALU.mult, op1=ALU.add), reads=[K("r"), "flags"], writes=[K("r")])
                    P.op("dve", lambda e: e.tensor_tensor(out=i_[:, 0:Tn], in0=r_[:, 0:Tn], in1=i_[:, 0:Tn], op=ALU.mult), reads=[K("r"), K("i")], writes=[K("i")])
                    if has_s:
                        a0 = a_[:, TP:T].rearrange("p (s t) -> p s t", t=8)[:, :, 0]
                        b0 = i_[:, TP:T].rearrange("p (s t) -> p s t", t=8)[:, :, 0]
                        P.op("dve", lambda e: e.tensor_tensor(out=t16[:], in0=a0, in1=h0_fm[:, j, :], op=ALU.mult), reads=[K("a"), "h0_fm"], writes=[("t16", q)])
                        P.op("dve", lambda e: e.tensor_tensor(out=b0, in0=b0, in1=t16[:], op=ALU.add), reads=[K("i"), ("t16", q)], writes=[K("i")])
                        P.op("dve", lambda e: e.memset(a0, 0.0), reads=[K("r"), ("t16", q)], writes=[K("a")])
                        P.op("dve", lambda e: e.tensor_tensor_scan(out=r_[:, 0:Tn], data0=a_[:, 0:Tn], data1=i_[:, 0:Tn], initial=hin[:, j:j + 1],
                                                                   op0=ALU.mult, op1=ALU.add), reads=[K("a"), K("i"), "hin"], writes=[K("r")])
                        P.op("pool", lambda e: e.tensor_copy(out=hl_fm[:, j, 0:16], in_=r_[:, TP:T].rearrange("p (s t) -> p s t", t=8)[:, :, 7]),
                             reads=[K("r")], writes=["hl_fm"])
                        P.op("pool", lambda e: e.tensor_copy(out=hl_fm[:, j, 16:17], in_=r_[:, TP - 1:TP]), reads=[K("r")], writes=["hl_fm"])
                        for ni, (t0, n) in enumerate(ntl):
                            eng = "dve" if ni % 2 == 0 else "pool"
                            P.op(eng, lambda e, t0=t0, n=n: e.tensor_tensor(out=yaT[:, j, t0:t0 + n], in0=r_[:, t0:t0 + n], in1=ggb[:, t0:t0 + n], op=ALU.mult),
                                 reads=[K("gg"), K("r")], writes=[("yaT", ni)])
                    else:
                        P.op("dve", lambda e: e.tensor_tensor_scan(out=r_[:, 0:Tn], data0=a_[:, 0:Tn], data1=i_[:, 0:Tn], initial=0.0,
                                                                   op0=ALU.mult, op1=ALU.add), reads=[K("a"), K("i")], writes=[K("r")])
                        P.op("dve", lambda e: e.tensor_scalar_mul(out=hin[:, j:j + 1], in0=r_[:, TP - 1:TP], scalar1=flags[:, 0:1]),
                             reads=[K("r"), "flags"], writes=["hin"])
                return LA, LB, LC, LAp

            def lru_bufs(Tn, has_s):
                Ls = []
                for q in range(2):
                    L = {}
                    L["xa"] = AR.f32(TP + 3 + 1)
                    if has_s:
                        L["xas"] = AR.f32(16, 11)
                        L["gg"] = AR.bf(Tn)
                    for k in ("xc", "r", "i", "a"):
                        L[k] = AR.f32(Tn)
                    L["xcb"] = AR.bf(Tn)
                    Ls.append(L)
                return Ls

            def rope(b, n, ct, st_, tkeys, out, okey, tmp, tk):
                x = banks[b][0:n, 0:256].rearrange("p (h two d) -> p h two d", h=2, two=2)
                x1, x2 = x[:, :, 0, :], x[:, :, 1, :]
                o = out.rearrange("p (h two d) -> p h two d", h=2, two=2)
                t1, t2, t3, t4 = (tmp[0:n, q, :].rearrange("p (h d) -> p h d", h=2) for q in range(4))
                P.op("dve", lambda e: e.tensor_tensor(out=t1, in0=x1, in1=ct, op=ALU.mult), reads=[BK(b)] + tkeys, writes=[(tk, 0)])
                P.op("dve", lambda e: e.tensor_tensor(out=t2, in0=x2, in1=st_, op=ALU.mult), reads=[BK(b)] + tkeys, writes=[(tk, 1)])
                P.op("dve", lambda e: e.tensor_tensor(out=t3, in0=x2, in1=ct, op=ALU.mult), reads=[BK(b)] + tkeys, writes=[(tk, 2)])
                P.op("dve", lambda e: e.tensor_tensor(out=t4, in0=x1, in1=st_, op=ALU.mult), reads=[BK(b)] + tkeys, writes=[(tk, 3)])
                P.op("pool", lambda e: e.tensor_tensor(out=o[:, :, 0, :], in0=t1, in1=t2, op=ALU.subtract), reads=[(tk, 0), (tk, 1)], writes=[okey])
                P.op("pool", lambda e: e.tensor_tensor(out=o[:, :, 1, :], in0=t3, in1=t4, op=ALU.add), reads=[(tk, 2), (tk, 3)], writes=[okey])

            AR.at(ARW - 8 * TP)
            hTq = AR.bf(16, TP)
            AR.at(0)
            xts = [AR.f32(D), AR.f32(D)]
            hbs = [AR.bf(D), AR.bf(D)]
            mark = AR.off

            def pre_src(i):
                t0, n = PT[i]
                xt = xts[i % 2]
                P.dma("sp", xt[0:n, :], xpre[t0:t0 + n, :], writes=[("xt", i % 2)])
                return xt[0:n, :], [("xt", i % 2)]
            run_gen(norm_T(pre_src, PT, 0, hTq, "hTq", "q", hbs))
            L = lru_bufs(TP, False)
            stages = {}

            def pre_make(j):
                wb, wkey = wload([(lambda w: wview(w, 16, 128), wsrc(w_in, 0, 16, 128 * j, 128))])
                stages[j] = lru_tile(j, hTq, "hTq", NTP, TP, False, L, wview(wb, 16, 128), wkey, 3, 4)
            for step in range(-2, 9):
                if 0 <= step - 1 < 8:
                    stages[step - 1][2]()
                if 0 <= step < 8:
                    stages[step][1]()
                if 0 <= step + 1 < 8:
                    stages[step + 1][0]()
                if 0 <= step + 2 < 8:
                    pre_make(step + 2)
                    stages[step + 2][3]()
            kv_start = AR.off
            khat = AR.bf(9, 256)
            vtm = [AR.bf(256), AR.bf(256)]
            rtmp = [AR.f32(4, 128), AR.f32(4, 128)]
            tbl = [AR.f32(2, 256), AR.f32(2, 256)]
            kv_end = AR.off
            do_barrier()
            AR.at(0)
            hT = AR.bf(16, T)
            yaT = AR.bf(8, T)
            ybT = AR.bf(8, T)
            XOFF = AR.off
            cv_fm = AR.f32(8, 51)
            cv_tm = AR.f32(1024)
            hl_tm = AR.f32(1024)
            markA = AR.off
            xtsA = [AR.f32(D), AR.f32(D)]
            hbsA = [AR.bf(D), AR.bf(D)]
            assert kv_start >= 8 * T and kv_end <= markA and AR.off <= ARW - 8 * TP, (kv_start, kv_end, markA, AR.off)

            def main_src(i):
                t0, n = TILES[i]
                xt = xtsA[i % 2]
                P.dma("sp", xt[0:n, :], xall[t0:t0 + n, :], writes=[("xtA", i % 2)])
                return xt[0:n, :], [("xtA", i % 2)]
            gA = norm_T(main_src, TILES, 0, hT, "hT", "m", hbsA)
            kvit = [0]
            for hg in range(4):
                wb, wkey = wload([(lambda w: wview(w, 16, 512)[:, :, 0:256], wsrc(w_in, 0, 16, 3072 + 256 * hg, 256)),
                                  (lambda w: wview(w, 16, 512)[:, :, 256:512], wsrc(w_in, 0, 16, 4096 + 256 * hg, 256))])
                wv = wview(wb, 16, 512)
                sb_ = 6 + hg % 2
                pend = [None]
                for i, (t0, n) in enumerate(PT):
                    tb = tbl[i % 2]
                    P.dma("sp", tb[0:n, :, :], tabpre[:, t0:t0 + n, 256 * hg:256 * hg + 256].rearrange("a p n -> p a n"), writes=[("tblq", i % 2)])
                    b = nb()
                    tm_proj(wv, wkey, 0, 256, 16, hTq, [("hTq", i)], t0, n, b)
                    rope(b, n, tb[0:n, 0, :].rearrange("p (h two d) -> p h two d", h=2, two=2)[:, :, 0, :],
                         tb[0:n, 1, :].rearrange("p (h two d) -> p h two d", h=2, two=2)[:, :, 0, :],
                         [("tblq", i % 2)], khat[0:n, i, :], ("khat", i), rtmp[i % 2], ("rtq", i % 2))
                    b2 = nb()
                    tm_proj(wv, wkey, 256, 256, 16, hTq, [("hTq", i)], t0, n, b2)
                    vt = vtm[i % 2]
                    P.op("act", lambda e, vt=vt, n=n, b2=b2: e.copy(out=vt[0:n, :], in_=banks[b2][0:n, 0:256]), reads=[BK(b2)], writes=[("vtq", i % 2)])
                    def s_acc(i=i, n=n, vt=vt):
                        for hh in range(2):
                            P.op("pe", lambda e, hh=hh: e.matmul(
                                banks[6 + hh][:, 0:128], lhsT=khat[0:n, i, hh * 128:hh * 128 + 128], rhs=vt[0:n, hh * 128:hh * 128 + 128],
                                start=(i == 0), stop=(i == 8)), reads=[("khat", i), ("vtq", i % 2)], writes=[BK(6 + hh)])
                    if pend[0] is not None:
                        pend[0]()
                    pend[0] = s_acc
                    kvit[0] += 1
                    if kvit[0] % 3 == 2:
                        next(gA, None)
                pend[0]()
                pend[0] = None
                for hh in range(2):
                    P.op("dve", lambda e, hg=hg, hh=hh: e.tensor_scalar_mul(out=spre[:, 2 * hg + hh, :], in0=banks[6 + hh][:, 0:128], scalar1=flags[:, 0:1]),
                         reads=[BK(6 + hh), "flags"], writes=[("spre", hg, hh)])
            for _ in gA:
                pass
            do_barrier()

            mark = markA
            AR.at(mark)
            L = lru_bufs(T, True)
            stages = {}

            def main_make(j):
                wb, wkey = wload([(lambda w: wview(w, 16, 256)[:, :, 0:128], wsrc(w_in, 0, 16, 128 * j, 128)),
                                  (lambda w: wview(w, 16, 256)[:, :, 128:256], wsrc(w_in, 0, 16, 1024 + 128 * j, 128))])
                stages[j] = lru_tile(j, hT, "hT", NT, T, True, L, wview(wb, 16, 256), wkey, 1, 2)
            for step in range(-2, 9):
                if 0 <= step - 1 < 8:
                    stages[step - 1][2]()
                if 0 <= step < 8:
                    stages[step][1]()
                if 0 <= step + 1 < 8:
                    stages[step + 1][0]()
                if 0 <= step + 2 < 8:
                    main_make(step + 2)
                    stages[step + 2][3]()
            for half in range(2):
                b = nb()
                for jj in range(4):
                    j = half * 4 + jj
                    P.op("pe", lambda e, j=j, jj=jj, b=b: e.transpose(banks[b][0:17, jj * 128:jj * 128 + 128], hl_fm[:, j, :], identf[:, :]),
                         reads=["hl_fm", "identf"], writes=[BK(b)])
                P.op("act", lambda e, half=half, b=b: e.copy(out=hl_tm[0:17, half * 512:half * 512 + 512], in_=banks[b][0:17, :]), reads=[BK(b)], writes=["hl_tm"])
            P.dma("sp", nlru_d, hl_tm[0:17, :], reads=["hl_tm"])
            for half in range(2):
                b = nb()
                for jj in range(4):
                    j = half * 4 + jj
                    P.op("pe", lambda e, j=j, jj=jj, b=b: e.transpose(banks[b][0:51, jj * 128:jj * 128 + 128], cv_fm[:, j, :], identf[:, :]),
                         reads=["cv_fm", "identf"], writes=[BK(b)])
                P.op("act", lambda e, half=half, b=b: e.copy(out=cv_tm[0:51, half * 512:half * 512 + 512], in_=banks[b][0:51, :]), reads=[BK(b)], writes=["cv_tm"])
            P.dma("sp", nconv_d, cv_tm[0:51, :], reads=["cv_tm"])
            do_barrier()

            AR.at(XOFF)
            qT = AR.bf(2, T)
            kT = AR.bf(2, T)
            kTM = AR.bf(10, 256)
            vTM = AR.bf(10, 256)
            gsT = AR.bf(10, 256)
            qtmp = [AR.bf(256), AR.bf(256)]
            rtmp = [AR.f32(4, 128), AR.f32(4, 128)]
            tbl = [AR.f32(4, 256), AR.f32(4, 256)]
            gtmp = [AR.f32(256), AR.f32(256)]
            scm = [[AR.bf(128), AR.bf(128)], [AR.bf(128), AR.bf(128)]]
            onb = [AR.f32(128), AR.f32(128)]
            ybt = [[AR.bf(128), AR.bf(128)], [AR.bf(128), AR.bf(128)]]
            s32 = [AR.f32(128), AR.f32(128)]
            s16 = [[AR.bf(128), AR.bf(128)], [AR.bf(128), AR.bf(128)]]
            scm_s = [AR.bf(128), AR.bf(128)]
            ybt_s = [AR.bf(128), AR.bf(128)]
            kvs = [AR.f32(128), AR.f32(128)]
            s0f = AR.f32(16, 128)
            s0b = AR.bf(16, 128)
            sout = s0f
            qm = AR.bf(2176)
            km = AR.bf(16, 128)
            P.op("pool", lambda e: e.memset(qm[:], 0.0), writes=["qm"])

            for hg in range(4):
                wb1, wk1 = wload([(lambda w: wview(w, 16, 512)[:, :, 0:256], wsrc(w_in, 0, 16, 2048 + 256 * hg, 256)),
                                  (lambda w: wview(w, 16, 512)[:, :, 256:512], wsrc(w_in, 0, 16, 3072 + 256 * hg, 256))])
                wb2, wk2 = wload([(lambda w: wview(w, 16, 512)[:, :, 0:256], wsrc(w_in, 0, 16, 4096 + 256 * hg, 256)),
                                  (lambda w: wview(w, 16, 512)[:, :, 256:512], wsrc(w_in, 0, 16, 5120 + 256 * hg, 256))], prefetch=False)
                wv1, wv2 = wview(wb1, 16, 512), wview(wb2, 16, 512)
                for i, (t0, n) in enumerate(TILES):
                    tb = tbl[i % 2]
                    P.dma("sp", tb[0:n, :, :], tab[:, t0:t0 + n, 256 * hg:256 * hg + 256].rearrange("a p n -> p a n"), writes=[("tbl", i % 2)])

                    def tv(a):
                        return tb[0:n, a, :].rearrange("p (h two d) -> p h two d", h=2, two=2)[:, :, 0, :]
                    b = nb()
                    tm_proj(wv1, wk1, 0, 256, 16, hT, [("hT", i)], t0, n, b)
                    qt = qtmp[i % 2]
                    rope(b, n, tv(0), tv(1), [("tbl", i % 2)], qt[0:n, :], ("qt", i % 2), rtmp[i % 2], ("rt", i % 2))
                    b = nb()
                    tm_proj(wv1, wk1, 256, 256, 16, hT, [("hT", i)], t0, n, b)
                    rope(b, n, tv(2), tv(3), [("tbl", i % 2)], kTM[0:n, i, :], ("kTM", i), rtmp[i % 2], ("rt", i % 2))
                    b = nb()
                    tm_proj(wv2, wk2, 0, 256, 16, hT, [("hT", i)], t0, n, b)
                    P.op("act", lambda e, i=i, n=n, b=b: e.copy(out=vTM[0:n, i, :], in_=banks[b][0:n, 0:256]), reads=[BK(b)], writes=[("vTM", i)])
                    b = nb()
                    tm_proj(wv2, wk2, 256, 256, 16, hT, [("hT", i)], t0, n, b)
                    gt = gtmp[i % 2]
                    P.op("act", lambda e, gt=gt, n=n, b=b: e.activation(out=gt[0:n, :], in_=banks[b][0:n, 0:256], func=AF.Silu), reads=[BK(b)], writes=[("gt", i % 2)])
                    P.op("pool", lambda e, gt=gt, i=i, n=n, hg=hg: e.tensor_tensor(out=gsT[0:n, i, :], in0=gt[0:n, :], in1=gnbc[0:n, 256 * hg:256 * hg + 256], op=ALU.mult),
                         reads=[("gt", i % 2), "gnbc"], writes=[("gsT", i)])
                    b = nb()
                    pv = banks[b][:].bitcast(BF16).rearrange("p (k n) -> p k n", k=8)
                    for hh in range(2):
                        P.op("pe", lambda e, pv=pv, hh=hh, qt=qt, n=n: e.transpose(pv[:, hh, 0:n], qt[0:n, hh * 128:hh * 128 + 128], identb[0:n, 0:n]),
                             reads=[("qt", i % 2), "identb"], writes=[BK(b)])
                        P.op("pe", lambda e, pv=pv, hh=hh, i=i, n=n: e.transpose(pv[:, 2 + hh, 0:n], kTM[0:n, i, hh * 128:hh * 128 + 128], identb[0:n, 0:n]),
                             reads=[("kTM", i), "identb"], writes=[BK(b)])
                    P.op("act", lambda e, pv=pv, t0=t0, n=n: e.copy(out=qT[:, :, t0:t0 + n], in_=pv[:, 0:2, 0:n]), reads=[BK(b)], writes=[("qT", i)])
                    P.op("act", lambda e, pv=pv, t0=t0, n=n: e.copy(out=kT[:, :, t0:t0 + n], in_=pv[:, 2:4, 0:n]), reads=[BK(b)], writes=[("kT", i)])
                for hh in range(2):
                    h = 2 * hg + hh
                    P.op("pool", lambda e, h=h, hh=hh: e.tensor_copy(out=s32[hh][:, :], in_=spre[:, h, :]), reads=[("spre", hg, hh)], writes=[("s32", hh)])
                    P.op("act", lambda e, hh=hh: e.copy(out=s16[hh][0][:, :], in_=s32[hh][:, :]), reads=[("s32", hh)], writes=[("s16", hh, 0)])
                wprefetch(2)
                items = [(i, hh) for i in range(10) for hh in range(2)]
                ctx = {}

                def S1a(k):
                    i, hh = items[k]
                    h = 2 * hg + hh
                    hs = slice(hh * 128, hh * 128 + 128)
                    P.dma("sp", s0f[:, :, :], sret[:, h, :, :].rearrange("s d v -> d s v"), writes=["s0f"])
                    P.op("act", lambda e: e.copy(out=s0b[:, :, :], in_=s0f[:, :, :]), reads=["s0f"], writes=["s0b"])
                    P.op("pool", lambda e: e.tensor_copy(out=qm[:, 0:2176].rearrange("p (s x) -> p s x", x=136)[:, :, 0:8],
                                                          in_=qT[:, hh, TP:T].rearrange("p (s t) -> p s t", t=8)),
                         reads=[("qT", 9), "qm"], writes=["qm"])
                    P.op("pool", lambda e: e.tensor_tensor(out=km[:, :, :], in0=kTM[:, 9, hs].unsqueeze(1).broadcast_to([128, 16, 128]),
                                                            in1=selm[:, :].unsqueeze(2).broadcast_to([128, 16, 128]), op=ALU.mult),
                         reads=[("kTM", 9), "selm"], writes=["km"])

                def S1(k):
                    i, hh = items[k]
                    t0, n = TILES[i]
                    bs = nb()
                    P.op("pe", lambda e: e.matmul(banks[bs][0:n, 0:n], lhsT=kT[:, hh, t0:t0 + n], rhs=qT[:, hh, t0:t0 + n], start=True, stop=True),
                         reads=[("kT", i), ("qT", i)], writes=[BK(bs)])
                    sm = scm_s[hh]
                    P.op("dve", lambda e: e.tensor_tensor(out=sm[0:n, 0:n], in0=banks[bs][0:n, 0:n], in1=masks[0:n, 1, 0:n], op=ALU.mult),
                         reads=[BK(bs), "masks"], writes=[("scm_s", hh)])

                def S2(k):
                    i, hh = items[k]
                    t0, n = TILES[i]
                    h = 2 * hg + hh
                    hs = slice(hh * 128, hh * 128 + 128)
                    samp = (i == 9)
                    sm = scm_s[hh]
                    bo = nb()
                    P.op("pe", lambda e: e.matmul(banks[bo][0:n, 0:128], lhsT=sm[0:n, 0:n], rhs=vTM[0:n, i, hs], start=True, stop=False),
                         reads=[("scm_s", hh), ("vTM", i)], writes=[BK(bo)])
                    if not samp:
                        P.op("pe", lambda e: e.matmul(banks[bo][0:n, 0:128], lhsT=qT[:, hh, t0:t0 + n], rhs=s16[hh][1][:, :], start=False, stop=True),
                             reads=[("qT", i), ("s16", hh, 1)], writes=[BK(bo)])
                    else:
                        for s_ in range(16):
                            P.op("pe", lambda e, s_=s_: e.matmul(banks[bo][0:128, 0:128], lhsT=qm[:, 128 * s_:128 * s_ + 128], rhs=s0b[:, s_, :], start=False, stop=(s_ == 15)),
                                 reads=["qm", "s0b"], writes=[BK(bo)])
                    P.op("dve", lambda e: e.bn_stats(out=bnst[0:n, hh, :], in_=banks[bo][0:n, 0:128]), reads=[BK(bo)], writes=[("bnst", hh)])
                    P.op("dve", lambda e: e.bn_aggr(out=mv[0:n, hh, :], in_=bnst[0:n, hh, :]), reads=[("bnst", hh)], writes=[("mv", hh)])
                    P.op("act", lambda e: e.activation(out=rsd[0:n, hh:hh + 1], in_=mv[0:n, hh, 1:2], func=AF.Sqrt, scale=1.0, bias=EPS),
                         reads=[("mv", hh)], writes=[("rsd", hh)])
                    P.op("dve", lambda e: e.reciprocal(out=rsd[0:n, hh:hh + 1], in_=rsd[0:n, hh:hh + 1]), reads=[("rsd", hh)], writes=[("rsd", hh)])
                    on = onb[hh]
                    P.op("dve", lambda e: e.tensor_scalar(out=on[0:n, :], in0=banks[bo][0:n, 0:128], scalar1=mv[0:n, hh, 0:1], scalar2=rsd[0:n, hh:hh + 1],
                                                          op0=ALU.subtract, op1=ALU.mult),
                         reads=[BK(bo), ("mv", hh), ("rsd", hh)], writes=[("on", hh)])
                    yb = ybt_s[hh]
                    P.op("pool", lambda e: e.tensor_tensor(out=yb[0:n, :], in0=on[0:n, :], in1=gsT[0:n, i, hs], op=ALU.mult),
                         reads=[("on", hh), ("gsT", i)], writes=[("ybt_s", hh)])

                def S3(k):
                    i, hh = items[k]
                    t0, n = TILES[i]
                    h = 2 * hg + hh
                    hs = slice(hh * 128, hh * 128 + 128)
                    samp = (i == 9)
                    yb = ybt_s[hh]
                    bt = nb()
                    pv = banks[bt][:].bitcast(BF16)
                    P.op("pe", lambda e: e.transpose(pv[:, 0:n], yb[0:n, :], identb[0:n, 0:n]), reads=[("ybt_s", hh), "identb"], writes=[BK(bt)])
                    P.op("act", lambda e: e.copy(out=ybT[:, h, t0:t0 + n], in_=pv[:, 0:n]), reads=[BK(bt)], writes=[("ybT", h, i)])
                    if not samp:
                        bk = nb()
                        P.op("pe", lambda e: e.matmul(banks[bk][:, 0:128], lhsT=kTM[0:n, i, hs], rhs=vTM[0:n, i, hs], start=True, stop=True),
                             reads=[("kTM", i), ("vTM", i)], writes=[BK(bk)])
                        gl = float(GAM[h] ** n)
                        last = (i == 8)
                        P.op("act", lambda e: e.mul(out=kvs[hh][:, :], in_=banks[bk][:, 0:128], mul=gl), reads=[BK(bk)], writes=[("kvs", hh)])
                        dst = sfin[:, h, :] if last else s32[hh][:, :]
                        P.op("dve", lambda e: e.scalar_tensor_tensor(out=dst, in0=s32[hh][:, :], scalar=gl, in1=kvs[hh][:, :], op0=ALU.mult, op1=ALU.add),
                             reads=[("kvs", hh), ("s32", hh)], writes=[("sfin", h)] if last else [("s32", hh)])
                        if not last:
                            P.op("pool", lambda e: e.tensor_copy(out=s16[hh][1][:, :], in_=s32[hh][:, :]), reads=[("s32", hh)], writes=[("s16", hh, 1)])
                    else:
                        g8 = float(GAM[h] ** 8)
                        for q4 in range(4):
                            bk = nb()
                            for s4 in range(4):
                                s_ = 4 * q4 + s4
                                P.op("pe", lambda e, bk=bk, s_=s_, s4=s4: e.matmul(banks[bk][:, 128 * s4:128 * s4 + 128], lhsT=km[:, s_, :], rhs=vTM[:, 9, hs], start=True, stop=True),
                                     reads=["km", ("vTM", 9)], writes=[BK(bk)])
                            P.op("dve", lambda e, bk=bk, q4=q4: e.tensor_tensor(out=sout[:, 4 * q4:4 * q4 + 4, :], in0=banks[bk][:, :].rearrange("p (s v) -> p s v", s=4),
                                                                               in1=s0f[:, 4 * q4:4 * q4 + 4, :], op=ALU.add),
                                 reads=[BK(bk), "s0f"], writes=["s0f"])
                            P.op("act", lambda e, q4=q4: e.mul(out=sout[:, 4 * q4:4 * q4 + 4, :], in_=sout[:, 4 * q4:4 * q4 + 4, :], mul=g8),
                                 reads=["s0f"], writes=["s0f"])
                        P.dma("sp", nrets_d[:, h, :, :].rearrange("s d v -> d s v"), sout[:, :, :], reads=["s0f"])

                def R1(i):
                    t0, n = TILES[i]
                    for hh in range(2):
                        bs = nb()
                        P.op("pe", lambda e, bs=bs, hh=hh: e.matmul(banks[bs][0:n, 0:n], lhsT=kT[:, hh, t0:t0 + n], rhs=qT[:, hh, t0:t0 + n], start=True, stop=True),
                             reads=[("kT", i), ("qT", i)], writes=[BK(bs)])
                        sm = scm[hh][i % 2]
                        P.op("dve", lambda e, bs=bs, sm=sm: e.tensor_tensor(out=sm[0:n, 0:n], in0=banks[bs][0:n, 0:n], in1=masks[0:n, 0, 0:n], op=ALU.mult),
                             reads=[BK(bs), "masks"], writes=[("scm", hh, i % 2)])

                def R2(i):
                    t0, n = TILES[i]
                    for hh in range(2):
                        hs = slice(hh * 128, hh * 128 + 128)
                        sm = scm[hh][i % 2]
                        bo = nb()
                        P.op("pe", lambda e, bo=bo, sm=sm, hs=hs: e.matmul(banks[bo][0:n, 0:128], lhsT=sm[0:n, 0:n], rhs=vTM[0:n, i, hs], start=True, stop=False),
                             reads=[("scm", hh, i % 2), ("vTM", i)], writes=[BK(bo)])
                        P.op("pe", lambda e, bo=bo, hh=hh: e.matmul(banks[bo][0:n, 0:128], lhsT=qT[:, hh, t0:t0 + n], rhs=s16[hh][i % 2][:, :], start=False, stop=True),
                             reads=[("qT", i), ("s16", hh, i % 2)], writes=[BK(bo)])
                        P.op("dve", lambda e, bo=bo, hh=hh: e.bn_stats(out=bnst[0:n, hh, :], in_=banks[bo][0:n, 0:128]), reads=[BK(bo)], writes=[("bnst", hh)])
                        P.op("dve", lambda e, hh=hh: e.bn_aggr(out=mv[0:n, hh, :], in_=bnst[0:n, hh, :]), reads=[("bnst", hh)], writes=[("mv", hh)])
                        P.op("act", lambda e, hh=hh: e.activation(out=rsd[0:n, hh:hh + 1], in_=mv[0:n, hh, 1:2], func=AF.Sqrt, scale=1.0, bias=EPS),
                             reads=[("mv", hh)], writes=[("rsd", hh)])
                        P.op("dve", lambda e, hh=hh: e.reciprocal(out=rsd[0:n, hh:hh + 1], in_=rsd[0:n, hh:hh + 1]), reads=[("rsd", hh)], writes=[("rsd", hh)])
                        on = onb[hh]
                        P.op("dve", lambda e, bo=bo, hh=hh, on=on: e.tensor_scalar(out=on[0:n, :], in0=banks[bo][0:n, 0:128], scalar1=mv[0:n, hh, 0:1], scalar2=rsd[0:n, hh:hh + 1],
                                                                                   op0=ALU.subtract, op1=ALU.mult),
                             reads=[BK(bo), ("mv", hh), ("rsd", hh)], writes=[("on", hh)])
                        yb = ybt[hh][i % 2]
                        P.op("pool", lambda e, on=on, yb=yb, hs=hs: e.tensor_tensor(out=yb[0:n, :], in0=on[0:n, :], in1=gsT[0:n, i, hs], op=ALU.mult),
                             reads=[("on", hh), ("gsT", i)], writes=[("ybt", hh, i % 2)])

                def RU(i):
                    t0, n = TILES[i]
                    for hh in range(2):
                        h = 2 * hg + hh
                        hs = slice(hh * 128, hh * 128 + 128)
                        bk = nb()
                        P.op("pe", lambda e, bk=bk, hs=hs: e.matmul(banks[bk][:, 0:128], lhsT=kTM[0:n, i, hs], rhs=vTM[0:n, i, hs], start=True, stop=True),
                             reads=[("kTM", i), ("vTM", i)], writes=[BK(bk)])
                        gl = float(GAM[h] ** n)
                        last = (i == 8)
                        P.op("act", lambda e, bk=bk, gl=gl, hh=hh: e.mul(out=kvs[hh][:, :], in_=banks[bk][:, 0:128], mul=gl), reads=[BK(bk)], writes=[("kvs", hh)])
                        dst = sfin[:, h, :] if last else s32[hh][:, :]
                        P.op("dve", lambda e, gl=gl, dst=dst, hh=hh: e.scalar_tensor_tensor(out=dst, in0=s32[hh][:, :], scalar=gl, in1=kvs[hh][:, :], op0=ALU.mult, op1=ALU.add),
                             reads=[("kvs", hh), ("s32", hh)], writes=[("sfin", h)] if last else [("s32", hh)])
                        if not last:
                            P.op("pool", lambda e, hh=hh: e.tensor_copy(out=s16[hh][(i + 1) % 2][:, :], in_=s32[hh][:, :]), reads=[("s32", hh)], writes=[("s16", hh, (i + 1) % 2)])

                def RT(i):
                    t0, n = TILES[i]
                    for hh in range(2):
                        h = 2 * hg + hh
                        yb = ybt[hh][i % 2]
                        bt = nb()
                        pv = banks[bt][:].bitcast(BF16)
                        P.op("pe", lambda e, pv=pv, yb=yb: e.transpose(pv[:, 0:n], yb[0:n, :], identb[0:n, 0:n]), reads=[("ybt", hh, i % 2), "identb"], writes=[BK(bt)])
                        P.op("act", lambda e, pv=pv, h=h: e.copy(out=ybT[:, h, t0:t0 + n], in_=pv[:, 0:n]), reads=[BK(bt)], writes=[("ybT", h, i)])

                samp_sched = {-2: [lambda: S1a(18)], 0: [lambda: S1(18)], 2: [lambda: S2(18)], 4: [lambda: S3(18), lambda: S1a(19)],
                              5: [lambda: S1(19)], 7: [lambda: S2(19)], 9: [lambda: S3(19)]}
                for t in range(-2, 10):
                    for f in samp_sched.get(t, []):
                        f()
                    if 0 <= t - 1 <= 8:
                        RT(t - 1)
                    if 0 <= t + 1 <= 8:
                        RU(t + 1)
                    if 0 <= t + 2 <= 8:
                        R1(t + 2)
                    if 0 <= t + 1 <= 8:
                        R2(t + 1)
            P.dma("sp", nretp_d.rearrange("h d v -> d h v"), sfin[:, :, :], reads=[("sfin", h) for h in range(8)])
            do_barrier()
            if debug:
                P.dma("sp", dbg_hT, hT[:, :, :])
                P.dma("sp", dbg_yaT, yaT[:, :, :])
                P.dma("sp", dbg_ybT, ybT[:, :, :])
                do_barrier()

            AR.at(ARW - 8 * T)
            mT = AR.bf(16, T)
            AR.at(XOFF)
            sga = [AR.f32(344), AR.f32(344)]
            sgb = [AR.f32(344), AR.f32(344)]
            t1b = [AR.f32(344), AR.f32(344)]
            t2b = [AR.f32(344), AR.f32(344)]
            assert AR.off <= ARW - 8 * T
            for m in range(16):
                wb, wkey = wload([(lambda w: w[:, 0:2048].rearrange("p (k n) -> p k n", k=16), wsrc(w_in, 0, 16, 6144 + 128 * m, 128)),
                                  (lambda w: w[:, 2048:4096].rearrange("p (k n) -> p k n", k=16), wsrc(w_in, 0, 16, 8192 + 128 * m, 128)),
                                  (lambda w: w[:, 4096:5120].rearrange("p (k n) -> p k n", k=8), wsrc(p_a, 0, 8, 128 * m, 128)),
                                  (lambda w: w[:, 5120:6144].rearrange("p (k n) -> p k n", k=8), wsrc(p_b, 0, 8, 128 * m, 128))])
                wga = wb[:, 0:2048].rearrange("p (k n) -> p k n", k=16)
                wgb = wb[:, 2048:4096].rearrange("p (k n) -> p k n", k=16)
                wpa = wb[:, 4096:5120].rearrange("p (k n) -> p k n", k=8)
                wpb = wb[:, 5120:6144].rearrange("p (k n) -> p k n", k=8)
                for ni, (t0, n) in enumerate(NT):
                    res = {}

                    def cons(name):
                        def c(ni_, t0_, n_, b):
                            res[name] = b
                        return c
                    fm_proj(wga, wkey, 0, 16, hT, "hT", [(t0, n)], cons("ga"))
                    fm_proj(wgb, wkey, 0, 16, hT, "hT", [(t0, n)], cons("gb"))
                    fm_proj(wpa, wkey, 0, 8, yaT, "yaT", [(t0, n)], cons("pa"), rkeys_fn=nt_over)
                    ybk = [("ybT", h, i) for h in range(8) for i in tt_over(t0, n)]
                    bpb = nb()
                    for kt in range(8):
                        P.op("pe", lambda e, kt=kt, bpb=bpb, t0=t0, n=n, wpb=wpb: e.matmul(banks[bpb][:, 0:n], lhsT=wpb[:, kt, :], rhs=ybT[:, kt, t0:t0 + n], start=(kt == 0), stop=(kt == 7)),
                             reads=[wkey] + [("ybT", kt, i) for i in tt_over(t0, n)], writes=[BK(bpb)])
                    q = (m * 4 + ni) % 2
                    P.op("act", lambda e, q=q, n=n, b=res["ga"]: e.activation(out=sga[q][:, 0:n], in_=banks[b][:, 0:n], func=AF.Sigmoid), reads=[BK(res["ga"])], writes=[("sga", q)])
                    P.op("act", lambda e, q=q, n=n, b=res["gb"]: e.activation(out=sgb[q][:, 0:n], in_=banks[b][:, 0:n], func=AF.Sigmoid), reads=[BK(res["gb"])], writes=[("sgb", q)])
                    P.op("dve", lambda e, q=q, n=n, b=res["pa"]: e.tensor_tensor(out=t1b[q][:, 0:n], in0=banks[b][:, 0:n], in1=sga[q][:, 0:n], op=ALU.mult),
                         reads=[BK(res["pa"]), ("sga", q)], writes=[("t1", q)])
                    P.op("dve", lambda e, q=q, n=n, b=bpb: e.tensor_tensor(out=t2b[q][:, 0:n], in0=banks[b][:, 0:n], in1=sgb[q][:, 0:n], op=ALU.mult),
                         reads=[BK(bpb), ("sgb", q)], writes=[("t2", q)])
                    P.op("pool", lambda e, q=q, n=n, m=m, t0=t0: e.tensor_tensor(out=mT[:, m, t0:t0 + n], in0=t1b[q][:, 0:n], in1=t2b[q][:, 0:n], op=ALU.add),
                         reads=[("t1", q), ("t2", q)], writes=[("mT", ni)])
            do_barrier()

            if debug:
                P.dma("sp", dbg_mT, mT[:, :, :])
                do_barrier()
            AR.at(0)
            acc = AR.f32(10, D)
            assert AR.off <= ARW - 8 * T
            for i, (t0, n) in enumerate(TILES):
                P.dma("sp", acc[0:n, i, :], xall[t0:t0 + n, :], writes=[("acc", i)])
            for cg in range(4):
                wb, wkey = wload([(lambda w: wview(w, 16, 512), wsrc(w_out, 0, 16, 512 * cg, 512))])
                wv = wview(wb, 16, 512)
                for i, (t0, n) in enumerate(TILES):
                    b = nb(0, 8)
                    tm_proj(wv, wkey, 0, 512, 16, mT, [("mT", q) for q in nt_over(t0, n)], t0, n, b)
                    P.op("dve", lambda e, i=i, n=n, b=b, cg=cg: e.tensor_tensor(out=acc[0:n, i, 512 * cg:512 * cg + 512], in0=banks[b][0:n, :], in1=acc[0:n, i, 512 * cg:512 * cg + 512], op=ALU.add),
                         reads=[BK(b), ("acc", i)], writes=[("acc", i)])
            do_barrier()

            if debug:
                P.dma("sp", dbg_x1, acc[:, :, :])
                do_barrier()
            AR.at(10 * D)
            h2T = AR.bf(16, T)
            uoff = AR.off
            uT = AR.bf(8, T)
            rtm = [AR.bf(344), AR.bf(344)]
            AR.at(uoff)
            hbs = [AR.bf(D), AR.bf(D)]

            def acc_src(i):
                t0, n = TILES[i]
                return acc[0:n, i, :], [("acc", i)]
            run_gen(norm_T(acc_src, TILES, 1, h2T, "h2T", "f", hbs))
            do_barrier()
            for fb in range(8):
                for sub in range(2):
                    wb, wkey = wload([(lambda w: wview(w, 16, 512), wsrc(w_up, 0, 16, 1024 * fb + 512 * sub, 512))])
                    wv = wview(wb, 16, 512)
                    for mm in range(4):
                        ft = 4 * sub + mm

                        def up_cons(ni, t0, n, b, ft=ft):
                            q = ni % 2
                            P.op("act", lambda e: e.activation(out=rtm[q][:, 0:n], in_=banks[b][:, 0:n], func=AF.Relu), reads=[BK(b)], writes=[("rtm", q)])
                            P.op("pool", lambda e: e.tensor_tensor(out=uT[:, ft, t0:t0 + n], in0=rtm[q][:, 0:n], in1=rtm[q][:, 0:n], op=ALU.mult),
                                 reads=[("rtm", q)], writes=[("uT", ft, ni)])
                        fm_proj(wv, wkey, 128 * mm, 16, h2T, "h2T", NT, up_cons)
                for cgp in range(2):
                    wb, wkey = wload([(lambda w: wview(w, 8, 1024), wsrc(w_down, 1024 * fb, 8, 1024 * cgp, 1024))])
                    wv = wview(wb, 8, 1024)
                    for c2 in range(2):
                        cg = 2 * cgp + c2
                        for i, (t0, n) in enumerate(TILES):
                            b = nb(0, 8)
                            for kt in range(8):
                                P.op("pe", lambda e, kt=kt, b=b, t0=t0, n=n, c2=c2, wv=wv: e.matmul(banks[b][0:n, :], lhsT=uT[:, kt, t0:t0 + n], rhs=wv[:, kt, 512 * c2:512 * c2 + 512],
                                                                                         start=(kt == 0), stop=(kt == 7)),
                                     reads=[wkey] + [("uT", kt, q) for q in nt_over(t0, n)], writes=[BK(b)])
                            P.op("dve", lambda e, i=i, n=n, b=b, cg=cg: e.tensor_tensor(out=acc[0:n, i, 512 * cg:512 * cg + 512], in0=banks[b][0:n, :], in1=acc[0:n, i, 512 * cg:512 * cg + 512], op=ALU.add),
                                 reads=[BK(b), ("acc", i)], writes=[("acc", i)])
            do_barrier()
            if debug:
                P.dma("sp", dbg_x2, acc[:, :, :])
                do_barrier()
            for i, hb, hk in norm_T(acc_src, TILES, 2, None, None, "z", hbs):
                t0, n = TILES[i]
                P.op("dve", lambda e, i=i, n=n: e.scalar_tensor_tensor(out=acc[0:n, i, :], in0=acc[0:n, i, :], scalar=rs[0:n, i:i + 1], in1=gbc[0:n, :], op0=ALU.mult, op1=ALU.mult),
                     reads=[("acc", i), ("rs", i), "gbc"], writes=[("acc", i)])
                P.dma("sp", y_d[t0:t0 + n, :], acc[0:n, i, :], reads=[("acc", i)])

            return P

        program()
        wl_state["dry"] = False
        build_nc.marks = []
        P = program()
        P.emit()
        build_nc.stats = P.stats
    return nc


def _tables(core):
    half = core % 2
    f32 = np.float32
    inv = (f32(10000.0) ** (-(np.arange(0, 128, 2, dtype=f32)) / f32(128))).astype(f32)
    t = np.arange(T)
    pos = np.where(t < TP, half * TP + t, 16384 + (t - TP) % 8).astype(f32)
    l = np.where(t < 928, t % CH, np.where(t < TP, t - 928, (t - TP) % 8)).astype(np.float64)
    ang = (pos[:, None] * inv[None, :]).astype(f32).astype(np.float64)
    cos, sin = np.cos(ang), np.sin(ang)
    gam = np.array(GAM, np.float64)
    dq = gam[None, :] ** (l[:, None] + 1.0)
    dk = gam[None, :] ** (-(l[:, None] + 1.0)) * (128.0 ** -0.5)
    tab = np.zeros((4, T, NH, 64), f32)
    tab[0] = cos[:, None, :] * dq[:, :, None]
    tab[1] = sin[:, None, :] * dq[:, :, None]
    tab[2] = cos[:, None, :] * dk[:, :, None]
    tab[3] = sin[:, None, :] * dk[:, :, None]
    tab = np.repeat(tab[:, :, :, None, :], 2, axis=3).reshape(4, T, 1024)
    tp = np.arange(TP)
    posq = tp.astype(f32)
    angq = (posq[:, None] * inv[None, :]).astype(f32).astype(np.float64)
    wq = gam[None, :] ** (TP - 1.0 - tp[:, None]) * (128.0 ** -0.5)
    tabq = np.zeros((2, TP, NH, 64), f32)
    tabq[0] = np.cos(angq)[:, None, :] * wq[:, :, None]
    tabq[1] = np.sin(angq)[:, None, :] * wq[:, :, None]
    tabq = np.repeat(tabq[:, :, :, None, :], 2, axis=3).reshape(2, TP, 1024)
    return tab, tabq


_NC_CACHE = {}


def kernel(x_prompt, x_sample, state_conv, state_lru, state_ret, meta_tokens, norm_mix_g, w_in, conv_w,
           conv_b, lru_wa, lru_ba, lru_wx, lru_bx, lru_lam, ret_norm_g, p_a, p_b, w_out, norm_ffn_g,
           w_up, w_down, norm_f_g):
    f32 = np.float32
    A = lambda a: np.ascontiguousarray(np.asarray(a, dtype=f32))
    x_prompt, x_sample = A(x_prompt), A(x_sample)
    meta = A(meta_tokens)
    dbg = bool(_NC_CACHE.get("debug"))
    if ("nc", dbg) not in _NC_CACHE:
        _NC_CACHE[("nc", dbg)] = build_nc(debug=dbg)
    nc = _NC_CACHE[("nc", dbg)]
    pp = np.arange(128)
    maskp = (pp[None, :] >= pp[:, None]).astype(f32)
    masks_ = maskp * (pp[None, :] // 8 == pp[:, None] // 8).astype(f32)
    masks = np.stack([maskp, masks_])
    selm = (pp[:, None] // 8 == np.arange(16)[None, :]).astype(f32)
    prm = np.concatenate([A(conv_w)[0], A(conv_b), A(lru_ba), A(lru_bx), A(lru_lam)], axis=0)
    gn3 = np.stack([A(norm_mix_g)[0], A(norm_ffn_g)[0], A(norm_f_g)])
    shared = dict(masks=masks, selm=selm, prm=A(prm), gn3=A(gn3), rng=A(ret_norm_g)[0], w_in=A(w_in)[0],
                  lwa=A(lru_wa)[0], lwx=A(lru_wx)[0], p_a=A(p_a)[0], p_b=A(p_b)[0], w_out=A(w_out)[0],
                  w_up=A(w_up)[0], w_down=A(w_down)[0])
    in_maps = []
    tabs = [_tables(0), _tables(1)]
    for c in range(8):
        b, half = c // 2, c % 2
        seq = np.concatenate([meta, x_prompt[b]], axis=0)
        own = seq[half * TP:(half + 1) * TP]
        pre = seq[0:TP]
        xs = x_sample[16 * c:16 * c + 16].reshape(128, D)
        fl = np.zeros((128, 8), f32)
        fl[:, 0] = half
        fl[:, 1] = 1 - half
        fl[:, 2] = half
        fl[:, 3] = 1.0
        fl[:, 4] = 0.0
        m = dict(shared)
        m.update(xall=A(np.concatenate([own, xs], axis=0)), xpre=A(pre),
                 sconv=A(state_conv[0, 16 * c:16 * c + 16].reshape(48, 1024)),
                 slru=A(state_lru[0, 16 * c:16 * c + 16]), sret=A(state_ret[0, 16 * c:16 * c + 16]),
                 flags=fl, tab=tabs[half][0], tabpre=tabs[half][1])
        in_maps.append(m)
    res = run_bass_kernel_spmd(nc, in_maps, core_ids=list(range(8)))
    R = res.results
    if dbg:
        _NC_CACHE["raw"] = R
    y_prompt = np.zeros((4, 2048, D), f32)
    y_sample = np.zeros((128, 8, D), f32)
    ncp = np.zeros((1, 4, 3, 1024), f32)
    nlp = np.zeros((1, 4, 1024), f32)
    nrp = np.zeros((1, 4, NH, 128, 128), f32)
    ncs = np.zeros((1, 128, 3, 1024), f32)
    nls = np.zeros((1, 128, 1024), f32)
    nrs = np.zeros((1, 128, NH, 128, 128), f32)
    for c in range(8):
        b, half = c // 2, c % 2
        r = R[c]
        y = np.asarray(r["y"])
        if half == 0:
            y_prompt[b, 0:TP - 16] = y[16:TP]
        else:
            y_prompt[b, TP - 16:] = y[0:TP]
            ncp[0, b] = np.asarray(r["nconv"])[48:51]
            nlp[0, b] = np.asarray(r["nlru"])[16]
            nrp[0, b] = np.asarray(r["nretp"])
        y_sample[16 * c:16 * c + 16] = y[TP:T].reshape(16, 8, D)
        ncs[0, 16 * c:16 * c + 16] = np.asarray(r["nconv"])[0:48].reshape(16, 3, 1024)
        nls[0, 16 * c:16 * c + 16] = np.asarray(r["nlru"])[0:16]
        nrs[0, 16 * c:16 * c + 16] = np.asarray(r["nrets"])
    return (y_prompt, y_sample, ncp, nlp, nrp, ncs, nls, nrs)
```
